# Optimizing a Trainium2 kernel written in Bass

```python
import math
import jax, jax.numpy as jnp
from jax import lax
import numpy as np

D_MODEL = 2048
BATCH = 16
SEQ = 2048
DEPTH = 2

CHUNK = 64
EPS = 1e-6
NEG_INF = -1e30
ROPE_THETA = 500000.0

N_HEADS_A = 8
HEAD_DIM = 128
ROT_DIM = HEAD_DIM // 4
N_IDX_HEADS = 8
IDX_DIM = 64
IDX_ROT_DIM = IDX_DIM // 4
TOPK_MAX = 256
Q_BLOCK = 128
WIDTH_A = N_HEADS_A * HEAD_DIM

WIDTH_B = D_MODEL // 4
SSM_GROUP = 16
N_SSM_GROUPS = WIDTH_B // SSM_GROUP
SSM_STATE = 64

WIDTH_C = D_MODEL // 4
POOL_WINDOWS = (2, 4, 8, 16)
POOL_GROUP = WIDTH_C // 4

N_BRANCH = 3

D_FF = 5504
CONV_WIDTH = 3

Q_W = WIDTH_A
K_W = HEAD_DIM
V_W = HEAD_DIM
QI_W = N_IDX_HEADS * IDX_DIM
KI_W = IDX_DIM
WI_W = N_IDX_HEADS
U_W = WIDTH_B
P_W = WIDTH_C
IN_SPLITS = (Q_W, Q_W + K_W, Q_W + K_W + V_W, Q_W + K_W + V_W + QI_W,
             Q_W + K_W + V_W + QI_W + KI_W, Q_W + K_W + V_W + QI_W + KI_W + WI_W,
             Q_W + K_W + V_W + QI_W + KI_W + WI_W + U_W)
D_IN = Q_W + K_W + V_W + QI_W + KI_W + WI_W + U_W + P_W

kernel_name = "chunk_causal_hybrid_dsa_s5_pool_convffn"


def rms_norm(x, g):
    xf = x.astype(jnp.float32)
    y = xf * lax.rsqrt(jnp.mean(xf * xf, axis=-1, keepdims=True) + EPS)
    return (y * g.astype(jnp.float32)).astype(x.dtype)


def partial_rope(x, pos, rot_dim):
    half = rot_dim // 2
    inv_freq = ROPE_THETA ** (-jnp.arange(half, dtype=jnp.float32) * (2.0 / rot_dim))
    ang = pos.astype(jnp.float32)[..., None] * inv_freq
    cos = jnp.cos(ang)[:, :, None, :]
    sin = jnp.sin(ang)[:, :, None, :]
    xf = x.astype(jnp.float32)
    x1 = xf[..., :half]
    x2 = xf[..., half:rot_dim]
    out = jnp.concatenate([x1 * cos - x2 * sin, x2 * cos + x1 * sin, xf[..., rot_dim:]], axis=-1)
    return out.astype(x.dtype)


def dsa_attention(q, k, v, qi, ki, wi, n_topk):
    B, L, H, dh = q.shape
    nb = L // Q_BLOCK
    key_pos = jnp.arange(L)
    kf = k.astype(jnp.float32)
    vf = v.astype(jnp.float32)
    kif = ki.astype(jnp.float32)

    def block(args):
        bi, qb, qib, wib = args
        qpos = bi * Q_BLOCK + jnp.arange(Q_BLOCK)
        limit = (qpos // CHUNK + 1) * CHUNK
        admissible = key_pos[None, :] < limit[:, None]
        s = jnp.einsum('bqhd,bsd->bqhs', qib.astype(jnp.float32), kif) * (IDX_DIM ** -0.5)
        idx_score = jnp.einsum('bqhs,bqh->bqs', jax.nn.relu(s), wib.astype(jnp.float32))
        idx_score = jnp.where(admissible[None], idx_score, NEG_INF)
        top_val, top_idx = lax.top_k(idx_score, n_topk)
        valid = top_val > NEG_INF * 0.5
        kg = jax.vmap(lambda kk, ii: kk[ii])(kf, top_idx)
        vg = jax.vmap(lambda vv, ii: vv[ii])(vf, top_idx)
        logits = jnp.einsum('bqhd,bqkd->bqhk', qb.astype(jnp.float32), kg) * (dh ** -0.5)
        logits = jnp.where(valid[:, :, None, :], logits, NEG_INF)
        probs = jax.nn.softmax(logits, axis=-1)
        o = jnp.einsum('bqhk,bqkd->bqhd', probs, vg)
        return o.astype(q.dtype)

    qs = q.reshape(B, nb, Q_BLOCK, H, dh).transpose(1, 0, 2, 3, 4)
    qis = qi.reshape(B, nb, Q_BLOCK, N_IDX_HEADS, IDX_DIM).transpose(1, 0, 2, 3, 4)
    wis = wi.reshape(B, nb, Q_BLOCK, N_IDX_HEADS).transpose(1, 0, 2, 3)
    out = lax.map(block, (jnp.arange(nb), qs, qis, wis))
    return out.transpose(1, 0, 2, 3, 4).reshape(B, L, H * dh)


def s5_mixer(u, a_re, a_im, b_re, b_im, c_re, c_im, d_skip, log_dt, w_glu):
    B, L, _ = u.shape
    G, P, I = N_SSM_GROUPS, SSM_STATE, SSM_GROUP
    f32 = jnp.float32
    uf = u.astype(f32).reshape(B, L, G, I)
    lam = lax.complex(a_re.astype(f32), a_im.astype(f32))
    dt = jnp.exp(log_dt.astype(f32))[:, None]
    lam_bar = jnp.exp(lam * dt)
    b_bar = ((lam_bar - 1.0) / lam)[..., None] * lax.complex(b_re.astype(f32), b_im.astype(f32))
    bu = jnp.einsum('gpi,blgi->blgp', b_bar, uf.astype(jnp.complex64))
    a_elems = jnp.broadcast_to(lam_bar, (1, L, G, P))

    def combine(left, right):
        a_l, b_l = left
        a_r, b_r = right
        return a_r * a_l, a_r * b_l + b_r

    _, states = lax.associative_scan(combine, (a_elems, bu), axis=1)
    c_t = lax.complex(c_re.astype(f32), c_im.astype(f32))
    y = jnp.einsum('gip,blgp->blgi', c_t, states).real + d_skip.astype(f32).reshape(G, I) * uf
    y = jax.nn.gelu(y.reshape(B, L, WIDTH_B))
    y = y * jax.nn.sigmoid(y @ w_glu.astype(f32))
    return y.astype(u.dtype)


def pool_mixer(p, w_pool, pool_scale):
    B, L, _ = p.shape
    pf = p.astype(jnp.float32)
    cs = jnp.pad(jnp.cumsum(pf, axis=1), ((0, 0), (1, 0), (0, 0)))
    t1 = jnp.arange(1, L + 1, dtype=jnp.float32)
    outs = []
    for g, w in enumerate(POOL_WINDOWS):
        sl = slice(g * POOL_GROUP, (g + 1) * POOL_GROUP)
        upper = cs[:, 1:, sl]
        lower = jnp.pad(cs[:, :L + 1 - w, sl], ((0, 0), (w - 1, 0), (0, 0)))
        mean = (upper - lower) / jnp.minimum(t1, float(w))[None, :, None]
        outs.append(mean - pf[:, :, sl])
    pooled = jnp.stack(outs, axis=2)
    y = jnp.einsum('blgi,gio->blgo', pooled, w_pool.astype(jnp.float32)).reshape(B, L, WIDTH_C)
    return (y * pool_scale.astype(jnp.float32)).astype(p.dtype)


def hybrid_mixer(h, positions, n_topk, w_in, g_q, g_k, a_re, a_im, b_re, b_im, c_re, c_im,
                 d_skip, log_dt, w_glu, w_pool, pool_scale, p_a, p_b, p_c, w_gate, b_gate, w_out):
    B, L, _ = h.shape
    proj = h @ w_in
    q, k, v, qi, ki, wi, u, p = jnp.split(proj, IN_SPLITS, axis=-1)
    q = partial_rope(rms_norm(q.reshape(B, L, N_HEADS_A, HEAD_DIM), g_q), positions, ROT_DIM)
    k = partial_rope(rms_norm(k, g_k)[:, :, None, :], positions, ROT_DIM)[:, :, 0]
    qi = partial_rope(qi.reshape(B, L, N_IDX_HEADS, IDX_DIM), positions, IDX_ROT_DIM)
    ki = partial_rope(ki[:, :, None, :], positions, IDX_ROT_DIM)[:, :, 0]
    o_a = dsa_attention(q, k, v, qi, ki, wi * (N_IDX_HEADS ** -0.5), n_topk)
    o_b = s5_mixer(u, a_re, a_im, b_re, b_im, c_re, c_im, d_skip, log_dt, w_glu)
    o_c = pool_mixer(p, w_pool, pool_scale)
    merged = jax.nn.sigmoid(h @ w_gate[0] + b_gate[0]) * (o_a @ p_a)
    merged = merged + jax.nn.sigmoid(h @ w_gate[1] + b_gate[1]) * (o_b @ p_b)
    merged = merged + jax.nn.sigmoid(h @ w_gate[2] + b_gate[2]) * (o_c @ p_c)
    return merged @ w_out


def conv_ffn(h, w_up, conv_w, conv_b, w_down):
    L = h.shape[1]
    a, b = jnp.split(h @ w_up, 2, axis=-1)
    a_pad = jnp.pad(a, ((0, 0), (CONV_WIDTH - 1, 0), (0, 0)))
    a_conv = conv_b + a_pad[:, 0:L] * conv_w[0]
    for j in range(1, CONV_WIDTH):
        a_conv = a_conv + a_pad[:, j:j + L] * conv_w[j]
    return (jax.nn.silu(a_conv) * b) @ w_down


def setup_inputs(seed: int = 0) -> dict:
    key = jax.random.key(seed)
    ks = jax.random.split(key, 40)
    f32 = jnp.float32
    G, P, I = N_SSM_GROUPS, SSM_STATE, SSM_GROUP

    def nrm(k, shape, scale):
        return jax.random.normal(k, shape, f32) * scale

    x = nrm(ks[0], (BATCH, SEQ, D_MODEL), 1.0)
    c = nrm(ks[1], (BATCH, D_MODEL), 1.0)
    offsets = jax.random.randint(ks[2], (BATCH, 1), 0, 64) * CHUNK
    positions = (offsets + jnp.arange(SEQ, dtype=jnp.int32)[None, :]).astype(jnp.int32)
    return {
        "x": x,
        "c": c,
        "positions": positions,
        "w_ada": nrm(ks[3], (DEPTH, D_MODEL, 6 * D_MODEL), 0.5 * D_MODEL ** -0.5),
        "b_ada": nrm(ks[4], (DEPTH, 6 * D_MODEL), 0.02),
        "g_norm1": 1.0 + nrm(ks[5], (DEPTH, D_MODEL), 0.02),
        "g_norm2": 1.0 + nrm(ks[6], (DEPTH, D_MODEL), 0.02),
        "w_in": nrm(ks[7], (DEPTH, D_MODEL, D_IN), D_MODEL ** -0.5),
        "g_q": 1.0 + nrm(ks[8], (DEPTH, HEAD_DIM), 0.02),
        "g_k": 1.0 + nrm(ks[9], (DEPTH, HEAD_DIM), 0.02),
        "a_re": -0.5 * (1.0 + nrm(ks[10], (DEPTH, G, P), 0.01)),
        "a_im": math.pi * jnp.arange(P, dtype=f32)[None, None, :] + nrm(ks[11], (DEPTH, G, P), 0.01),
        "b_re": nrm(ks[12], (DEPTH, G, P, I), (2.0 * I) ** -0.5),
        "b_im": nrm(ks[13], (DEPTH, G, P, I), (2.0 * I) ** -0.5),
        "c_re": nrm(ks[14], (DEPTH, G, I, P), (2.0 * P) ** -0.5),
        "c_im": nrm(ks[15], (DEPTH, G, I, P), (2.0 * P) ** -0.5),
        "d_skip": nrm(ks[16], (DEPTH, WIDTH_B), 1.0),
        "log_dt": jax.random.uniform(ks[17], (DEPTH, G), f32, math.log(1e-3), math.log(1e-1)),
        "w_glu": nrm(ks[18], (DEPTH, WIDTH_B, WIDTH_B), WIDTH_B ** -0.5),
        "w_pool": nrm(ks[19], (DEPTH, 4, POOL_GROUP, POOL_GROUP), POOL_GROUP ** -0.5),
        "pool_scale": 1.0 + nrm(ks[20], (DEPTH, WIDTH_C), 0.02),
        "p_a": nrm(ks[21], (DEPTH, WIDTH_A, D_MODEL), WIDTH_A ** -0.5),
        "p_b": nrm(ks[22], (DEPTH, WIDTH_B, D_MODEL), WIDTH_B ** -0.5),
        "p_c": nrm(ks[23], (DEPTH, WIDTH_C, D_MODEL), WIDTH_C ** -0.5),
        "w_gate": nrm(ks[24], (DEPTH, N_BRANCH, D_MODEL, D_MODEL), D_MODEL ** -0.5),
        "b_gate": nrm(ks[25], (DEPTH, N_BRANCH, D_MODEL), 0.02),
        "w_out": nrm(ks[26], (DEPTH, D_MODEL, D_MODEL), D_MODEL ** -0.5),
        "w_up": nrm(ks[27], (DEPTH, D_MODEL, 2 * D_FF), D_MODEL ** -0.5),
        "conv_w": nrm(ks[28], (DEPTH, CONV_WIDTH, D_FF), CONV_WIDTH ** -0.5),
        "conv_b": nrm(ks[29], (DEPTH, D_FF), 0.02),
        "w_down": nrm(ks[30], (DEPTH, D_FF, D_MODEL), D_FF ** -0.5),
    }


def reference(x, c, positions, w_ada, b_ada, g_norm1, g_norm2, w_in, g_q, g_k, a_re, a_im,
              b_re, b_im, c_re, c_im, d_skip, log_dt, w_glu, w_pool, pool_scale, p_a, p_b, p_c,
              w_gate, b_gate, w_out, w_up, conv_w, conv_b, w_down):
    L = x.shape[1]
    n_topk = min(TOPK_MAX, L // 4)
    c_act = jax.nn.silu(c)
    for l in range(DEPTH):
        mod = c_act @ w_ada[l] + b_ada[l]
        sh1, sc1, gt1, sh2, sc2, gt2 = [m[:, None, :] for m in jnp.split(mod, 6, axis=-1)]
        h = rms_norm(x, g_norm1[l]) * (1.0 + sc1) + sh1
        mix = hybrid_mixer(h, positions, n_topk, w_in[l], g_q[l], g_k[l], a_re[l], a_im[l],
                           b_re[l], b_im[l], c_re[l], c_im[l], d_skip[l], log_dt[l], w_glu[l],
                           w_pool[l], pool_scale[l], p_a[l], p_b[l], p_c[l], w_gate[l],
                           b_gate[l], w_out[l])
        x = x + gt1 * mix
        h = rms_norm(x, g_norm2[l]) * (1.0 + sc2) + sh2
        x = x + gt2 * conv_ffn(h, w_up[l], conv_w[l], conv_b[l], w_down[l])
    return x
```

```python
from contextlib import ExitStack
import math
import numpy as np
import ml_dtypes
import concourse.bass as bass
import concourse.mybir as mybir
from concourse.bass_utils import run_bass_kernel_spmd

F32 = mybir.dt.float32
BF16 = mybir.dt.bfloat16
I32 = mybir.dt.int32
ALU = mybir.AluOpType
AF = mybir.ActivationFunctionType

NCORES = 8
D = 2048
SEQ = 2048
NB = 2
NTOK = NB * SEQ
T = 512
NT = NTOK // T
DFF = 5504
NFC = DFF // 128
EPS = 1e-6
PI = math.pi
TWO_PI = 2.0 * math.pi
NBIS = 18
NQ = 9
ENGS = ("pe", "act", "dve", "pool", "sp")


class Sched:
    def __init__(self, nc, name):
        self.nc = nc
        self.name = name
        self.streams = {e: [] for e in ENGS}
        self.cnt = {}
        self.res = {}
        self.seen = {e: {} for e in ENGS}

    def op(self, eng, fn, reads=(), writes=(), dma=None, ndma=1):
        deps = {}

        def add(tok):
            if tok is None:
                return
            k, v = tok
            if deps.get(k, 0) < v:
                deps[k] = v

        for r in reads:
            st = self.res.get(r)
            if st:
                add(st[0])
        for w in writes:
            st = self.res.get(w)
            if st:
                add(st[0])
                for k, v in st[1].items():
                    add((k, v))
        if dma is None:
            key = eng
            self.cnt[key] = self.cnt.get(key, 0) + 1
        else:
            key = dma
            self.cnt[key] = self.cnt.get(key, 0) + 16 * ndma
        tok = (key, self.cnt[key])
        waits = []
        seen = self.seen[eng]
        for k, v in deps.items():
            if k == eng and eng == "pe":
                continue
            if seen.get(k, 0) >= v:
                continue
            seen[k] = v
            waits.append((k, v))
        self.streams[eng].append((fn, waits, key if dma is None else None))
        for r in reads:
            st = self.res.setdefault(r, [None, {}])
            if st[1].get(key, 0) < tok[1]:
                st[1][key] = tok[1]
        for w in writes:
            self.res[w] = [tok, {}]
        return tok

    def emit(self):
        nc = self.nc
        dma_keys = [k for k in self.cnt if k not in ENGS]
        sems = {k: nc.alloc_semaphore(name=f"{self.name}_{k}") for k in self.cnt}
        with ExitStack() as st:
            block = st.enter_context(nc.Block())

            def mk(name):
                def body(eng):
                    for fn, waits, inc in self.streams[name]:
                        for k, v in waits:
                            eng.wait_ge(sems[k], v)
                        if inc is None:
                            fn(eng, sems)
                        else:
                            fn(eng).then_inc(sems[inc], 1)
                    if name == "sp":
                        for k in dma_keys:
                            eng.wait_ge(sems[k], self.cnt[k])
                return body

            block.tensor(mk("pe"))
            block.scalar(mk("act"))
            block.vector(mk("dve"))
            block.gpsimd(mk("pool"))
            block.sync(mk("sp"))
        nc.clear_and_free_semaphores(list(sems.values()))
        nc.all_engine_barrier()


class Phase:
    def __init__(self, nc, name):
        self.nc = nc
        self.name = name
        self.st = ExitStack()
        self.S = Sched(nc, name)
        self.nbank = 0
        self.banks = []
        self.rr_i = 0
        self.pref = ""
        self.cur = None
        self.tasklists = []

    def _k(self, keys):
        if not self.pref:
            return list(keys)
        return [k if k.startswith("@") else self.pref + k for k in keys]

    def _op(self, eng, fn, reads=(), writes=(), dma=None):
        reads = self._k(reads)
        writes = self._k(writes)
        if self.cur is not None:
            self.cur.append((eng, fn, reads, writes, dma))
        else:
            self.S.op(eng, fn, reads=reads, writes=writes, dma=dma)

    def begin_task(self, name):
        self.pref = name + "_"
        self.cur = []
        self.tasklists.append(self.cur)

    def end_task(self):
        self.pref = ""
        self.cur = None

    def run_tasks(self):
        lists = self.tasklists
        idx = [0] * len(lists)
        tot = [max(1, len(x)) for x in lists]
        while True:
            best = None
            for i, lst in enumerate(lists):
                if idx[i] < len(lst):
                    fr = idx[i] / tot[i]
                    if best is None or fr < best[0]:
                        best = (fr, i)
            if best is None:
                break
            i = best[1]
            eng, fn, reads, writes, dma = lists[i][idx[i]]
            idx[i] += 1
            self.S.op(eng, fn, reads=reads, writes=writes, dma=dma)
        self.tasklists = []

    def sb(self, name, shape, dt):
        return self.st.enter_context(self.nc.sbuf_tensor(f"{self.name}_{name}", shape, dt))

    def ps(self, name, shape=(128, 512), dt=F32):
        return self.st.enter_context(self.nc.psum_tensor(f"{self.name}_{name}", list(shape), dt))

    def mkbanks(self, n):
        self.banks = [self.ps(f"bk{i}") for i in range(n)]

    def bank(self):
        i = self.nbank % len(self.banks)
        self.nbank += 1
        return self.banks[i], f"bk{i}"

    def close(self):
        self.S.emit()
        self.st.close()

    def dma(self, out, in_, key, reads=(), writes=(), **kw):
        key = self.pref + key
        self._op("sp", lambda e, sems: e.dma_start(out=out, in_=in_, **kw).then_inc(sems[key], 16),
                 reads=reads, writes=writes, dma=key)

    def mmg(self, out, pairs, reads, writes):
        pairs = list(pairs)

        def fn(e):
            n = len(pairs)
            ins = None
            for i, (l, r) in enumerate(pairs):
                ins = e.matmul(out, lhsT=l, rhs=r, start=(i == 0), stop=(i == n - 1))
            return ins
        self._op("pe", fn, reads=reads, writes=writes)

    def mm(self, out, lhsT, rhs, start, stop, reads, writes):
        self._op("pe", lambda e: e.matmul(out, lhsT=lhsT, rhs=rhs, start=start, stop=stop), reads=reads, writes=writes)

    def tr(self, out, in_, ident, reads, writes):
        self._op("pe", lambda e: e.transpose(out, in_, ident), reads=reads, writes=writes)

    def act(self, out, in_, func, reads, writes, bias=None, scale=None, accum=None):
        kw = {}
        if bias is not None:
            kw["bias"] = bias
        if scale is not None:
            kw["scale"] = scale
        if accum is not None:
            kw["accum_out"] = accum
        self._op("act", lambda e: e.activation(out=out, in_=in_, func=func, **kw), reads=reads, writes=writes)

    def ts(self, eng, out, in0, s1, s2, op0, op1, reads, writes, accum=None):
        kw = {}
        if op1 is not None:
            kw["op1"] = op1
        if accum is not None:
            kw["accum_out"] = accum
        self._op(eng, lambda e: e.tensor_scalar(out=out, in0=in0, scalar1=s1, scalar2=s2, op0=op0, **kw),
                  reads=reads, writes=writes)

    def tt(self, eng, out, in0, in1, op, reads, writes):
        self._op(eng, lambda e: e.tensor_tensor(out=out, in0=in0, in1=in1, op=op), reads=reads, writes=writes)

    def stt(self, eng, out, in0, scalar, in1, op0, op1, reads, writes):
        self._op(eng, lambda e: e.scalar_tensor_tensor(out=out, in0=in0, scalar=scalar, in1=in1, op0=op0, op1=op1),
                  reads=reads, writes=writes)

    def cp(self, eng, out, in_, reads, writes):
        self._op(eng, lambda e: e.tensor_copy(out=out, in_=in_), reads=reads, writes=writes)

    def memset(self, eng, out, val, writes):
        self._op(eng, lambda e: e.memset(out, val), writes=writes)

    def recip(self, out, in_, reads, writes):
        self._op("dve", lambda e: e.reciprocal(out=out, in_=in_), reads=reads, writes=writes)

    def scan(self, out, d0, d1, init, reads, writes):
        self._op("dve", lambda e: e.tensor_tensor_scan(out=out, data0=d0, data1=d1, initial=init,
                                                        op0=ALU.mult, op1=ALU.add), reads=reads, writes=writes)

    def range_reduce(self, eng, x, tmpf, tmpi, key, tkeys):
        self.ts(eng, tmpf, x, 1.0 / TWO_PI, None, ALU.mult, None, [key], [tkeys[0]])
        self.cp(eng, tmpi, tmpf, [tkeys[0]], [tkeys[1]])
        self.cp(eng, tmpf, tmpi, [tkeys[1]], [tkeys[0]])
        self.stt(eng, x, tmpf, -TWO_PI, x, ALU.mult, ALU.add, [tkeys[0], key], [key])
        self.ts(eng, tmpf, x, PI, -TWO_PI, ALU.is_gt, ALU.mult, [key], [tkeys[0]])
        self.tt(eng, x, x, tmpf, ALU.add, [key, tkeys[0]], [key])
        self.ts(eng, tmpf, x, -PI, TWO_PI, ALU.is_lt, ALU.mult, [key], [tkeys[0]])
        self.tt(eng, x, x, tmpf, ALU.add, [key, tkeys[0]], [key])

    def load_w(self, dst, src, kcs, ncols, key, stg, stgkey, i0=0):
        cap = stg[0].shape[1]
        g = max(1, min(kcs, cap // ncols))
        ns = len(stg)
        engs = ("pool", "act", "dve")
        for gi, kc0 in enumerate(range(0, kcs, g)):
            gg = min(g, kcs - kc0)
            cnt = self.rr_i
            self.rr_i += 1
            par = cnt % ns
            sview = stg[par][:, 0:gg * ncols].rearrange("p (k n) -> p k n", k=gg)
            self.dma(sview, src[kc0 * 128:(kc0 + gg) * 128, :].rearrange("(k p) n -> p k n", p=128),
                     f"d_{stgkey}{par}", writes=[f"{stgkey}{par}"])
            eng = engs[cnt % 3]
            wk = [f"{key}_{kc}" for kc in range(kc0, kc0 + gg)]
            if eng == "act":
                self.act(dst[:, kc0:kc0 + gg, :], sview, AF.Copy, [f"{stgkey}{par}"], wk)
            else:
                self.cp(eng, dst[:, kc0:kc0 + gg, :], sview, [f"{stgkey}{par}"], wk)


def build(upto=None, debug=False, only=None):
    nc = bass.Bass("TRN2", target_bir_lowering=False)
    kinds = {}

    def din(name, shape, dt=F32):
        return nc.dram_tensor(name, list(shape), dt, kind="ExternalInput").ap()

    def dscr(name, shape, dt):
        kind = "ExternalOutput" if debug else "Internal"
        return nc.dram_tensor(name, list(shape), dt, kind=kind).ap()

    SPECS = {
        "x": ([NTOK, D], F32),
        "c": ([NB, D], F32),
        "pos": ([NB, SEQ], I32),
        "w_ada": ([2, D, 6 * D], F32),
        "b_ada": ([2, 6 * D], F32),
        "g_norm": ([2, 2, 128, 16], F32),
        "w_inR": ([2, D, 2816], F32),
        "w_vw": ([2, D, 136], F32),
        "g_qk": ([2, 128, 2], F32),
        "a_row": ([2, 3, 2048], F32),
        "a_st": ([2, 3, 128, 16], F32),
        "braw": ([2, 2, 128, 2048], F32),
        "craw": ([2, 2, 128, 2048], F32),
        "dskip": ([2, 128, 4], F32),
        "w_glu": ([2, 512, 512], F32),
        "w_pool": ([2, 128, 4, 128], F32),
        "pscale": ([2, 128, 4], F32),
        "p_a": ([2, 1024, D], F32),
        "p_b": ([2, 512, D], F32),
        "p_c": ([2, 512, D], F32),
        "w_gate": ([2, 3, D, D], F32),
        "b_gate": ([2, 3, 128, 16], F32),
        "w_out": ([2, D, D], F32),
        "w_up": ([2, D, 2 * DFF], F32),
        "conv_w": ([2, 128, 3, NFC], F32),
        "conv_b": ([2, 128, NFC], F32),
        "w_down": ([2, DFF, D], F32),
        "ident": ([128, 128], BF16),
        "rmat": ([2, 128, 128], F32),
        "invf": ([128, 2], F32),
        "iota": ([128, T], F32),
        "halv": ([128, 3 * NQ], F32),
        "invc": ([128, 4, 16], F32),
    }

    class _Lazy(dict):
        def __missing__(self, key):
            shape, dt = SPECS[key]
            v = din(key, shape, dt)
            self[key] = v
            return v
    I = _Lazy()
    OUT = nc.dram_tensor("out", [NTOK, D], F32, kind="ExternalOutput").ap()

    MOD = din("MOD", [2, NB, 6 * D]) if only else dscr("MOD", [2, NB, 6 * D], F32)
    HT = dscr("HT", [128, 16, NTOK], BF16)
    QT = dscr("QT", [128, 8, NTOK], BF16)
    KT = dscr("KT", [128, NTOK], BF16)
    QIT = dscr("QIT", [128, 4, NTOK], BF16)
    KIT = dscr("KIT", [128, NTOK], BF16)
    VV = dscr("VV", [NTOK, 128], BF16)
    WI = dscr("WI", [NTOK, 8], F32)
    UT = dscr("UT", [128, 4, NTOK], BF16)
    PT = dscr("PT", [128, 4, NTOK], BF16)
    OAT = dscr("OAT", [128, 8, NTOK], BF16)
    OBT = dscr("OBT", [128, 4, NTOK], BF16)
    OCT = dscr("OCT", [128, 4, NTOK], BF16)
    MT = dscr("MT", [128, 16, NTOK], BF16)
    XA = dscr("XA", [NTOK, D], F32)
    XB = dscr("XB", [NTOK, D], F32)

    phases_done = [0]

    def stop_now():
        phases_done[0] += 1
        return upto is not None and phases_done[0] > upto

    def phase_mod():
        P = Phase(nc, "p0")
        cT = P.sb("cT", [128, NB, 16], F32)
        cA = P.sb("cA", [128, NB, 16], F32)
        bada = P.sb("bada", [NB, 6 * D], F32)
        wst = [P.sb(f"wst{i}", [128, 16, 512], F32) for i in range(4)]
        mrow = [P.sb(f"mrow{i}", [NB, 512], F32) for i in range(2)]
        P.mkbanks(2)
        for b in range(NB):
            P.dma(cT[:, b, :], I["c"][b, :].rearrange("(kc p) -> p kc", p=128), "d_c", writes=["cT"], allow_slow_non_contiguous=True)
        P.act(cA[:], cT[:], AF.Silu, ["cT"], ["cA"])
        for l in range(2):
            P.dma(bada[:], I["b_ada"][l:l + 1, :].partition_broadcast(NB), "d_bada", writes=["bada"])
            for cg in range(24):
                par = cg % 4
                P.dma(wst[par][:], I["w_ada"][l, :, cg * 512:(cg + 1) * 512].rearrange("(kc p) n -> p kc n", p=128),
                      f"d_wst{par}", writes=[f"wst{par}"])
                bk, bkk = P.bank()
                P.mmg(bk[0:NB, :], [(cA[:, :, kc], wst[par][:, kc, :]) for kc in range(16)], ["cA", f"wst{par}"], [bkk])
                mp = cg % 2
                P.tt("dve", mrow[mp][:], bk[0:NB, :], bada[:, cg * 512:(cg + 1) * 512], ALU.add, [bkk, "bada"], [f"mrow{mp}"])
                P.dma(MOD[l, :, cg * 512:(cg + 1) * 512], mrow[mp][:], f"s_mrow{mp}", reads=[f"mrow{mp}"])
        P.close()

    def phase_norm(l, which, XS):
        P = Phase(nc, f"n{l}{which}")
        epsb = P.sb("epsb", [128, 1], F32)
        ident = P.sb("ident", [128, 128], BF16)
        sc = P.sb("sc", [128, 16], F32)
        sh = [P.sb(f"sh{i}", [128, 16], F32) for i in range(NB)]
        g = P.sb("g", [128, 16], F32)
        A = [P.sb(f"A{i}", [128, 16], F32) for i in range(NB)]
        P.memset("dve", epsb[:], EPS, ["@epsb"])
        P.dma(ident[:], I["ident"], "d_id", writes=["@ident"])
        P.dma(g[:], I["g_norm"][l, which], "d_g", writes=["g"])
        off = 3 * which * D
        for b in range(NB):
            P.dma(sh[b][:], MOD[l, b, off:off + D].rearrange("(kc p) -> p kc", p=128), f"d_sh{b}", writes=[f"@sh{b}"],
                  allow_slow_non_contiguous=True)
            P.dma(sc[:], MOD[l, b, off + D:off + 2 * D].rearrange("(kc p) -> p kc", p=128), "d_sc", writes=["sc"],
                  allow_slow_non_contiguous=True)
            P.stt("dve", A[b][:], sc[:], 1.0, g[:], ALU.add, ALU.mult, ["sc", "g"], [f"@A{b}"])
        for b in range(NB):
            sfx = f"_{b}"
            xt = [P.sb(f"xt{i}" + sfx, [128, D], F32) for i in range(2)]
            junk = P.sb("junk" + sfx, [128, D], BF16)
            xn = [P.sb(f"xn{i}" + sfx, [128, D], BF16) for i in range(2)]
            hst = [P.sb(f"hst{i}" + sfx, [128, 16, 128], BF16) for i in range(2)]
            ssq = [P.sb(f"ssq{i}" + sfx, [128, 1], F32) for i in range(2)]
            rt = [P.sb(f"rt{i}" + sfx, [128, 1], F32) for i in range(2)]
            rstd = [P.sb(f"rstd{i}" + sfx, [128, 1], F32) for i in range(2)]
            pbk = [P.ps(f"pb{i}" + sfx, (128, 1024), BF16) for i in range(4)]
            P.begin_task(f"b{b}")
            for R in range(b * 16, (b + 1) * 16):
                par = R % 2
                P.dma(xt[par][:], XS[R * 128:(R + 1) * 128, :], f"d_xt{par}", writes=[f"xt{par}"])
                P.act(junk[:], xt[par][:], AF.Square, [f"xt{par}"], ["junk", f"ssq{par}"], accum=ssq[par][:])
                P.act(rt[par][:], ssq[par][:], AF.Sqrt, [f"ssq{par}", "@epsb"], [f"rt{par}"], bias=epsb[:], scale=1.0 / D)
                P.recip(rstd[par][:], rt[par][:], [f"rt{par}"], [f"rstd{par}"])
                P.ts("dve", xn[par][:], xt[par][:], rstd[par][:, 0:1], None, ALU.mult, None, [f"xt{par}", f"rstd{par}"], [f"xn{par}"])
                for q4 in range(4):
                    for j in range(4):
                        kc = q4 * 4 + j
                        P.tr(pbk[q4][:, j * 128:(j + 1) * 128], xn[par][:, kc * 128:(kc + 1) * 128], ident[:],
                             [f"xn{par}", "@ident"], [f"pb{q4}"])
                    for j in range(4):
                        kc = q4 * 4 + j
                        P.act(hst[par][:, kc, :], pbk[q4][:, j * 128:(j + 1) * 128], AF.Identity,
                              [f"pb{q4}", f"@A{b}", f"@sh{b}"], [f"hst{par}_{kc}"], bias=sh[b][:, kc:kc + 1], scale=A[b][:, kc:kc + 1])
                P.dma(HT[:, :, R * 128:(R + 1) * 128], hst[par][:], f"s_hst{par}",
                      reads=[f"hst{par}_{kc}" for kc in range(16)])
            P.end_task()
        P.run_tasks()
        P.close()

    def phase_win(l):
        P = Phase(nc, f"w{l}")
        Wq = P.sb("Wq", [128, 16, 2816], BF16)
        Wv = P.sb("Wv", [128, 16, 136], BF16)
        stg = [P.sb(f"stg{i}", [128, 2816], F32) for i in range(3)]
        hT = [P.sb(f"hT{i}", [128, 16, T], BF16) for i in range(2)]
        ones = P.sb("ones", [128, 128], BF16)
        rmat = P.sb("rmat", [128, 2, 128], F32)
        invf = P.sb("invf", [128, 2], F32)
        gqk = P.sb("gqk", [128, 2], F32)
        epsb = P.sb("epsb", [128, 1], F32)
        posi = [P.sb(f"posi{i}", [128, T], I32) for i in range(2)]
        posf = [P.sb(f"posf{i}", [128, T], F32) for i in range(2)]
        tabs = [[P.sb(f"tab{j}_{i}", [128, T], F32) for i in range(4)] for j in range(2)]
        tmpf = P.sb("tmpf", [128, T], F32)
        tmpi = P.sb("tmpi", [128, T], I32)
        sqb = [P.sb(f"sqb{i}", [128, T], BF16) for i in range(2)]
        rtt = [P.sb(f"rtt{i}", [128, T], F32) for i in range(2)]
        qn = [P.sb(f"qn{i}", [128, T], F32) for i in range(2)]
        t1 = [P.sb(f"t1{i}", [128, T], F32) for i in range(2)]
        t2 = [P.sb(f"t2{i}", [128, T], F32) for i in range(2)]
        ost = [P.sb(f"ost{i}", [128, T], BF16) for i in range(2)]
        vst = [P.sb(f"vst{i}", [128, 128], BF16) for i in range(2)]
        wst = [P.sb(f"wst{i}", [128, 8], F32) for i in range(2)]
        P.mkbanks(8)
        P.memset("dve", epsb[:], EPS, ["epsb"])
        P.memset("dve", ones[:], 1.0 / 128.0, ["ones"])
        P.dma(rmat[:], I["rmat"].rearrange("r k m -> k r m"), "d_rm", writes=["rmat"])
        P.dma(invf[:], I["invf"], "d_if", writes=["invf"])
        P.dma(gqk[:], I["g_qk"][l], "d_gqk", writes=["gqk"])
        P.load_w(Wv, I["w_vw"][l], 16, 136, "Wv", stg, "stg")
        P.load_w(Wq, I["w_inR"][l], 16, 2816, "Wq", stg, "stg")
        WqK = [f"Wq_{kc}" for kc in range(16)]
        WvK = [f"Wv_{kc}" for kc in range(16)]
        def table_ops(ti):
            b_ = ti // 4
            t0_ = ti * T
            tp = ti % 2
            pk = f"tb{tp}_"
            ops = []
            ops.append(lambda: P.dma(posi[tp][:], I["pos"][b_:b_ + 1, t0_ - b_ * SEQ:t0_ - b_ * SEQ + T].partition_broadcast(128),
                                     f"d_pos{tp}", writes=[pk + "posi"]))
            ops.append(lambda: P.cp("dve", posf[tp][:], posi[tp][:], [pk + "posi"], [pk + "posf"]))
            for k in range(4):
                which = k // 2
                shift = (PI / 2.0) if (k % 2 == 0) else 0.0
                x = tabs[tp][k][:]
                key = pk + f"tab{k}"
                ops.append(lambda x=x, key=key, which=which, shift=shift: P.ts(
                    "dve", x, posf[tp][:], invf[:, which:which + 1], shift, ALU.mult, ALU.add, [pk + "posf", "invf"], [key]))
                tk = ["tmpf", "tmpi"]
                ops.append(lambda x=x, key=key: P.ts("dve", tmpf[:], x, 1.0 / TWO_PI, None, ALU.mult, None, [key], [tk[0]]))
                ops.append(lambda: P.cp("dve", tmpi[:], tmpf[:], [tk[0]], [tk[1]]))
                ops.append(lambda: P.cp("dve", tmpf[:], tmpi[:], [tk[1]], [tk[0]]))
                ops.append(lambda x=x, key=key: P.stt("dve", x, tmpf[:], -TWO_PI, x, ALU.mult, ALU.add, [tk[0], key], [key]))
                ops.append(lambda x=x, key=key: P.ts("dve", tmpf[:], x, PI, -TWO_PI, ALU.is_gt, ALU.mult, [key], [tk[0]]))
                ops.append(lambda x=x, key=key: P.tt("dve", x, x, tmpf[:], ALU.add, [key, tk[0]], [key]))
                ops.append(lambda x=x, key=key: P.ts("dve", tmpf[:], x, -PI, TWO_PI, ALU.is_lt, ALU.mult, [key], [tk[0]]))
                ops.append(lambda x=x, key=key: P.tt("dve", x, x, tmpf[:], ALU.add, [key, tk[0]], [key]))
                ops.append(lambda x=x, key=key: P.act(x, x, AF.Sin, [key], [key]))
            return ops

        cnt = 0
        for ti in range(NT):
            b = ti // 4
            t0 = ti * T
            hp = ti % 2
            P.dma(hT[hp][:], HT[:, :, t0:t0 + T], f"d_hT{hp}", writes=[f"hT{hp}"])
            if ti == 0:
                for th_ in table_ops(0):
                    th_()
            nxt = table_ops(ti + 1) if ti + 1 < NT else []
            tab = tabs[ti % 2]
            tpk = f"tb{ti % 2}_"
            for m in range(22):
                bk, bkk = P.bank()
                P.mmg(bk[:], [(Wq[:, kc, m * 128:(m + 1) * 128], hT[hp][:, kc, :]) for kc in range(16)],
                      WqK + [f"hT{hp}"], [bkk])
                pr = cnt % 2
                cnt += 1
                if m <= 13:
                    isqk = m <= 8
                    if isqk:
                        gcol = 0 if m < 8 else 1
                        P.act(sqb[pr][:], bk[:], AF.Square, [bkk], [f"sqb{pr}"])
                        bk2, bkk2 = P.bank()
                        P.mm(bk2[:], ones[:], sqb[pr][:], True, True, ["ones", f"sqb{pr}"], [bkk2])
                        P.act(rtt[pr][:], bk2[:], AF.Sqrt, [bkk2, "epsb"], [f"rtt{pr}"], bias=epsb[:], scale=1.0)
                        P.recip(rtt[pr][:], rtt[pr][:], [f"rtt{pr}"], [f"rtt{pr}"])
                        P.stt("dve", qn[pr][:], bk[:], gqk[:, gcol:gcol + 1], rtt[pr][:], ALU.mult, ALU.mult,
                              [bkk, "gqk", f"rtt{pr}"], [f"qn{pr}"])
                        ri, Ct, St, Ck, Sk = 0, tab[0], tab[1], tpk + "tab0", tpk + "tab1"
                    else:
                        P.act(qn[pr][:], bk[:], AF.Copy, [bkk], [f"qn{pr}"])
                        ri, Ct, St, Ck, Sk = 1, tab[2], tab[3], tpk + "tab2", tpk + "tab3"
                    bk3, bkk3 = P.bank()
                    P.mm(bk3[:], rmat[:, ri, :], qn[pr][:], True, True, ["rmat", f"qn{pr}"], [bkk3])
                    P.tt("pool", t1[pr][:], qn[pr][:], Ct[:], ALU.mult, [f"qn{pr}", Ck], [f"t1{pr}"])
                    P.tt("dve", t2[pr][:], bk3[:], St[:], ALU.mult, [bkk3, Sk], [f"t2{pr}"])
                    P.tt("dve", ost[pr][:], t1[pr][:], t2[pr][:], ALU.add, [f"t1{pr}", f"t2{pr}"], [f"ost{pr}"])
                    if m < 8:
                        dst = QT[:, m, t0:t0 + T]
                    elif m == 8:
                        dst = KT[:, t0:t0 + T]
                    elif m < 13:
                        dst = QIT[:, m - 9, t0:t0 + T]
                    else:
                        dst = KIT[:, t0:t0 + T]
                else:
                    P.act(ost[pr][:], bk[:], AF.Copy, [bkk], [f"ost{pr}"])
                    dst = UT[:, m - 14, t0:t0 + T] if m < 18 else PT[:, m - 18, t0:t0 + T]
                P.dma(dst, ost[pr][:], f"s_ost{pr}", reads=[f"ost{pr}"])
                for _ in range(2):
                    if nxt:
                        nxt.pop(0)()
            while nxt:
                nxt.pop(0)()
            for ts_ in range(4):
                bk, bkk = P.bank()
                pr = ts_ % 2
                P.mmg(bk[:, 0:136], [(hT[hp][:, kc, ts_ * 128:(ts_ + 1) * 128], Wv[:, kc, :]) for kc in range(16)],
                      WvK + [f"hT{hp}"], [bkk])
                P.act(vst[pr][:], bk[:, 0:128], AF.Copy, [bkk], [f"vst{pr}"])
                P.act(wst[pr][:], bk[:, 128:136], AF.Copy, [bkk], [f"wst{pr}"], scale=8.0 ** -0.5)
                r0 = t0 + ts_ * 128
                P.dma(VV[r0:r0 + 128, :], vst[pr][:], f"s_vst{pr}", reads=[f"vst{pr}"])
                P.dma(WI[r0:r0 + 128, :], wst[pr][:], f"s_wst{pr}", reads=[f"wst{pr}"])
        P.close()

    def phase_dsa(l):
        P = Phase(nc, f"a{l}")
        ident = P.sb("ident", [128, 128], BF16)
        ones = P.sb("ones", [128, 128], BF16)
        halv = P.sb("halv", [128, 3 * NQ], F32)
        P.dma(ident[:], I["ident"], "d_id", writes=["@ident"])
        P.dma(halv[:], I["halv"], "d_hv", writes=["@halv"])
        P.memset("dve", ones[:], 1.0, ["@ones"])
        AXX = mybir.AxisListType.X
        for b in range(NB):
            sfx = f"_{b}"
            Kc = P.sb("Kc" + sfx, [128, SEQ], BF16)
            Kic = P.sb("Kic" + sfx, [128, SEQ], BF16)
            Vc = P.sb("Vc" + sfx, [128, 16, 128], BF16)
            qT = [P.sb(f"qT{i}" + sfx, [128, 8, 128], BF16) for i in range(2)]
            qiT = [P.sb(f"qiT{i}" + sfx, [128, 4, 128], BF16) for i in range(2)]
            wi = [P.sb(f"wi{i}" + sfx, [128, 8], F32) for i in range(2)]
            score = P.sb("score" + sfx, [128, SEQ], F32)
            rl = [P.sb(f"rl{i}" + sfx, [128, 512], F32) for i in range(3)]
            junk = P.sb("junk" + sfx, [128, SEQ], F32)
            mask = P.sb("mask" + sfx, [128, SEQ], BF16)
            maskT = P.sb("maskT" + sfx, [128, 16, 128], BF16)
            pT = [P.sb(f"pT{i}" + sfx, [128, 512], BF16) for i in range(2)]
            wk = P.sb("wk" + sfx, [128, 3 * NQ], F32)
            th3 = P.sb("th3" + sfx, [128, 3], F32)
            cs = P.sb("cs" + sfx, [128, 2], F32)
            g2j = P.sb("g2j" + sfx, [128, 2], F32)
            gs = P.sb("gs" + sfx, [128, 1], F32)
            gs2 = P.sb("gs2" + sfx, [128, 1], F32)
            junkA = P.sb("junkA" + sfx, [128, SEQ], BF16)
            junkB = P.sb("junkB" + sfx, [128, SEQ], BF16)
            lo = P.sb("lo" + sfx, [128, 1], F32)
            hi = P.sb("hi" + sfx, [128, 1], F32)
            mid = P.sb("mid" + sfx, [128, 1], F32)
            cntt = P.sb("cntt" + sfx, [128, 1], F32)
            stp = P.sb("stp" + sfx, [128, 1], F32)
            rinv = P.sb("rinv" + sfx, [128, 512], F32)
            ost = [P.sb(f"ost{i}" + sfx, [128, 8, 128], BF16) for i in range(2)]
            psS = [P.ps("psS0" + sfx), P.ps("psS1" + sfx)]
            psO = P.ps("psO" + sfx)
            psI = P.ps("psI" + sfx)
            psR = psI
            psT = psI.bitcast(BF16)
            ibanks = [(psI, "psI"), (psS[0], "psS0"), (psS[1], "psS1")]
            P.begin_task(f"b{b}")
            s0 = b * SEQ
            P.dma(Kc[:], KT[:, s0:s0 + SEQ], "d_Kc", writes=["Kc"])
            P.dma(Kic[:], KIT[:, s0:s0 + SEQ], "d_Kic", writes=["Kic"])
            P.dma(Vc[:], VV[s0:s0 + SEQ, :].rearrange("(kt p) d -> p kt d", p=128), "d_Vc", writes=["Vc"])
            for qb in range(16):
                par = qb % 2
                r0 = s0 + qb * 128
                N = 128 * (qb + 1)
                nkt = qb + 1
                P.dma(qT[par][:], QT[:, :, r0:r0 + 128], f"d_qT{par}", writes=[f"qT{par}"])
                P.dma(qiT[par][:], QIT[:, :, r0:r0 + 128], f"d_qiT{par}", writes=[f"qiT{par}"])
                P.dma(wi[par][:], WI[r0:r0 + 128, :], f"d_wi{par}", writes=[f"wi{par}"])
                nkb = (N + 511) // 512
                ci = 0
                for kb in range(nkb):
                    c0 = kb * 512
                    cw_ = min(512, N - c0)
                    for h in range(8):
                        po = 64 * (h % 2)
                        ibk, ibkk = ibanks[ci % 3]
                        P.mm(ibk[:, 0:cw_], qiT[par][po:po + 64, h // 2, :], Kic[po:po + 64, c0:c0 + cw_], True, True,
                             [f"qiT{par}", "Kic"], [ibkk])
                        rp = ci % 3
                        ci += 1
                        P.act(rl[rp][:, 0:cw_], ibk[:, 0:cw_], AF.Relu, [ibkk], [f"rl{rp}"], scale=0.125)
                        if h == 0:
                            P.ts("dve", score[:, c0:c0 + cw_], rl[rp][:, 0:cw_], wi[par][:, 0:1], None, ALU.mult, None,
                                 [f"rl{rp}", f"wi{par}"], ["score"])
                        else:
                            P.stt("dve", score[:, c0:c0 + cw_], rl[rp][:, 0:cw_], wi[par][:, h:h + 1], score[:, c0:c0 + cw_],
                                  ALU.mult, ALU.add, [f"rl{rp}", f"wi{par}", "score"], ["score"])
                if qb >= 2:
                    P._op("dve", lambda e, N=N, lo=lo, score=score: e.tensor_reduce(out=lo[:], in_=score[:, 0:N - 64], axis=AXX, op=ALU.min),
                          reads=["score"], writes=["lo"])
                    P._op("dve", lambda e, N=N, hi=hi, score=score: e.tensor_reduce(out=hi[:], in_=score[:, 0:N - 64], axis=AXX, op=ALU.max),
                          reads=["score"], writes=["hi"])
                    P._op("dve", lambda e, N=N, mid=mid, score=score: e.tensor_reduce(out=mid[64:128, :], in_=score[64:128, N - 64:N], axis=AXX, op=ALU.min),
                          reads=["score"], writes=["mid"])
                    P.tt("dve", lo[64:128, :], lo[64:128, :], mid[64:128, :], ALU.min, ["lo", "mid"], ["lo"])
                    P._op("dve", lambda e, N=N, mid=mid, score=score: e.tensor_reduce(out=mid[64:128, :], in_=score[64:128, N - 64:N], axis=AXX, op=ALU.max),
                          reads=["score"], writes=["mid"])
                    P.tt("dve", hi[64:128, :], hi[64:128, :], mid[64:128, :], ALU.max, ["hi", "mid"], ["hi"])
                P.memset("dve", score[0:64, N - 64:N], -1e30, ["score"])
                if qb >= 2:
                    P.tt("dve", hi[:], hi[:], lo[:], ALU.subtract, ["hi", "lo"], ["hi"])
                    P.ts("dve", hi[:], hi[:], 1.001, 1e-6, ALU.mult, ALU.add, ["hi"], ["hi"])
                    P.stt("dve", lo[:], hi[:], -0.0005, lo[:], ALU.mult, ALU.add, ["hi", "lo"], ["lo"])
                    P.ts("dve", wk[:], halv[:], hi[:, 0:1], None, ALU.mult, None, ["@halv", "hi"], ["wk"])
                    for it in range(NQ):
                        P.ts("dve", th3[:], wk[:, 3 * it:3 * it + 3], lo[:, 0:1], None, ALU.add, None, ["wk", "lo"], ["th3"])
                        P.act(junkA[:, 0:N], score[:, 0:N], AF.Sign, ["score", "th3"], ["junkA", "cs0"], bias=th3[:, 0:1], scale=-1.0,
                              accum=cs[:, 0:1])
                        P.act(junkB[:, 0:N], score[:, 0:N], AF.Sign, ["score", "th3"], ["junkB", "cs1"], bias=th3[:, 1:2], scale=-1.0,
                              accum=cs[:, 1:2])
                        P.ts("dve", junk[:, 0:N], score[:, 0:N], th3[:, 2:3], 0.0, ALU.is_gt, ALU.add, ["score", "th3"],
                             ["junk", "cntt"], accum=cntt[:])
                        P.ts("dve", g2j[:], cs[:], float(N - 511), 0.0, ALU.is_le, ALU.add, ["cs0", "cs1"], ["g2j", "gs"], accum=gs[:])
                        P.stt("dve", gs2[:], cntt[:], 255.5, gs[:], ALU.is_ge, ALU.add, ["cntt", "gs"], ["gs2"])
                        P.stt("dve", lo[:], gs2[:], wk[:, 3 * it:3 * it + 1], lo[:], ALU.mult, ALU.add, ["gs2", "wk", "lo"], ["lo"])
                else:
                    P.memset("dve", lo[:], -1e29, ["lo"])
                P.ts("dve", mask[:, 0:N], score[:, 0:N], lo[:, 0:1], None, ALU.is_gt, None, ["score", "lo"], ["mask"])
                for kt in range(nkt):
                    j = kt % 4
                    P.tr(psT[:, j * 128:(j + 1) * 128], mask[:, kt * 128:(kt + 1) * 128], ident[:], ["mask", "@ident"], ["psI"])
                    P.act(maskT[:, kt, :], psT[:, j * 128:(j + 1) * 128], AF.Copy, ["psI"], [f"maskT{kt}"])
                for half in range(2):
                    for kt in range(nkt):
                        pp = kt % 2
                        P.mm(psS[pp][:], Kc[:, kt * 128:(kt + 1) * 128],
                             qT[par][:, 4 * half:4 * half + 4, :].rearrange("p a b -> p (a b)"), True, True,
                             ["Kc", f"qT{par}"], [f"psS{pp}"])
                        P.act(pT[pp][:], psS[pp][:], AF.Exp, [f"psS{pp}"], [f"pT{pp}"], scale=128.0 ** -0.5)
                        P.tt("pool" if kt % 2 else "dve", pT[pp][:].rearrange("p (a b) -> p a b", a=4),
                             pT[pp][:].rearrange("p (a b) -> p a b", a=4),
                             maskT[:, kt, :].unsqueeze(1).to_broadcast([128, 4, 128]), ALU.mult,
                             [f"pT{pp}", f"maskT{kt}"], [f"pT{pp}"])
                        P.mm(psO[:], Vc[:, kt, :], pT[pp][:], kt == 0, kt == nkt - 1, ["Vc", f"pT{pp}"], ["psO"])
                        P.mm(psR[:], ones[:], pT[pp][:], kt == 0, kt == nkt - 1, ["@ones", f"pT{pp}"], ["psI"])
                    P.recip(rinv[:], psR[:], ["psI"], ["rinv"])
                    P.tt("dve", ost[par][:, 4 * half:4 * half + 4, :].rearrange("p a b -> p (a b)"), psO[:], rinv[:],
                         ALU.mult, ["psO", "rinv"], [f"ost{par}_{half}"])
                P.dma(OAT[:, :, r0:r0 + 128], ost[par][:], f"s_ost{par}", reads=[f"ost{par}_0", f"ost{par}_1"])
            P.end_task()
        P.run_tasks()
        P.close()

    def phase_s5(l):
        P = Phase(nc, f"s{l}")
        arow = P.sb("arow", [128, 3, 2048], F32)
        w0 = [P.sb(f"w0{i}", [128, 2048], F32) for i in range(8)]
        wi32 = P.sb("wi32", [128, 2048], I32)
        ast = P.sb("ast", [128, 3, 16], F32)
        sm = [P.sb(f"sm{i}", [128, 16], F32) for i in range(8)]
        smi = P.sb("smi", [128, 16], I32)
        rho = P.sb("rho", [128, 16], F32)
        ETc = P.sb("ETc", [128, 16], F32)
        ETs = P.sb("ETs", [128, 16], F32)
        Ec = P.sb("Ec", [128, 16, T], BF16)
        Es = P.sb("Es", [128, 16, T], BF16)
        tmpi = P.sb("tmpi", [128, T], I32)
        fence = P.sb("fence", [128, 1], F32)
        Bre = P.sb("Bre", [128, 16, 128], BF16)
        Bim = P.sb("Bim", [128, 16, 128], BF16)
        Cre = P.sb("Cre", [128, 16, 128], BF16)
        Cim = P.sb("Cim", [128, 16, 128], BF16)
        Wg = P.sb("Wg", [128, 4, 512], BF16)
        stg = [P.sb(f"stg{i}", [128, 2048], F32) for i in range(1)]
        dsk = P.sb("dsk", [128, 4], F32)
        uT = [P.sb(f"uT{i}", [128, 4, T], BF16) for i in range(2)]
        sre = [P.sb(f"sre{i}", [128, T], BF16) for i in range(2)]
        sim = [P.sb(f"sim{i}", [128, T], BF16) for i in range(2)]
        car = P.sb("car", [128, 2, 16], F32)
        zl = P.sb("zl", [128, 2, 16], F32)
        ct = [P.sb(f"ct{i}", [128, 16], F32) for i in range(4)]
        ygf = P.sb("ygf", [128, 4, T], F32)
        ygb = P.sb("ygb", [128, 4, T], BF16)
        ost = [P.sb(f"ost{i}", [128, T], BF16) for i in range(2)]
        psA = [P.ps(f"psA{i}") for i in range(2)]
        psB = [P.ps(f"psB{i}") for i in range(2)]
        psY = [P.ps(f"psY{i}") for i in range(2)]
        psG = [P.ps(f"psG{i}") for i in range(2)]

        for a_ in range(3):
            P.dma(arow[:, a_, :], I["a_row"][l, a_:a_ + 1, :].partition_broadcast(128), "d_arow", writes=["arow"])
        P.dma(ast[:], I["a_st"][l].rearrange("a p t -> p a t"), "d_ast", writes=["ast"])
        P.dma(dsk[:], I["dskip"][l], "d_dsk", writes=["@dsk"])

        def lam_setup(eng, are, aim, ldt, W, Wi, pre, srck):
            dt_, ar, th, mag, cs, sn, tf, t2_ = W
            k = [f"{pre}{i}" for i in range(8)]
            P.act(dt_, ldt, AF.Exp, [srck], [k[0]])
            P.tt(eng, ar, are, dt_, ALU.mult, [srck, k[0]], [k[1]])
            P.tt(eng, th, aim, dt_, ALU.mult, [srck, k[0]], [k[2]])
            P.act(mag, ar, AF.Exp, [k[1]], [k[3]])
            P.ts(eng, sn, th, TWO_PI, None, ALU.add, None, [k[2]], [k[5]])
            P.ts(eng, cs, th, TWO_PI + PI / 2.0, None, ALU.add, None, [k[2]], [k[4]])
            P.range_reduce(eng, sn, tf, Wi, k[5], [k[6], pre + "i"])
            P.range_reduce(eng, cs, tf, Wi, k[4], [k[6], pre + "i"])
            return k

        W = [w[:] for w in w0]
        k = lam_setup("dve", arow[:, 0, :], arow[:, 1, :], arow[:, 2, :], W, wi32[:], "rw", "arow")
        dt_, ar, th, mag, cs, sn, tf, t2_ = W
        P.act(sn, sn, AF.Sin, [k[5]], [k[5]])
        P.act(cs, cs, AF.Sin, [k[4]], [k[4]])
        P.tt("dve", cs, cs, mag, ALU.mult, [k[4], k[3]], [k[4]])
        P.ts("dve", cs, cs, -1.0, None, ALU.add, None, [k[4]], [k[4]])
        P.tt("dve", sn, sn, mag, ALU.mult, [k[5], k[3]], [k[5]])
        P.tt("dve", tf, arow[:, 0, :], arow[:, 0, :], ALU.mult, ["arow"], [k[6]])
        P.tt("dve", t2_, arow[:, 1, :], arow[:, 1, :], ALU.mult, ["arow"], [k[7]])
        P.tt("dve", tf, tf, t2_, ALU.add, [k[6], k[7]], [k[6]])
        P.recip(tf, tf, [k[6]], [k[6]])
        P.tt("dve", dt_, cs, arow[:, 0, :], ALU.mult, [k[4], "arow"], [k[0]])
        P.tt("dve", t2_, sn, arow[:, 1, :], ALU.mult, [k[5], "arow"], [k[7]])
        P.tt("dve", dt_, dt_, t2_, ALU.add, [k[0], k[7]], [k[0]])
        P.tt("dve", dt_, dt_, tf, ALU.mult, [k[0], k[6]], [k[0]])
        P.tt("dve", ar, sn, arow[:, 0, :], ALU.mult, [k[5], "arow"], [k[1]])
        P.tt("dve", t2_, cs, arow[:, 1, :], ALU.mult, [k[4], "arow"], [k[7]])
        P.tt("dve", ar, ar, t2_, ALU.subtract, [k[1], k[7]], [k[1]])
        P.tt("dve", ar, ar, tf, ALU.mult, [k[1], k[6]], [k[1]])
        cre_, cim_ = dt_, ar
        P.dma(th, I["braw"][l, 0], "d_br", reads=[k[2]], writes=[k[2]])
        P.dma(mag, I["braw"][l, 1], "d_bi", reads=[k[3]], writes=[k[3]])
        P.tt("dve", cs, cre_, th, ALU.mult, [k[0], k[2]], [k[4]])
        P.tt("dve", sn, cim_, mag, ALU.mult, [k[1], k[3]], [k[5]])
        P.tt("dve", Bre[:].rearrange("p a b -> p (a b)"), cs, sn, ALU.subtract, [k[4], k[5]], ["@Bre"])
        P.tt("dve", cs, cre_, mag, ALU.mult, [k[0], k[3]], [k[4]])
        P.tt("dve", sn, cim_, th, ALU.mult, [k[1], k[2]], [k[5]])
        P.tt("dve", Bim[:].rearrange("p a b -> p (a b)"), cs, sn, ALU.add, [k[4], k[5]], ["@Bim"])
        P.dma(th, I["craw"][l, 0], "d_br", reads=[k[2]], writes=[k[2]])
        P.dma(mag, I["craw"][l, 1], "d_bi", reads=[k[3]], writes=[k[3]])
        P.cp("dve", Cre[:].rearrange("p a b -> p (a b)"), th, [k[2]], ["@Cre"])
        P.ts("dve", Cim[:].rearrange("p a b -> p (a b)"), mag, -1.0, None, ALU.mult, None, [k[3]], ["@Cim"])
        tnames = ["iota", "ang", "tmpf", "bre0", "bre1", "bim0", "bim1", "a1", "a20", "a21", "a3", "a40", "a41", "btr", "bti",
                  "zr0", "zr1", "zi0", "zi1", "m10", "m11", "m20", "m21", "m3", "m4", "yv", "x2", "inn", "sg", "sg20", "sg21"]
        P.memset("dve", fence[:], 0.0, list(k) + tnames)
        TT = {}
        for i_, nm in enumerate(tnames):
            TT[nm] = w0[i_ // 4][:, (i_ % 4) * T:(i_ % 4 + 1) * T]

        class _V:
            def __init__(self, ap):
                self.ap = ap

            def __getitem__(self, idx):
                return self.ap[idx]
        iota, ang, tmpf = _V(TT["iota"]), _V(TT["ang"]), _V(TT["tmpf"])
        bre = [_V(TT["bre0"]), _V(TT["bre1"])]
        bim = [_V(TT["bim0"]), _V(TT["bim1"])]
        a1 = _V(TT["a1"])
        a2 = [_V(TT["a20"]), _V(TT["a21"])]
        a3 = _V(TT["a3"])
        a4 = [_V(TT["a40"]), _V(TT["a41"])]
        btr, bti = _V(TT["btr"]), _V(TT["bti"])
        zr = [_V(TT["zr0"]), _V(TT["zr1"])]
        zi = [_V(TT["zi0"]), _V(TT["zi1"])]
        m1 = [_V(TT["m10"]), _V(TT["m11"])]
        m2 = [_V(TT["m20"]), _V(TT["m21"])]
        m3, m4 = _V(TT["m3"]), _V(TT["m4"])
        yv, x2, inn, sg = _V(TT["yv"]), _V(TT["x2"]), _V(TT["inn"]), _V(TT["sg"])
        sg2 = [_V(TT["sg20"]), _V(TT["sg21"])]
        P.dma(iota[:], I["iota"], "d_iota", writes=["iota"])
        S8 = [s[:] for s in sm]
        k2 = lam_setup("dve", ast[:, 0, :], ast[:, 1, :], ast[:, 2, :], S8, smi[:], "sw", "ast")
        sdt, sar, sth, smag, scs, ssn, stf, st2 = S8
        P.cp("dve", rho[:], smag, [k2[3]], ["@rho"])
        P.ts("dve", sth, ssn, PI, None, ALU.add, None, [k2[5]], [k2[2]])
        P.ts("dve", stf, sth, float(T), float((PI * T) % TWO_PI), ALU.mult, ALU.add, [k2[2]], [k2[6]])
        P.cp("dve", st2, stf, [k2[6]], [k2[7]])
        P.ts("dve", st2, st2, PI / 2.0, None, ALU.add, None, [k2[7]], [k2[7]])
        P.range_reduce("dve", stf, sdt, smi[:], k2[6], [k2[0], "swi"])
        P.range_reduce("dve", st2, sdt, smi[:], k2[7], [k2[0], "swi"])
        P.act(ETs[:], stf, AF.Sin, [k2[6]], ["ETs"])
        P.act(ETc[:], st2, AF.Sin, [k2[7]], ["ETc"])
        P.ts("dve", ang[:], iota[:], PI, None, ALU.mult, None, ["iota"], ["ang"])
        P.range_reduce("dve", ang[:], tmpf[:], tmpi[:], "ang", ["tmpf", "tmpi"])
        P.ts("dve", ang[:], ang[:], PI, None, ALU.add, None, ["ang"], ["ang"])
        for st_ in range(16):
            for cs_i, (dstT, shift) in enumerate(((Es, 0.0), (Ec, PI / 2.0))):
                P.stt("dve", a1[:], iota[:], sth[:, st_:st_ + 1], ang[:], ALU.mult, ALU.add, ["iota", k2[2], "ang"], ["a1"])
                if shift:
                    P.ts("dve", a1[:], a1[:], shift, None, ALU.add, None, ["a1"], ["a1"])
                P.range_reduce("dve", a1[:], tmpf[:], tmpi[:], "a1", ["tmpf", "tmpi"])
                P.act(dstT[:, st_, :], a1[:], AF.Sin, ["a1"], [f"@E{cs_i}_{st_}"])
        EK = lambda st_: [f"@E0_{st_}", f"@E1_{st_}"]
        P.load_w(Wg, I["w_glu"][l], 4, 512, "Wg", stg, "stg")
        WgK = [f"Wg_{kc}" for kc in range(4)]
        X1 = {nm: P.sb("x1_" + nm, [128, T], F32) for nm in ("a1", "a3", "btr", "bti", "m3", "m4", "yv", "x2", "inn", "sg")}
        singles = [dict(a1=a1, a3=a3, btr=btr, bti=bti, m3=m3, m4=m4, yv=yv, x2=x2, inn=inn, sg=sg), X1]
        dumA = P.sb("dumA", [128, 1], F32)
        dumP = P.sb("dumP", [128, 1], F32)
        P.memset("dve", fence[:], 1.0, list(k) + tnames + ["@fence"])
        P.act(dumA[:], fence[:], AF.Copy, ["@fence"], ["dumA"])
        P.cp("pool", dumP[:], fence[:], ["@fence"], ["dumP"])

        for ti in range(NT):
            t0 = ti * T
            up = ti % 2
            P.dma(uT[up][:], UT[:, :, t0:t0 + T], f"d_uT{up}", writes=[f"@uT{up}"])
            if ti % 4 == 0:
                P.memset("pool", car[:], 0.0, ["@car"])
            for e in range(2):
                P.begin_task(f"e{e}")
                sgl = singles[e]
                a1_, a3_, btr_, bti_, m3_, m4_ = sgl["a1"], sgl["a3"], sgl["btr"], sgl["bti"], sgl["m3"], sgl["m4"]
                yv_, x2_, inn_, sg_ = sgl["yv"], sgl["x2"], sgl["inn"], sgl["sg"]
                pr = e
                for c in (e, e + 2):
                    for q in range(4):
                        st_ = 4 * c + q
                        P.mm(psA[pr][:], Bre[:, st_, :], uT[up][:, c, :], True, True, ["@Bre", f"@uT{up}"], ["psA"])
                        P.mm(psB[pr][:], Bim[:, st_, :], uT[up][:, c, :], True, True, ["@Bim", f"@uT{up}"], ["psB"])
                        P.act(bre[pr][:], psA[pr][:], AF.Copy, ["psA"], ["bre"])
                        P.act(bim[pr][:], psB[pr][:], AF.Copy, ["psB"], ["bim"])
                        ek = EK(st_)
                        P.tt("pool", a2[pr][:], bim[pr][:], Es[:, st_, :], ALU.mult, ["bim"] + ek, ["a2"])
                        P.tt("pool", a4[pr][:], bre[pr][:], Es[:, st_, :], ALU.mult, ["bre"] + ek, ["a4"])
                        P.tt("dve", a1_[:], bre[pr][:], Ec[:, st_, :], ALU.mult, ["bre"] + ek, ["a1"])
                        P.tt("dve", btr_[:], a1_[:], a2[pr][:], ALU.add, ["a1", "a2"], ["btr"])
                        P.tt("pool", a3_[:], bim[pr][:], Ec[:, st_, :], ALU.mult, ["bim"] + ek, ["a3"])
                        P.tt("dve", bti_[:], a3_[:], a4[pr][:], ALU.subtract, ["a3", "a4"], ["bti"])
                        P.scan(zr[pr][:], rho[:, st_:st_ + 1].to_broadcast([128, T]), btr_[:], car[:, 0, st_:st_ + 1],
                               ["@rho", "btr", "@car"], ["zr"])
                        P.scan(zi[pr][:], rho[:, st_:st_ + 1].to_broadcast([128, T]), bti_[:], car[:, 1, st_:st_ + 1],
                               ["@rho", "bti", "@car"], ["zi"])
                        P.cp("pool", zl[:, 0, st_:st_ + 1], zr[pr][:, T - 1:T], ["zr"], [f"@zlr{st_}"])
                        P.cp("pool", zl[:, 1, st_:st_ + 1], zi[pr][:, T - 1:T], ["zi"], [f"@zli{st_}"])
                        P.tt("pool", m1[pr][:], zi[pr][:], Es[:, st_, :], ALU.mult, ["zi"] + ek, ["m1"])
                        P.tt("pool", m2[pr][:], zr[pr][:], Es[:, st_, :], ALU.mult, ["zr"] + ek, ["m2"])
                        P.tt("dve", m3_[:], zr[pr][:], Ec[:, st_, :], ALU.mult, ["zr"] + ek, ["m3"])
                        P.tt("dve", sre[pr][:], m3_[:], m1[pr][:], ALU.subtract, ["m3", "m1"], ["sre"])
                        P.tt("pool", m4_[:], zi[pr][:], Ec[:, st_, :], ALU.mult, ["zi"] + ek, ["m4"])
                        P.tt("dve", sim[pr][:], m4_[:], m2[pr][:], ALU.add, ["m4", "m2"], ["sim"])
                        P.mm(psY[pr][:], Cre[:, st_, :], sre[pr][:], q == 0, False, ["@Cre", "sre"], ["psY"])
                        P.mm(psY[pr][:], Cim[:, st_, :], sim[pr][:], False, q == 3, ["@Cim", "sim"], ["psY"])
                    P.stt("dve", yv_[:], uT[up][:, c, :], dsk[:, c:c + 1], psY[pr][:], ALU.mult, ALU.add,
                          [f"@uT{up}", "@dsk", "psY"], ["yv"])
                    P.act(x2_[:], yv_[:], AF.Square, ["yv"], ["x2"])
                    P.ts("dve", x2_[:], x2_[:], 0.044715, 1.0, ALU.mult, ALU.add, ["x2"], ["x2"])
                    P.tt("dve", inn_[:], x2_[:], yv_[:], ALU.mult, ["x2", "yv"], ["inn"])
                    P.act(sg_[:], inn_[:], AF.Sigmoid, ["inn"], ["sg"], scale=2.0 * math.sqrt(2.0 / PI))
                    P.tt("dve", ygf[:, c, :], yv_[:], sg_[:], ALU.mult, ["yv", "sg"], [f"@ygf{c}"])
                    P.cp("pool", ygb[:, c, :], ygf[:, c, :], [f"@ygf{c}"], [f"@ygb{c}"])
                P.end_task()
            P.run_tasks()
            ZK = [f"@zlr{i}" for i in range(16)] + [f"@zli{i}" for i in range(16)]
            P.tt("pool", ct[0][:], zl[:, 0, :], ETc[:], ALU.mult, ZK + ["ETc"], ["ct0"])
            P.tt("pool", ct[1][:], zl[:, 1, :], ETs[:], ALU.mult, ZK + ["ETs"], ["ct1"])
            P.tt("pool", ct[2][:], zl[:, 1, :], ETc[:], ALU.mult, ZK + ["ETc"], ["ct2"])
            P.tt("pool", ct[3][:], zl[:, 0, :], ETs[:], ALU.mult, ZK + ["ETs"], ["ct3"])
            P.tt("pool", car[:, 0, :], ct[0][:], ct[1][:], ALU.subtract, ["ct0", "ct1"], ["@car"])
            P.tt("pool", car[:, 1, :], ct[2][:], ct[3][:], ALU.add, ["ct2", "ct3"], ["@car"])
            for m in range(4):
                gp = m % 2
                P.mmg(psG[gp][:], [(Wg[:, kc, m * 128:(m + 1) * 128], ygb[:, kc, :]) for kc in range(4)],
                      WgK + [f"@ygb{kc}" for kc in range(4)], [f"psG{gp}"])
                P.act(sg2[gp][:], psG[gp][:], AF.Sigmoid, [f"psG{gp}"], [f"sg2{gp}"])
                P.tt("dve", ost[gp][:], ygf[:, m, :], sg2[gp][:], ALU.mult, [f"@ygf{m}", f"sg2{gp}"], [f"ost{gp}"])
                P.dma(OBT[:, m, t0:t0 + T], ost[gp][:], f"s_ost{gp}", reads=[f"ost{gp}"])
        P.close()

    def phase_pool(l):
        P = Phase(nc, f"c{l}")
        Wp = P.sb("Wp", [128, 4, 128], BF16)
        Wpf = P.sb("Wpf", [128, 4, 128], F32)
        psc = P.sb("psc", [128, 4], F32)
        invc = P.sb("invc", [128, 4, 16], F32)
        pb = [P.sb(f"pb{i}", [128, 4, 528], BF16) for i in range(2)]
        pf = P.sb("pf", [128, 4, 528], F32)
        sa = P.sb("sa", [128, 528], F32)
        sb_ = P.sb("sb", [128, 528], F32)
        fx = P.sb("fx", [128, 16], F32)
        pl = [P.sb(f"pl{i}", [128, T], BF16) for i in range(2)]
        ost = [P.sb(f"ost{i}", [128, T], BF16) for i in range(2)]
        P.mkbanks(2)
        P.dma(Wpf[:], I["w_pool"][l], "d_wp", writes=["Wpf"])
        P.cp("dve", Wp[:], Wpf[:], ["Wpf"], ["Wp"])
        P.dma(psc[:], I["pscale"][l], "d_psc", writes=["psc"])
        P.dma(invc[:], I["invc"], "d_invc", writes=["invc"])
        for ti in range(NT):
            t0 = ti * T
            pp = ti % 2
            first = (ti % 4 == 0)
            if first:
                P.memset("pool", pb[pp][:, :, 0:16], 0.0, [f"pb{pp}"])
                P.dma(pb[pp][:, :, 16:528], PT[:, :, t0:t0 + T], f"d_pb{pp}", writes=[f"pb{pp}"])
            else:
                P.dma(pb[pp][:, :, 1:528], PT[:, :, t0 - 15:t0 + T], f"d_pb{pp}", writes=[f"pb{pp}"])
            P.cp("pool", pf[:, :, 1:528], pb[pp][:, :, 1:528], [f"pb{pp}"], ["pf"])
            for g in range(4):
                w = 2 ** (g + 1)
                src = pf[:, g, :]
                bufs = [sa, sb_]
                cur = None
                d = 1
                i = 0
                while d < w:
                    dstb = bufs[i % 2]
                    s_in = src if cur is None else cur[:]
                    lo_c = 2 * d
                    P.tt("dve", dstb[:, lo_c:528], s_in[:, lo_c:528], s_in[:, lo_c - d:528 - d], ALU.add,
                         ["pf", "sa", "sb"], ["sa" if i % 2 == 0 else "sb"])
                    cur = dstb
                    d *= 2
                    i += 1
                gp = g % 2
                P.stt("dve", pl[gp][:], cur[:, 16:528], 1.0 / w, pf[:, g, 16:528], ALU.mult, ALU.subtract,
                      ["sa", "sb", "pf"], [f"pl{gp}"])
                if first:
                    P.tt("dve", fx[:], cur[:, 16:32], invc[:, g, :], ALU.mult, ["sa", "sb", "invc"], ["fx"])
                    P.tt("dve", pl[gp][:, 0:16], fx[:], pf[:, g, 16:32], ALU.subtract, ["fx", "pf", f"pl{gp}"], [f"pl{gp}"])
                bk, bkk = P.bank()
                P.mm(bk[:], Wp[:, g, :], pl[gp][:], True, True, ["Wp", f"pl{gp}"], [bkk])
                P.act(ost[gp][:], bk[:], AF.Copy, [bkk, "psc"], [f"ost{gp}"], scale=psc[:, g:g + 1])
                P.dma(OCT[:, g, t0:t0 + T], ost[gp][:], f"s_ost{gp}", reads=[f"ost{gp}"])
        P.close()

    def phase_merge(l, mg):
        P = Phase(nc, f"m{l}{mg}")
        Wg = [P.sb(f"Wg{b}", [128, 16, 512], BF16) for b in range(3)]
        KCB = (8, 4, 4)
        Pw = [P.sb(f"Pw{b}", [128, KCB[b], 512], BF16) for b in range(3)]
        stg = [P.sb(f"stg{i}", [128, 2048], F32) for i in range(4)]
        bg = P.sb("bg", [128, 3, 16], F32)
        hT = [P.sb(f"hT{i}", [128, 16, T], BF16) for i in range(2)]
        oT = [[P.sb(f"oT{b}_{i}", [128, KCB[b], T], BF16) for i in range(2)] for b in range(3)]
        sgb = [P.sb(f"sgb{i}", [128, T], F32) for i in range(3)]
        cb_ = [P.sb(f"cb{i}", [128, T], F32) for i in range(3)]
        ost = [P.sb(f"ost{i}", [128, T], BF16) for i in range(2)]
        P.mkbanks(8)
        P.dma(bg[:], I["b_gate"][l].rearrange("b p m -> p b m"), "d_bg", writes=["bg"])
        psrc = (I["p_a"], I["p_b"], I["p_c"])
        cs = slice(mg * 512, (mg + 1) * 512)
        i0 = 0
        for b in range(3):
            P.load_w(Wg[b], I["w_gate"][l, b][:, cs], 16, 512, f"Wg{b}", stg, "stg", i0)
            P.load_w(Pw[b], psrc[b][l][:, cs], KCB[b], 512, f"Pw{b}", stg, "stg", i0)
        OS = (OAT, OBT, OCT)
        for ti in range(NT):
            t0 = ti * T
            hp = ti % 2
            P.dma(hT[hp][:], HT[:, :, t0:t0 + T], f"d_hT{hp}", writes=[f"hT{hp}"])
            for b in range(3):
                P.dma(oT[b][hp][:], OS[b][:, :, t0:t0 + T], f"d_oT{b}{hp}", writes=[f"oT{b}{hp}"])
            for mi in range(4):
                m = mg * 4 + mi
                for b in range(3):
                    bk, bkk = P.bank()
                    P.mmg(bk[:], [(Wg[b][:, kc, mi * 128:(mi + 1) * 128], hT[hp][:, kc, :]) for kc in range(16)],
                          [f"Wg{b}_{kc}" for kc in range(16)] + [f"hT{hp}"], [bkk])
                    P.act(sgb[b][:], bk[:], AF.Sigmoid, [bkk, "bg"], [f"sgb{b}"], bias=bg[:, b, m:m + 1], scale=1.0)
                    bk2, bkk2 = P.bank()
                    P.mmg(bk2[:], [(Pw[b][:, kc, mi * 128:(mi + 1) * 128], oT[b][hp][:, kc, :]) for kc in range(KCB[b])],
                          [f"Pw{b}_{kc}" for kc in range(KCB[b])] + [f"oT{b}{hp}"], [bkk2])
                    P.tt("dve", cb_[b][:], sgb[b][:], bk2[:], ALU.mult, [f"sgb{b}", bkk2], [f"cb{b}"])
                op_ = mi % 2
                P.tt("pool", cb_[0][:], cb_[0][:], cb_[1][:], ALU.add, ["cb0", "cb1"], ["cb0"])
                P.tt("pool", ost[op_][:], cb_[0][:], cb_[2][:], ALU.add, ["cb0", "cb2"], [f"ost{op_}"])
                P.dma(MT[:, m, t0:t0 + T], ost[op_][:], f"s_ost{op_}", reads=[f"ost{op_}"])
        P.close()

    def phase_wout(l, XS, XD):
        P = Phase(nc, f"o{l}")
        Wo = P.sb("Wo", [128, 16, D], BF16)
        stg = [P.sb(f"stg{i}", [128, D], F32) for i in range(4)]
        gt = P.sb("gt", [128, D], F32)
        mT = [P.sb(f"mT{i}", [128, 16, T], BF16) for i in range(2)]
        xt = [P.sb(f"xt{i}", [128, D], F32) for i in range(2)]
        tmp = [P.sb(f"tmp{i}", [128, 512], F32) for i in range(2)]
        xo = [P.sb(f"xo{i}", [128, D], F32) for i in range(2)]
        P.mkbanks(4)
        junk = P.sb("junk", [128, D], BF16)
        xn = [P.sb(f"xn{i}", [128, D], BF16) for i in range(2)]
        hst = [P.sb(f"hst{i}", [128, 16, 128], BF16) for i in range(2)]
        ssq = [P.sb(f"ssq{i}", [128, 1], F32) for i in range(2)]
        rt = [P.sb(f"rt{i}", [128, 1], F32) for i in range(2)]
        rstd = [P.sb(f"rstd{i}", [128, 1], F32) for i in range(2)]
        epsb = P.sb("epsb", [128, 1], F32)
        ident = P.sb("ident", [128, 128], BF16)
        sc = P.sb("sc", [128, 16], F32)
        sh = [P.sb(f"sh{i}", [128, 16], F32) for i in range(NB)]
        g = P.sb("g", [128, 16], F32)
        A = [P.sb(f"A{i}", [128, 16], F32) for i in range(NB)]
        pbk = [P.ps(f"pb{i}", (128, 1024), BF16) for i in range(4)]
        P.memset("dve", epsb[:], EPS, ["epsb"])
        P.dma(ident[:], I["ident"], "d_id", writes=["ident"])
        P.dma(g[:], I["g_norm"][l, 1], "d_g", writes=["g"])
        off = 3 * D
        for b in range(NB):
            P.dma(sh[b][:], MOD[l, b, off:off + D].rearrange("(kc p) -> p kc", p=128), f"d_sh{b}", writes=[f"sh{b}"],
                  allow_slow_non_contiguous=True)
            P.dma(sc[:], MOD[l, b, off + D:off + 2 * D].rearrange("(kc p) -> p kc", p=128), "d_sc", writes=["sc"],
                  allow_slow_non_contiguous=True)
            P.stt("dve", A[b][:], sc[:], 1.0, g[:], ALU.add, ALU.mult, ["sc", "g"], [f"A{b}"])
        P.load_w(Wo, I["w_out"][l], 16, D, "Wo", stg, "stg")
        WoK = [f"Wo_{kc}" for kc in range(16)]
        for ti in range(NT):
            t0 = ti * T
            b = ti // 4
            hp = ti % 2
            if ti % 4 == 0:
                P.dma(gt[:], MOD[l, b:b + 1, 2 * D:3 * D].partition_broadcast(128), "d_gt", writes=["gt"])
            P.dma(mT[hp][:], MT[:, :, t0:t0 + T], f"d_mT{hp}", writes=[f"mT{hp}"])
            for ts_ in range(4):
                xp = ts_ % 2
                r0 = t0 + ts_ * 128
                P.dma(xt[xp][:], XS[r0:r0 + 128, :], f"d_xt{xp}", writes=[f"xt{xp}"])
                for n in range(4):
                    bk, bkk = P.bank()
                    P.mmg(bk[:], [(mT[hp][:, kc, ts_ * 128:(ts_ + 1) * 128], Wo[:, kc, n * 512:(n + 1) * 512]) for kc in range(16)],
                          WoK + [f"mT{hp}"], [bkk])
                    tp = n % 2
                    P.tt("dve", tmp[tp][:], bk[:], gt[:, n * 512:(n + 1) * 512], ALU.mult, [bkk, "gt"], [f"tmp{tp}"])
                    P.tt("pool", xo[xp][:, n * 512:(n + 1) * 512], tmp[tp][:], xt[xp][:, n * 512:(n + 1) * 512], ALU.add,
                         [f"tmp{tp}", f"xt{xp}"], [f"xo{xp}_{n}"])
                XK = [f"xo{xp}_{n}" for n in range(4)]
                P.dma(XD[r0:r0 + 128, :], xo[xp][:], f"s_xo{xp}", reads=XK)
                par = xp
                P.act(junk[:], xo[xp][:], AF.Square, XK, ["junk", f"ssq{par}"], accum=ssq[par][:])
                P.act(rt[par][:], ssq[par][:], AF.Sqrt, [f"ssq{par}", "epsb"], [f"rt{par}"], bias=epsb[:], scale=1.0 / D)
                P.recip(rstd[par][:], rt[par][:], [f"rt{par}"], [f"rstd{par}"])
                P.ts("dve", xn[par][:], xo[xp][:], rstd[par][:, 0:1], None, ALU.mult, None, XK + [f"rstd{par}"], [f"xn{par}"])
                for q4 in range(4):
                    for j in range(4):
                        kc = q4 * 4 + j
                        P.tr(pbk[q4][:, j * 128:(j + 1) * 128], xn[par][:, kc * 128:(kc + 1) * 128], ident[:],
                             [f"xn{par}", "ident"], [f"pb{q4}"])
                    for j in range(4):
                        kc = q4 * 4 + j
                        P.act(hst[par][:, kc, :], pbk[q4][:, j * 128:(j + 1) * 128], AF.Identity,
                              [f"pb{q4}", f"A{b}", f"sh{b}"], [f"hst{par}_{kc}"], bias=sh[b][:, kc:kc + 1], scale=A[b][:, kc:kc + 1])
                P.dma(HT[:, :, r0:r0 + 128], hst[par][:], f"s_hst{par}", reads=[f"hst{par}_{kc}" for kc in range(16)])
        P.close()

    def phase_ffn(l, j0, nj, XS, XD):
        P = Phase(nc, f"f{l}_{j0}")
        Wa = P.sb("Wa", [128, 16, nj * 128], BF16)
        Wb = P.sb("Wb", [128, 16, nj * 128], BF16)
        Wd = P.sb("Wd", [128, nj, D], BF16)
        stg = [P.sb(f"stg{i}", [128, 1152], F32) for i in range(3)]
        cw = P.sb("cw", [128, 3, NFC], F32)
        cbias = P.sb("cbias", [128, NFC], F32)
        gt = P.sb("gt", [128, D], F32)
        hT = [P.sb(f"hT{i}", [128, 16, T], BF16) for i in range(2)]
        abuf = [P.sb(f"abuf{i}", [128, T + 2], F32) for i in range(2)]
        acc = [P.sb(f"acc{i}", [128, T], F32) for i in range(2)]
        sl = [P.sb(f"sl{i}", [128, T], F32) for i in range(2)]
        carry = P.sb("carry", [128, nj, 2], F32)
        actT = P.sb("actT", [128, nj, T], BF16)
        xt = [P.sb(f"xt{i}", [128, D], F32) for i in range(2)]
        tmp = [P.sb(f"tmp{i}", [128, 512], F32) for i in range(2)]
        P.mkbanks(8)
        P.dma(cw[:], I["conv_w"][l], "d_cw", writes=["cw"])
        P.dma(cbias[:], I["conv_b"][l], "d_cb", writes=["cbias"])
        c0 = j0 * 128
        P.load_w(Wa, I["w_up"][l][:, c0:c0 + nj * 128], 16, nj * 128, "Wa", stg, "stg")
        P.load_w(Wb, I["w_up"][l][:, DFF + c0:DFF + c0 + nj * 128], 16, nj * 128, "Wb", stg, "stg")
        for hf in range(2):
            P.load_w(Wd[:, :, hf * 1024:(hf + 1) * 1024], I["w_down"][l][c0:c0 + nj * 128, hf * 1024:(hf + 1) * 1024], nj, 1024, f"Wd{hf}", stg, "stg")
        WaK = [f"Wa_{kc}" for kc in range(16)]
        WbK = [f"Wb_{kc}" for kc in range(16)]
        state = {"cnt": 0, "ub": 0, "db": 0}

        def ubank():
            i = state["ub"] % 4
            state["ub"] += 1
            return P.banks[i], f"bk{i}"

        def dbank():
            i = 4 + state["db"] % 4
            state["db"] += 1
            return P.banks[i], f"bk{i}"

        def up_chunk(ti, jj):
            hp = ti % 2
            j = j0 + jj
            pr = state["cnt"] % 2
            state["cnt"] += 1
            bkA, kA = ubank()
            P.mmg(bkA[:], [(Wa[:, kc, jj * 128:(jj + 1) * 128], hT[hp][:, kc, :]) for kc in range(16)], WaK + [f"hT{hp}"], [kA])
            bkB, kB = ubank()
            P.mmg(bkB[:], [(Wb[:, kc, jj * 128:(jj + 1) * 128], hT[hp][:, kc, :]) for kc in range(16)], WbK + [f"hT{hp}"], [kB])
            P.cp("pool", abuf[pr][:, 0:2], carry[:, jj, :], [f"carry{jj}"], [f"abuf{pr}"])
            P.act(abuf[pr][:, 2:T + 2], bkA[:], AF.Copy, [kA], [f"abuf{pr}"])
            P.cp("pool", carry[:, jj, :], abuf[pr][:, T:T + 2], [f"abuf{pr}"], [f"carry{jj}"])
            P.act(acc[pr][:], bkA[:], AF.Identity, [kA, "cw", "cbias"], [f"acc{pr}"], bias=cbias[:, j:j + 1], scale=cw[:, 2, j:j + 1])
            P.stt("dve", acc[pr][:], abuf[pr][:, 1:T + 1], cw[:, 1, j:j + 1], acc[pr][:], ALU.mult, ALU.add,
                  [f"abuf{pr}", "cw", f"acc{pr}"], [f"acc{pr}"])
            P.stt("dve", acc[pr][:], abuf[pr][:, 0:T], cw[:, 0, j:j + 1], acc[pr][:], ALU.mult, ALU.add,
                  [f"abuf{pr}", "cw", f"acc{pr}"], [f"acc{pr}"])

            def stage2():
                P.act(sl[pr][:], acc[pr][:], AF.Silu, [f"acc{pr}"], [f"sl{pr}"])
                P.tt("dve", actT[:, jj, :], sl[pr][:], bkB[:], ALU.mult, [f"sl{pr}", kB], [f"actT{jj}"])
            return stage2

        AK = [f"actT{jj}" for jj in range(nj)]

        def down(ti):
            t0 = ti * T
            b = ti // 4
            if ti % 4 == 0:
                P.dma(gt[:], MOD[l, b:b + 1, 5 * D:6 * D].partition_broadcast(128), "d_gt", writes=["gt"])
            for ts_ in range(4):
                xp = ts_ % 2
                r0 = t0 + ts_ * 128
                P.dma(xt[xp][:], XS[r0:r0 + 128, :], f"d_xt{xp}", writes=[f"xt{xp}"])
                for n in range(4):
                    bk, bkk = dbank()
                    P.mmg(bk[:], [(actT[:, jj, ts_ * 128:(ts_ + 1) * 128], Wd[:, jj, n * 512:(n + 1) * 512]) for jj in range(nj)],
                          AK + [f"Wd{n // 2}_{jj}" for jj in range(nj)], [bkk])
                    tp = n % 2
                    P.tt("dve", tmp[tp][:], bk[:], gt[:, n * 512:(n + 1) * 512], ALU.mult, [bkk, "gt"], [f"tmp{tp}"])
                    P.tt("pool", xt[xp][:, n * 512:(n + 1) * 512], tmp[tp][:], xt[xp][:, n * 512:(n + 1) * 512], ALU.add,
                         [f"tmp{tp}", f"xt{xp}"], [f"xt{xp}"])
                P.dma(XD[r0:r0 + 128, :], xt[xp][:], f"s_xo{xp}", reads=[f"xt{xp}"])

        pend = None
        pend_down = None
        P.dma(hT[0][:], HT[:, :, 0:T], "d_hT0", writes=["hT0"])
        for ti in range(NT):
            if ti % 4 == 0:
                P.memset("pool", carry[:], 0.0, [f"carry{jj_}" for jj_ in range(nj)])
            for jj in range(nj):
                s2 = up_chunk(ti, jj)
                if pend is not None:
                    pend()
                pend = s2
                if jj == 0:
                    if pend_down is not None:
                        pend_down()
                        pend_down = None
                    if ti + 1 < NT:
                        hp2 = (ti + 1) % 2
                        P.dma(hT[hp2][:], HT[:, :, (ti + 1) * T:(ti + 2) * T], f"d_hT{hp2}", writes=[f"hT{hp2}"])
            pend_down = (lambda ti=ti: down(ti))
        pend()
        pend_down()
        P.close()

    FG = [(0, 9), (9, 9), (18, 9), (27, 8), (35, 8)]

    def run():
        phase_mod()
        if stop_now():
            return
        xin = I["x"]
        for l in range(2):
            phase_norm(l, 0, xin)
            if stop_now():
                return
            phase_win(l)
            if stop_now():
                return
            phase_dsa(l)
            if stop_now():
                return
            phase_s5(l)
            if stop_now():
                return
            phase_pool(l)
            if stop_now():
                return
            for mg in range(4):
                phase_merge(l, mg)
            if stop_now():
                return
            phase_wout(l, xin, XA)
            if stop_now():
                return
            xfinal = OUT if l == 1 else XB
            chain = [XA, XB, XA, XB, xfinal] if False else None
            for gi, (j0, nj) in enumerate(FG):
                phase_ffn(l, j0, nj, XA if gi == 0 else XB, xfinal if gi == len(FG) - 1 else XB)
            if stop_now():
                return
            xin = XB

    if only == "norm":
        phase_norm(0, 0, I["x"])
    else:
        run()
    nc._declared_inputs = list(I.keys()) + (["MOD"] if only else [])
    return nc


def _host_layout(inputs):
    f = lambda a: np.ascontiguousarray(np.asarray(a, dtype=np.float32))
    w_in = f(inputs["w_in"])
    q, k, v, qi, ki, wi, u, p = np.split(w_in, [1024, 1152, 1280, 1792, 1856, 1864, 2376], axis=-1)
    sh = {}
    sh["w_inR"] = np.ascontiguousarray(np.concatenate([q, k, qi, ki, ki, u, p], axis=-1))
    sh["w_vw"] = np.ascontiguousarray(np.concatenate([v, wi], axis=-1))
    sh["w_ada"] = f(inputs["w_ada"])
    sh["b_ada"] = f(inputs["b_ada"])
    g1 = f(inputs["g_norm1"]).reshape(2, 16, 128).transpose(0, 2, 1)
    g2 = f(inputs["g_norm2"]).reshape(2, 16, 128).transpose(0, 2, 1)
    sh["g_norm"] = np.ascontiguousarray(np.stack([g1, g2], axis=1))
    sh["g_qk"] = np.ascontiguousarray(np.stack([f(inputs["g_q"]), f(inputs["g_k"])], axis=-1))
    a_re, a_im, ldt = f(inputs["a_re"]), f(inputs["a_im"]), f(inputs["log_dt"])
    ldt_e = np.repeat(ldt[:, :, None], 64, axis=2)
    row = np.stack([a_re.reshape(2, 2048), a_im.reshape(2, 2048), ldt_e.reshape(2, 2048)], axis=1)
    sh["a_row"] = np.ascontiguousarray(row)
    sh["a_st"] = np.ascontiguousarray(row.reshape(2, 3, 16, 128).transpose(0, 1, 3, 2))
    b_re, b_im = f(inputs["b_re"]), f(inputs["b_im"])
    c_re, c_im = f(inputs["c_re"]), f(inputs["c_im"])
    braw = np.zeros((2, 2, 128, 16, 128), np.float32)
    craw = np.zeros((2, 2, 128, 16, 128), np.float32)
    for st in range(16):
        c_, q_ = st // 4, st % 4
        for gl in range(2):
            g = 8 * c_ + 2 * q_ + gl
            gc = 2 * q_ + gl
            for ri, (bsrc, csrc) in enumerate(((b_re, c_re), (b_im, c_im))):
                braw[:, ri, gc * 16:(gc + 1) * 16, st, gl * 64:(gl + 1) * 64] = bsrc[:, g].transpose(0, 2, 1)
                craw[:, ri, gl * 64:(gl + 1) * 64, st, gc * 16:(gc + 1) * 16] = csrc[:, g].transpose(0, 2, 1)
    sh["braw"] = braw.reshape(2, 2, 128, 2048)
    sh["craw"] = craw.reshape(2, 2, 128, 2048)
    sh["dskip"] = np.ascontiguousarray(f(inputs["d_skip"]).reshape(2, 4, 128).transpose(0, 2, 1))
    sh["w_glu"] = f(inputs["w_glu"])
    sh["w_pool"] = np.ascontiguousarray(f(inputs["w_pool"]).transpose(0, 2, 1, 3))
    sh["pscale"] = np.ascontiguousarray(f(inputs["pool_scale"]).reshape(2, 4, 128).transpose(0, 2, 1))
    sh["p_a"], sh["p_b"], sh["p_c"] = f(inputs["p_a"]), f(inputs["p_b"]), f(inputs["p_c"])
    sh["w_gate"] = f(inputs["w_gate"])
    sh["b_gate"] = np.ascontiguousarray(f(inputs["b_gate"]).reshape(2, 3, 16, 128).transpose(0, 1, 3, 2))
    sh["w_out"] = f(inputs["w_out"])
    sh["w_up"] = f(inputs["w_up"])
    sh["conv_w"] = np.ascontiguousarray(f(inputs["conv_w"]).reshape(2, 3, NFC, 128).transpose(0, 3, 1, 2))
    sh["conv_b"] = np.ascontiguousarray(f(inputs["conv_b"]).reshape(2, NFC, 128).transpose(0, 2, 1))
    sh["w_down"] = f(inputs["w_down"])
    sh["ident"] = np.eye(128, dtype=np.float32).astype(ml_dtypes.bfloat16)
    rm = np.zeros((2, 128, 128), np.float32)
    for j in range(16):
        rm[0, 16 + j, j] = -1.0
        rm[0, j, 16 + j] = 1.0
    for hb in (0, 64):
        for j in range(8):
            rm[1, hb + 8 + j, hb + j] = -1.0
            rm[1, hb + j, hb + 8 + j] = 1.0
    sh["rmat"] = rm
    invf = np.zeros((128, 2), np.float32)
    fq = (500000.0 ** (-np.arange(16, dtype=np.float32) * np.float32(2.0 / 32))).astype(np.float32)
    fi = (500000.0 ** (-np.arange(8, dtype=np.float32) * np.float32(2.0 / 16))).astype(np.float32)
    invf[0:16, 0] = fq
    invf[16:32, 0] = fq
    for hb in (0, 64):
        invf[hb:hb + 8, 1] = fi
        invf[hb + 8:hb + 16, 1] = fi
    sh["invf"] = invf
    sh["iota"] = np.ascontiguousarray(np.broadcast_to(np.arange(T, dtype=np.float32), (128, T)))
    q3 = np.array([[k * 0.25 ** (it + 1) for k in (1, 2, 3)] for it in range(NQ)], np.float32).reshape(-1)
    sh["halv"] = np.ascontiguousarray(np.broadcast_to(q3, (128, 3 * NQ)))
    invc = np.zeros((128, 4, 16), np.float32)
    for g in range(4):
        w = 2 ** (g + 1)
        invc[:, g, :] = 1.0 / np.minimum(np.arange(1, 17), w)
    sh["invc"] = invc
    return sh


def _in_maps(inputs, ncores):
    sh = _host_layout(inputs)
    x = np.asarray(inputs["x"], np.float32)
    c = np.asarray(inputs["c"], np.float32)
    pos = np.asarray(inputs["positions"], np.int32)
    maps = []
    for i in range(ncores):
        m = dict(sh)
        m["x"] = np.ascontiguousarray(x[NB * i:NB * (i + 1)].reshape(NTOK, D))
        m["c"] = np.ascontiguousarray(c[NB * i:NB * (i + 1)])
        m["pos"] = np.ascontiguousarray(pos[NB * i:NB * (i + 1)])
        maps.append(m)
    return maps


def kernel(**inputs):
    nc = build()
    maps = _in_maps(inputs, NCORES)
    maps = [{k: m[k] for k in nc._declared_inputs} for m in maps]
    res = run_bass_kernel_spmd(nc, maps, core_ids=list(range(NCORES)))
    outs = [np.asarray(r["out"], np.float32).reshape(NB, SEQ, D) for r in res.results]
    return np.concatenate(outs, axis=0)
```

```python
from contextlib import ExitStack
import math
import numpy as np
import ml_dtypes
import concourse.bass as bass
import concourse.mybir as mybir
from concourse.bass_utils import run_bass_kernel_spmd

F32 = mybir.dt.float32
BF16 = mybir.dt.bfloat16
I32 = mybir.dt.int32
ALU = mybir.AluOpType
AF = mybir.ActivationFunctionType

NCORES = 8
D = 2048
SEQ = 2048
NB = 2
NTOK = NB * SEQ
T = 512
NT = NTOK // T
DFF = 5504
NFC = DFF // 128
EPS = 1e-6
PI = math.pi
TWO_PI = 2.0 * math.pi
NBIS = 18
NQ = 9
ENGS = ("pe", "act", "dve", "pool", "sp")


class Sched:
    def __init__(self, nc, name):
        self.nc = nc
        self.name = name
        self.streams = {e: [] for e in ENGS}
        self.cnt = {}
        self.res = {}
        self.seen = {e: {} for e in ENGS}

    def op(self, eng, fn, reads=(), writes=(), dma=None, ndma=1):
        deps = {}

        def add(tok):
            if tok is None:
                return
            k, v = tok
            if deps.get(k, 0) < v:
                deps[k] = v

        for r in reads:
            st = self.res.get(r)
            if st:
                add(st[0])
        for w in writes:
            st = self.res.get(w)
            if st:
                add(st[0])
                for k, v in st[1].items():
                    add((k, v))
        if dma is None:
            key = eng
            self.cnt[key] = self.cnt.get(key, 0) + 1
        else:
            key = dma
            self.cnt[key] = self.cnt.get(key, 0) + 16 * ndma
        tok = (key, self.cnt[key])
        waits = []
        seen = self.seen[eng]
        for k, v in deps.items():
            if k == eng and eng == "pe":
                continue
            if seen.get(k, 0) >= v:
                continue
            seen[k] = v
            waits.append((k, v))
        self.streams[eng].append((fn, waits, key if dma is None else None))
        for r in reads:
            st = self.res.setdefault(r, [None, {}])
            if st[1].get(key, 0) < tok[1]:
                st[1][key] = tok[1]
        for w in writes:
            self.res[w] = [tok, {}]
        return tok

    def emit(self):
        nc = self.nc
        dma_keys = [k for k in self.cnt if k not in ENGS]
        sems = {k: nc.alloc_semaphore(name=f"{self.name}_{k}") for k in self.cnt}
        with ExitStack() as st:
            block = st.enter_context(nc.Block())

            def mk(name):
                def body(eng):
                    for fn, waits, inc in self.streams[name]:
                        for k, v in waits:
                            eng.wait_ge(sems[k], v)
                        if inc is None:
                            fn(eng, sems)
                        else:
                            fn(eng).then_inc(sems[inc], 1)
                    if name == "sp":
                        for k in dma_keys:
                            eng.wait_ge(sems[k], self.cnt[k])
                return body

            block.tensor(mk("pe"))
            block.scalar(mk("act"))
            block.vector(mk("dve"))
            block.gpsimd(mk("pool"))
            block.sync(mk("sp"))
        nc.clear_and_free_semaphores(list(sems.values()))
        nc.all_engine_barrier()


class Phase:
    def __init__(self, nc, name):
        self.nc = nc
        self.name = name
        self.st = ExitStack()
        self.S = Sched(nc, name)
        self.nbank = 0
        self.banks = []
        self.rr_i = 0
        self.pref = ""
        self.cur = None
        self.tasklists = []

    def _k(self, keys):
        if not self.pref:
            return list(keys)
        return [k if k.startswith("@") else self.pref + k for k in keys]

    def _op(self, eng, fn, reads=(), writes=(), dma=None):
        reads = self._k(reads)
        writes = self._k(writes)
        if self.cur is not None:
            self.cur.append((eng, fn, reads, writes, dma))
        else:
            self.S.op(eng, fn, reads=reads, writes=writes, dma=dma)

    def begin_task(self, name):
        self.pref = name + "_"
        self.cur = []
        self.tasklists.append(self.cur)

    def end_task(self):
        self.pref = ""
        self.cur = None

    def run_tasks(self):
        lists = self.tasklists
        idx = [0] * len(lists)
        tot = [max(1, len(x)) for x in lists]
        while True:
            best = None
            for i, lst in enumerate(lists):
                if idx[i] < len(lst):
                    fr = idx[i] / tot[i]
                    if best is None or fr < best[0]:
                        best = (fr, i)
            if best is None:
                break
            i = best[1]
            eng, fn, reads, writes, dma = lists[i][idx[i]]
            idx[i] += 1
            self.S.op(eng, fn, reads=reads, writes=writes, dma=dma)
        self.tasklists = []

    def sb(self, name, shape, dt):
        return self.st.enter_context(self.nc.sbuf_tensor(f"{self.name}_{name}", shape, dt))

    def ps(self, name, shape=(128, 512), dt=F32):
        return self.st.enter_context(self.nc.psum_tensor(f"{self.name}_{name}", list(shape), dt))

    def mkbanks(self, n):
        self.banks = [self.ps(f"bk{i}") for i in range(n)]

    def bank(self):
        i = self.nbank % len(self.banks)
        self.nbank += 1
        return self.banks[i], f"bk{i}"

    def close(self):
        self.S.emit()
        self.st.close()

    def dma(self, out, in_, key, reads=(), writes=(), **kw):
        key = self.pref + key
        self._op("sp", lambda e, sems: e.dma_start(out=out, in_=in_, **kw).then_inc(sems[key], 16),
                 reads=reads, writes=writes, dma=key)

    def mmg(self, out, pairs, reads, writes):
        pairs = list(pairs)

        def fn(e):
            n = len(pairs)
            ins = None
            for i, (l, r) in enumerate(pairs):
                ins = e.matmul(out, lhsT=l, rhs=r, start=(i == 0), stop=(i == n - 1))
            return ins
        self._op("pe", fn, reads=reads, writes=writes)

    def mm(self, out, lhsT, rhs, start, stop, reads, writes):
        self._op("pe", lambda e: e.matmul(out, lhsT=lhsT, rhs=rhs, start=start, stop=stop), reads=reads, writes=writes)

    def tr(self, out, in_, ident, reads, writes):
        self._op("pe", lambda e: e.transpose(out, in_, ident), reads=reads, writes=writes)

    def act(self, out, in_, func, reads, writes, bias=None, scale=None, accum=None):
        kw = {}
        if bias is not None:
            kw["bias"] = bias
        if scale is not None:
            kw["scale"] = scale
        if accum is not None:
            kw["accum_out"] = accum
        self._op("act", lambda e: e.activation(out=out, in_=in_, func=func, **kw), reads=reads, writes=writes)

    def ts(self, eng, out, in0, s1, s2, op0, op1, reads, writes, accum=None):
        kw = {}
        if op1 is not None:
            kw["op1"] = op1
        if accum is not None:
            kw["accum_out"] = accum
        self._op(eng, lambda e: e.tensor_scalar(out=out, in0=in0, scalar1=s1, scalar2=s2, op0=op0, **kw),
                  reads=reads, writes=writes)

    def tt(self, eng, out, in0, in1, op, reads, writes):
        self._op(eng, lambda e: e.tensor_tensor(out=out, in0=in0, in1=in1, op=op), reads=reads, writes=writes)

    def stt(self, eng, out, in0, scalar, in1, op0, op1, reads, writes):
        self._op(eng, lambda e: e.scalar_tensor_tensor(out=out, in0=in0, scalar=scalar, in1=in1, op0=op0, op1=op1),
                  reads=reads, writes=writes)

    def cp(self, eng, out, in_, reads, writes):
        self._op(eng, lambda e: e.tensor_copy(out=out, in_=in_), reads=reads, writes=writes)

    def memset(self, eng, out, val, writes):
        self._op(eng, lambda e: e.memset(out, val), writes=writes)

    def recip(self, out, in_, reads, writes):
        self._op("dve", lambda e: e.reciprocal(out=out, in_=in_), reads=reads, writes=writes)

    def scan(self, out, d0, d1, init, reads, writes):
        self._op("dve", lambda e: e.tensor_tensor_scan(out=out, data0=d0, data1=d1, initial=init,
                                                        op0=ALU.mult, op1=ALU.add), reads=reads, writes=writes)

    def range_reduce(self, eng, x, tmpf, tmpi, key, tkeys):
        self.ts(eng, tmpf, x, 1.0 / TWO_PI, None, ALU.mult, None, [key], [tkeys[0]])
        self.cp(eng, tmpi, tmpf, [tkeys[0]], [tkeys[1]])
        self.cp(eng, tmpf, tmpi, [tkeys[1]], [tkeys[0]])
        self.stt(eng, x, tmpf, -TWO_PI, x, ALU.mult, ALU.add, [tkeys[0], key], [key])
        self.ts(eng, tmpf, x, PI, -TWO_PI, ALU.is_gt, ALU.mult, [key], [tkeys[0]])
        self.tt(eng, x, x, tmpf, ALU.add, [key, tkeys[0]], [key])
        self.ts(eng, tmpf, x, -PI, TWO_PI, ALU.is_lt, ALU.mult, [key], [tkeys[0]])
        self.tt(eng, x, x, tmpf, ALU.add, [key, tkeys[0]], [key])

    def load_w(self, dst, src, kcs, ncols, key, stg, stgkey, i0=0):
        cap = stg[0].shape[1]
        g = max(1, min(kcs, cap // ncols))
        ns = len(stg)
        engs = ("pool", "act", "dve")
        for gi, kc0 in enumerate(range(0, kcs, g)):
            gg = min(g, kcs - kc0)
            cnt = self.rr_i
            self.rr_i += 1
            par = cnt % ns
            sview = stg[par][:, 0:gg * ncols].rearrange("p (k n) -> p k n", k=gg)
            self.dma(sview, src[kc0 * 128:(kc0 + gg) * 128, :].rearrange("(k p) n -> p k n", p=128),
                     f"d_{stgkey}{par}", writes=[f"{stgkey}{par}"])
            eng = engs[cnt % 3]
            wk = [f"{key}_{kc}" for kc in range(kc0, kc0 + gg)]
            if eng == "act":
                self.act(dst[:, kc0:kc0 + gg, :], sview, AF.Copy, [f"{stgkey}{par}"], wk)
            else:
                self.cp(eng, dst[:, kc0:kc0 + gg, :], sview, [f"{stgkey}{par}"], wk)


def build(upto=None, debug=False, only=None):
    nc = bass.Bass("TRN2", target_bir_lowering=False)
    kinds = {}

    def din(name, shape, dt=F32):
        return nc.dram_tensor(name, list(shape), dt, kind="ExternalInput").ap()

    def dscr(name, shape, dt):
        kind = "ExternalOutput" if debug else "Internal"
        return nc.dram_tensor(name, list(shape), dt, kind=kind).ap()

    SPECS = {
        "x": ([NTOK, D], F32),
        "c": ([NB, D], F32),
        "pos": ([NB, SEQ], I32),
        "w_ada": ([2, D, 6 * D], F32),
        "b_ada": ([2, 6 * D], F32),
        "g_norm": ([2, 2, 128, 16], F32),
        "w_inR": ([2, D, 2816], F32),
        "w_vw": ([2, D, 136], F32),
        "g_qk": ([2, 128, 2], F32),
        "a_row": ([2, 3, 2048], F32),
        "a_st": ([2, 3, 128, 16], F32),
        "braw": ([2, 2, 128, 2048], F32),
        "craw": ([2, 2, 128, 2048], F32),
        "dskip": ([2, 128, 4], F32),
        "w_glu": ([2, 512, 512], F32),
        "w_pool": ([2, 128, 4, 128], F32),
        "pscale": ([2, 128, 4], F32),
        "p_a": ([2, 1024, D], F32),
        "p_b": ([2, 512, D], F32),
        "p_c": ([2, 512, D], F32),
        "w_gate": ([2, 3, D, D], F32),
        "b_gate": ([2, 3, 128, 16], F32),
        "w_out": ([2, D, D], F32),
        "w_up": ([2, D, 2 * DFF], F32),
        "conv_w": ([2, 128, 3, NFC], F32),
        "conv_b": ([2, 128, NFC], F32),
        "w_down": ([2, DFF, D], F32),
        "ident": ([128, 128], BF16),
        "rmat": ([2, 128, 128], F32),
        "invf": ([128, 2], F32),
        "iota": ([128, T], F32),
        "halv": ([128, 3 * NQ], F32),
        "invc": ([128, 4, 16], F32),
    }

    class _Lazy(dict):
        def __missing__(self, key):
            shape, dt = SPECS[key]
            v = din(key, shape, dt)
            self[key] = v
            return v
    I = _Lazy()
    OUT = nc.dram_tensor("out", [NTOK, D], F32, kind="ExternalOutput").ap()

    MOD = din("MOD", [2, NB, 6 * D]) if only else dscr("MOD", [2, NB, 6 * D], F32)
    HT = dscr("HT", [128, 16, NTOK], BF16)
    QT = dscr("QT", [128, 8, NTOK], BF16)
    KT = dscr("KT", [128, NTOK], BF16)
    QIT = dscr("QIT", [128, 4, NTOK], BF16)
    KIT = dscr("KIT", [128, NTOK], BF16)
    VV = dscr("VV", [NTOK, 128], BF16)
    WI = dscr("WI", [NTOK, 8], F32)
    UT = dscr("UT", [128, 4, NTOK], BF16)
    PT = dscr("PT", [128, 4, NTOK], BF16)
    OAT = dscr("OAT", [128, 8, NTOK], BF16)
    OBT = dscr("OBT", [128, 4, NTOK], BF16)
    OCT = dscr("OCT", [128, 4, NTOK], BF16)
    MT = dscr("MT", [128, 16, NTOK], BF16)
    XA = dscr("XA", [NTOK, D], F32)
    XB = dscr("XB", [NTOK, D], F32)

    phases_done = [0]

    def stop_now():
        phases_done[0] += 1
        return upto is not None and phases_done[0] > upto

    def phase_mod():
        P = Phase(nc, "p0")
        cT = P.sb("cT", [128, NB, 16], F32)
        cA = P.sb("cA", [128, NB, 16], F32)
        bada = P.sb("bada", [NB, 6 * D], F32)
        wst = [P.sb(f"wst{i}", [128, 16, 512], F32) for i in range(4)]
        mrow = [P.sb(f"mrow{i}", [NB, 512], F32) for i in range(2)]
        P.mkbanks(2)
        for b in range(NB):
            P.dma(cT[:, b, :], I["c"][b, :].rearrange("(kc p) -> p kc", p=128), "d_c", writes=["cT"], allow_slow_non_contiguous=True)
        P.act(cA[:], cT[:], AF.Silu, ["cT"], ["cA"])
        for l in range(2):
            P.dma(bada[:], I["b_ada"][l:l + 1, :].partition_broadcast(NB), "d_bada", writes=["bada"])
            for cg in range(24):
                par = cg % 4
                P.dma(wst[par][:], I["w_ada"][l, :, cg * 512:(cg + 1) * 512].rearrange("(kc p) n -> p kc n", p=128),
                      f"d_wst{par}", writes=[f"wst{par}"])
                bk, bkk = P.bank()
                P.mmg(bk[0:NB, :], [(cA[:, :, kc], wst[par][:, kc, :]) for kc in range(16)], ["cA", f"wst{par}"], [bkk])
                mp = cg % 2
                P.tt("dve", mrow[mp][:], bk[0:NB, :], bada[:, cg * 512:(cg + 1) * 512], ALU.add, [bkk, "bada"], [f"mrow{mp}"])
                P.dma(MOD[l, :, cg * 512:(cg + 1) * 512], mrow[mp][:], f"s_mrow{mp}", reads=[f"mrow{mp}"])
        P.close()

    def phase_norm(l, which, XS):
        P = Phase(nc, f"n{l}{which}")
        epsb = P.sb("epsb", [128, 1], F32)
        ident = P.sb("ident", [128, 128], BF16)
        sc = P.sb("sc", [128, 16], F32)
        sh = [P.sb(f"sh{i}", [128, 16], F32) for i in range(NB)]
        g = P.sb("g", [128, 16], F32)
        A = [P.sb(f"A{i}", [128, 16], F32) for i in range(NB)]
        P.memset("dve", epsb[:], EPS, ["@epsb"])
        P.dma(ident[:], I["ident"], "d_id", writes=["@ident"])
        P.dma(g[:], I["g_norm"][l, which], "d_g", writes=["g"])
        off = 3 * which * D
        for b in range(NB):
            P.dma(sh[b][:], MOD[l, b, off:off + D].rearrange("(kc p) -> p kc", p=128), f"d_sh{b}", writes=[f"@sh{b}"],
                  allow_slow_non_contiguous=True)
            P.dma(sc[:], MOD[l, b, off + D:off + 2 * D].rearrange("(kc p) -> p kc", p=128), "d_sc", writes=["sc"],
                  allow_slow_non_contiguous=True)
            P.stt("dve", A[b][:], sc[:], 1.0, g[:], ALU.add, ALU.mult, ["sc", "g"], [f"@A{b}"])
        for b in range(NB):
            sfx = f"_{b}"
            xt = [P.sb(f"xt{i}" + sfx, [128, D], F32) for i in range(2)]
            junk = P.sb("junk" + sfx, [128, D], BF16)
            xn = [P.sb(f"xn{i}" + sfx, [128, D], BF16) for i in range(2)]
            hst = [P.sb(f"hst{i}" + sfx, [128, 16, 128], BF16) for i in range(2)]
            ssq = [P.sb(f"ssq{i}" + sfx, [128, 1], F32) for i in range(2)]
            rt = [P.sb(f"rt{i}" + sfx, [128, 1], F32) for i in range(2)]
            rstd = [P.sb(f"rstd{i}" + sfx, [128, 1], F32) for i in range(2)]
            pbk = [P.ps(f"pb{i}" + sfx, (128, 1024), BF16) for i in range(4)]
            P.begin_task(f"b{b}")
            for R in range(b * 16, (b + 1) * 16):
                par = R % 2
                P.dma(xt[par][:], XS[R * 128:(R + 1) * 128, :], f"d_xt{par}", writes=[f"xt{par}"])
                P.act(junk[:], xt[par][:], AF.Square, [f"xt{par}"], ["junk", f"ssq{par}"], accum=ssq[par][:])
                P.act(rt[par][:], ssq[par][:], AF.Sqrt, [f"ssq{par}", "@epsb"], [f"rt{par}"], bias=epsb[:], scale=1.0 / D)
                P.recip(rstd[par][:], rt[par][:], [f"rt{par}"], [f"rstd{par}"])
                P.ts("dve", xn[par][:], xt[par][:], rstd[par][:, 0:1], None, ALU.mult, None, [f"xt{par}", f"rstd{par}"], [f"xn{par}"])
                for q4 in range(4):
                    for j in range(4):
                        kc = q4 * 4 + j
                        P.tr(pbk[q4][:, j * 128:(j + 1) * 128], xn[par][:, kc * 128:(kc + 1) * 128], ident[:],
                             [f"xn{par}", "@ident"], [f"pb{q4}"])
                    for j in range(4):
                        kc = q4 * 4 + j
                        P.act(hst[par][:, kc, :], pbk[q4][:, j * 128:(j + 1) * 128], AF.Identity,
                              [f"pb{q4}", f"@A{b}", f"@sh{b}"], [f"hst{par}_{kc}"], bias=sh[b][:, kc:kc + 1], scale=A[b][:, kc:kc + 1])
                P.dma(HT[:, :, R * 128:(R + 1) * 128], hst[par][:], f"s_hst{par}",
                      reads=[f"hst{par}_{kc}" for kc in range(16)])
            P.end_task()
        P.run_tasks()
        P.close()

    def phase_win(l):
        P = Phase(nc, f"w{l}")
        Wq = P.sb("Wq", [128, 16, 2816], BF16)
        Wv = P.sb("Wv", [128, 16, 136], BF16)
        stg = [P.sb(f"stg{i}", [128, 2816], F32) for i in range(3)]
        hT = [P.sb(f"hT{i}", [128, 16, T], BF16) for i in range(2)]
        ones = P.sb("ones", [128, 128], BF16)
        rmat = P.sb("rmat", [128, 2, 128], F32)
        invf = P.sb("invf", [128, 2], F32)
        gqk = P.sb("gqk", [128, 2], F32)
        epsb = P.sb("epsb", [128, 1], F32)
        posi = [P.sb(f"posi{i}", [128, T], I32) for i in range(2)]
        posf = [P.sb(f"posf{i}", [128, T], F32) for i in range(2)]
        tabs = [[P.sb(f"tab{j}_{i}", [128, T], F32) for i in range(4)] for j in range(2)]
        tmpf = P.sb("tmpf", [128, T], F32)
        tmpi = P.sb("tmpi", [128, T], I32)
        sqb = [P.sb(f"sqb{i}", [128, T], BF16) for i in range(2)]
        rtt = [P.sb(f"rtt{i}", [128, T], F32) for i in range(2)]
        qn = [P.sb(f"qn{i}", [128, T], F32) for i in range(2)]
        t1 = [P.sb(f"t1{i}", [128, T], F32) for i in range(2)]
        t2 = [P.sb(f"t2{i}", [128, T], F32) for i in range(2)]
        ost = [P.sb(f"ost{i}", [128, T], BF16) for i in range(2)]
        vst = [P.sb(f"vst{i}", [128, 128], BF16) for i in range(2)]
        wst = [P.sb(f"wst{i}", [128, 8], F32) for i in range(2)]
        P.mkbanks(8)
        P.memset("dve", epsb[:], EPS, ["epsb"])
        P.memset("dve", ones[:], 1.0 / 128.0, ["ones"])
        P.dma(rmat[:], I["rmat"].rearrange("r k m -> k r m"), "d_rm", writes=["rmat"])
        P.dma(invf[:], I["invf"], "d_if", writes=["invf"])
        P.dma(gqk[:], I["g_qk"][l], "d_gqk", writes=["gqk"])
        P.load_w(Wv, I["w_vw"][l], 16, 136, "Wv", stg, "stg")
        P.load_w(Wq, I["w_inR"][l], 16, 2816, "Wq", stg, "stg")
        WqK = [f"Wq_{kc}" for kc in range(16)]
        WvK = [f"Wv_{kc}" for kc in range(16)]
        def table_ops(ti):
            b_ = ti // 4
            t0_ = ti * T
            tp = ti % 2
            pk = f"tb{tp}_"
            ops = []
            ops.append(lambda: P.dma(posi[tp][:], I["pos"][b_:b_ + 1, t0_ - b_ * SEQ:t0_ - b_ * SEQ + T].partition_broadcast(128),
                                     f"d_pos{tp}", writes=[pk + "posi"]))
            ops.append(lambda: P.cp("dve", posf[tp][:], posi[tp][:], [pk + "posi"], [pk + "posf"]))
            for k in range(4):
                which = k // 2
                shift = (PI / 2.0) if (k % 2 == 0) else 0.0
                x = tabs[tp][k][:]
                key = pk + f"tab{k}"
                ops.append(lambda x=x, key=key, which=which, shift=shift: P.ts(
                    "dve", x, posf[tp][:], invf[:, which:which + 1], shift, ALU.mult, ALU.add, [pk + "posf", "invf"], [key]))
                tk = ["tmpf", "tmpi"]
                ops.append(lambda x=x, key=key: P.ts("dve", tmpf[:], x, 1.0 / TWO_PI, None, ALU.mult, None, [key], [tk[0]]))
                ops.append(lambda: P.cp("dve", tmpi[:], tmpf[:], [tk[0]], [tk[1]]))
                ops.append(lambda: P.cp("dve", tmpf[:], tmpi[:], [tk[1]], [tk[0]]))
                ops.append(lambda x=x, key=key: P.stt("dve", x, tmpf[:], -TWO_PI, x, ALU.mult, ALU.add, [tk[0], key], [key]))
                ops.append(lambda x=x, key=key: P.ts("dve", tmpf[:], x, PI, -TWO_PI, ALU.is_gt, ALU.mult, [key], [tk[0]]))
                ops.append(lambda x=x, key=key: P.tt("dve", x, x, tmpf[:], ALU.add, [key, tk[0]], [key]))
                ops.append(lambda x=x, key=key: P.ts("dve", tmpf[:], x, -PI, TWO_PI, ALU.is_lt, ALU.mult, [key], [tk[0]]))
                ops.append(lambda x=x, key=key: P.tt("dve", x, x, tmpf[:], ALU.add, [key, tk[0]], [key]))
                ops.append(lambda x=x, key=key: P.act(x, x, AF.Sin, [key], [key]))
            return ops

        cnt = 0
        for ti in range(NT):
            b = ti // 4
            t0 = ti * T
            hp = ti % 2
            P.dma(hT[hp][:], HT[:, :, t0:t0 + T], f"d_hT{hp}", writes=[f"hT{hp}"])
            if ti == 0:
                for th_ in table_ops(0):
                    th_()
            nxt = table_ops(ti + 1) if ti + 1 < NT else []
            tab = tabs[ti % 2]
            tpk = f"tb{ti % 2}_"
            for m in range(22):
                bk, bkk = P.bank()
                P.mmg(bk[:], [(Wq[:, kc, m * 128:(m + 1) * 128], hT[hp][:, kc, :]) for kc in range(16)],
                      WqK + [f"hT{hp}"], [bkk])
                pr = cnt % 2
                cnt += 1
                if m <= 13:
                    isqk = m <= 8
                    if isqk:
                        gcol = 0 if m < 8 else 1
                        P.act(sqb[pr][:], bk[:], AF.Square, [bkk], [f"sqb{pr}"])
                        bk2, bkk2 = P.bank()
                        P.mm(bk2[:], ones[:], sqb[pr][:], True, True, ["ones", f"sqb{pr}"], [bkk2])
                        P.act(rtt[pr][:], bk2[:], AF.Sqrt, [bkk2, "epsb"], [f"rtt{pr}"], bias=epsb[:], scale=1.0)
                        P.recip(rtt[pr][:], rtt[pr][:], [f"rtt{pr}"], [f"rtt{pr}"])
                        P.stt("dve", qn[pr][:], bk[:], gqk[:, gcol:gcol + 1], rtt[pr][:], ALU.mult, ALU.mult,
                              [bkk, "gqk", f"rtt{pr}"], [f"qn{pr}"])
                        ri, Ct, St, Ck, Sk = 0, tab[0], tab[1], tpk + "tab0", tpk + "tab1"
                    else:
                        P.act(qn[pr][:], bk[:], AF.Copy, [bkk], [f"qn{pr}"])
                        ri, Ct, St, Ck, Sk = 1, tab[2], tab[3], tpk + "tab2", tpk + "tab3"
                    bk3, bkk3 = P.bank()
                    P.mm(bk3[:], rmat[:, ri, :], qn[pr][:], True, True, ["rmat", f"qn{pr}"], [bkk3])
                    P.tt("pool", t1[pr][:], qn[pr][:], Ct[:], ALU.mult, [f"qn{pr}", Ck], [f"t1{pr}"])
                    P.tt("dve", t2[pr][:], bk3[:], St[:], ALU.mult, [bkk3, Sk], [f"t2{pr}"])
                    P.tt("dve", ost[pr][:], t1[pr][:], t2[pr][:], ALU.add, [f"t1{pr}", f"t2{pr}"], [f"ost{pr}"])
                    if m < 8:
                        dst = QT[:, m, t0:t0 + T]
                    elif m == 8:
                        dst = KT[:, t0:t0 + T]
                    elif m < 13:
                        dst = QIT[:, m - 9, t0:t0 + T]
                    else:
                        dst = KIT[:, t0:t0 + T]
                else:
                    P.act(ost[pr][:], bk[:], AF.Copy, [bkk], [f"ost{pr}"])
                    dst = UT[:, m - 14, t0:t0 + T] if m < 18 else PT[:, m - 18, t0:t0 + T]
                P.dma(dst, ost[pr][:], f"s_ost{pr}", reads=[f"ost{pr}"])
                for _ in range(2):
                    if nxt:
                        nxt.pop(0)()
            while nxt:
                nxt.pop(0)()
            for ts_ in range(4):
                bk, bkk = P.bank()
                pr = ts_ % 2
                P.mmg(bk[:, 0:136], [(hT[hp][:, kc, ts_ * 128:(ts_ + 1) * 128], Wv[:, kc, :]) for kc in range(16)],
                      WvK + [f"hT{hp}"], [bkk])
                P.act(vst[pr][:], bk[:, 0:128], AF.Copy, [bkk], [f"vst{pr}"])
                P.act(wst[pr][:], bk[:, 128:136], AF.Copy, [bkk], [f"wst{pr}"], scale=8.0 ** -0.5)
                r0 = t0 + ts_ * 128
                P.dma(VV[r0:r0 + 128, :], vst[pr][:], f"s_vst{pr}", reads=[f"vst{pr}"])
                P.dma(WI[r0:r0 + 128, :], wst[pr][:], f"s_wst{pr}", reads=[f"wst{pr}"])
        P.close()

    def phase_dsa(l):
        P = Phase(nc, f"a{l}")
        ident = P.sb("ident", [128, 128], BF16)
        ones = P.sb("ones", [128, 128], BF16)
        halv = P.sb("halv", [128, 3 * NQ], F32)
        P.dma(ident[:], I["ident"], "d_id", writes=["@ident"])
        P.dma(halv[:], I["halv"], "d_hv", writes=["@halv"])
        P.memset("dve", ones[:], 1.0, ["@ones"])
        AXX = mybir.AxisListType.X
        for b in range(NB):
            sfx = f"_{b}"
            Kc = P.sb("Kc" + sfx, [128, SEQ], BF16)
            Kic = P.sb("Kic" + sfx, [128, SEQ], BF16)
            Vc = P.sb("Vc" + sfx, [128, 16, 128], BF16)
            qT = [P.sb(f"qT{i}" + sfx, [128, 8, 128], BF16) for i in range(2)]
            qiT = [P.sb(f"qiT{i}" + sfx, [128, 4, 128], BF16) for i in range(2)]
            wi = [P.sb(f"wi{i}" + sfx, [128, 8], F32) for i in range(2)]
            score = P.sb("score" + sfx, [128, SEQ], F32)
            rl = [P.sb(f"rl{i}" + sfx, [128, 512], F32) for i in range(3)]
            junk = P.sb("junk" + sfx, [128, SEQ], F32)
            mask = P.sb("mask" + sfx, [128, SEQ], BF16)
            maskT = P.sb("maskT" + sfx, [128, 16, 128], BF16)
            pT = [P.sb(f"pT{i}" + sfx, [128, 512], BF16) for i in range(2)]
            wk = P.sb("wk" + sfx, [128, 3 * NQ], F32)
            th3 = P.sb("th3" + sfx, [128, 3], F32)
            cs = P.sb("cs" + sfx, [128, 2], F32)
            g2j = P.sb("g2j" + sfx, [128, 2], F32)
            gs = P.sb("gs" + sfx, [128, 1], F32)
            gs2 = P.sb("gs2" + sfx, [128, 1], F32)
            junkA = P.sb("junkA" + sfx, [128, SEQ], BF16)
            junkB = P.sb("junkB" + sfx, [128, SEQ], BF16)
            lo = P.sb("lo" + sfx, [128, 1], F32)
            hi = P.sb("hi" + sfx, [128, 1], F32)
            mid = P.sb("mid" + sfx, [128, 1], F32)
            cntt = P.sb("cntt" + sfx, [128, 1], F32)
            stp = P.sb("stp" + sfx, [128, 1], F32)
            rinv = P.sb("rinv" + sfx, [128, 512], F32)
            ost = [P.sb(f"ost{i}" + sfx, [128, 8, 128], BF16) for i in range(2)]
            psS = [P.ps("psS0" + sfx), P.ps("psS1" + sfx)]
            psO = P.ps("psO" + sfx)
            psI = P.ps("psI" + sfx)
            psR = psI
            psT = psI.bitcast(BF16)
            ibanks = [(psI, "psI"), (psS[0], "psS0"), (psS[1], "psS1")]
            P.begin_task(f"b{b}")
            s0 = b * SEQ
            P.dma(Kc[:], KT[:, s0:s0 + SEQ], "d_Kc", writes=["Kc"])
            P.dma(Kic[:], KIT[:, s0:s0 + SEQ], "d_Kic", writes=["Kic"])
            P.dma(Vc[:], VV[s0:s0 + SEQ, :].rearrange("(kt p) d -> p kt d", p=128), "d_Vc", writes=["Vc"])
            for qb in range(16):
                par = qb % 2
                r0 = s0 + qb * 128
                N = 128 * (qb + 1)
                nkt = qb + 1
                P.dma(qT[par][:], QT[:, :, r0:r0 + 128], f"d_qT{par}", writes=[f"qT{par}"])
                P.dma(qiT[par][:], QIT[:, :, r0:r0 + 128], f"d_qiT{par}", writes=[f"qiT{par}"])
                P.dma(wi[par][:], WI[r0:r0 + 128, :], f"d_wi{par}", writes=[f"wi{par}"])
                nkb = (N + 511) // 512
                ci = 0
                for kb in range(nkb):
                    c0 = kb * 512
                    cw_ = min(512, N - c0)
                    for h in range(8):
                        po = 64 * (h % 2)
                        ibk, ibkk = ibanks[ci % 3]
                        P.mm(ibk[:, 0:cw_], qiT[par][po:po + 64, h // 2, :], Kic[po:po + 64, c0:c0 + cw_], True, True,
                             [f"qiT{par}", "Kic"], [ibkk])
                        rp = ci % 3
                        ci += 1
                        P.act(rl[rp][:, 0:cw_], ibk[:, 0:cw_], AF.Relu, [ibkk], [f"rl{rp}"], scale=0.125)
                        if h == 0:
                            P.ts("dve", score[:, c0:c0 + cw_], rl[rp][:, 0:cw_], wi[par][:, 0:1], None, ALU.mult, None,
                                 [f"rl{rp}", f"wi{par}"], ["score"])
                        else:
                            P.stt("dve", score[:, c0:c0 + cw_], rl[rp][:, 0:cw_], wi[par][:, h:h + 1], score[:, c0:c0 + cw_],
                                  ALU.mult, ALU.add, [f"rl{rp}", f"wi{par}", "score"], ["score"])
                if qb >= 2:
                    P._op("dve", lambda e, N=N, lo=lo, score=score: e.tensor_reduce(out=lo[:], in_=score[:, 0:N - 64], axis=AXX, op=ALU.min),
                          reads=["score"], writes=["lo"])
                    P._op("dve", lambda e, N=N, hi=hi, score=score: e.tensor_reduce(out=hi[:], in_=score[:, 0:N - 64], axis=AXX, op=ALU.max),
                          reads=["score"], writes=["hi"])
                    P._op("dve", lambda e, N=N, mid=mid, score=score: e.tensor_reduce(out=mid[64:128, :], in_=score[64:128, N - 64:N], axis=AXX, op=ALU.min),
                          reads=["score"], writes=["mid"])
                    P.tt("dve", lo[64:128, :], lo[64:128, :], mid[64:128, :], ALU.min, ["lo", "mid"], ["lo"])
                    P._op("dve", lambda e, N=N, mid=mid, score=score: e.tensor_reduce(out=mid[64:128, :], in_=score[64:128, N - 64:N], axis=AXX, op=ALU.max),
                          reads=["score"], writes=["mid"])
                    P.tt("dve", hi[64:128, :], hi[64:128, :], mid[64:128, :], ALU.max, ["hi", "mid"], ["hi"])
                P.memset("dve", score[0:64, N - 64:N], -1e30, ["score"])
                if qb >= 2:
                    P.tt("dve", hi[:], hi[:], lo[:], ALU.subtract, ["hi", "lo"], ["hi"])
                    P.ts("dve", hi[:], hi[:], 1.001, 1e-6, ALU.mult, ALU.add, ["hi"], ["hi"])
                    P.stt("dve", lo[:], hi[:], -0.0005, lo[:], ALU.mult, ALU.add, ["hi", "lo"], ["lo"])
                    P.ts("dve", wk[:], halv[:], hi[:, 0:1], None, ALU.mult, None, ["@halv", "hi"], ["wk"])
                    for it in range(NQ):
                        P.ts("dve", th3[:], wk[:, 3 * it:3 * it + 3], lo[:, 0:1], None, ALU.add, None, ["wk", "lo"], ["th3"])
                        P.act(junkA[:, 0:N], score[:, 0:N], AF.Sign, ["score", "th3"], ["junkA", "cs0"], bias=th3[:, 0:1], scale=-1.0,
                              accum=cs[:, 0:1])
                        P.act(junkB[:, 0:N], score[:, 0:N], AF.Sign, ["score", "th3"], ["junkB", "cs1"], bias=th3[:, 1:2], scale=-1.0,
                              accum=cs[:, 1:2])
                        P.ts("dve", junk[:, 0:N], score[:, 0:N], th3[:, 2:3], 0.0, ALU.is_gt, ALU.add, ["score", "th3"],
                             ["junk", "cntt"], accum=cntt[:])
                        P.ts("dve", g2j[:], cs[:], float(N - 511), 0.0, ALU.is_le, ALU.add, ["cs0", "cs1"], ["g2j", "gs"], accum=gs[:])
                        P.stt("dve", gs2[:], cntt[:], 255.5, gs[:], ALU.is_ge, ALU.add, ["cntt", "gs"], ["gs2"])
                        P.stt("dve", lo[:], gs2[:], wk[:, 3 * it:3 * it + 1], lo[:], ALU.mult, ALU.add, ["gs2", "wk", "lo"], ["lo"])
                else:
                    P.memset("dve", lo[:], -1e29, ["lo"])
                P.ts("dve", mask[:, 0:N], score[:, 0:N], lo[:, 0:1], None, ALU.is_gt, None, ["score", "lo"], ["mask"])
                for kt in range(nkt):
                    j = kt % 4
                    P.tr(psT[:, j * 128:(j + 1) * 128], mask[:, kt * 128:(kt + 1) * 128], ident[:], ["mask", "@ident"], ["psI"])
                    P.act(maskT[:, kt, :], psT[:, j * 128:(j + 1) * 128], AF.Copy, ["psI"], [f"maskT{kt}"])
                for half in range(2):
                    for kt in range(nkt):
                        pp = kt % 2
                        P.mm(psS[pp][:], Kc[:, kt * 128:(kt + 1) * 128],
                             qT[par][:, 4 * half:4 * half + 4, :].rearrange("p a b -> p (a b)"), True, True,
                             ["Kc", f"qT{par}"], [f"psS{pp}"])
                        P.act(pT[pp][:], psS[pp][:], AF.Exp, [f"psS{pp}"], [f"pT{pp}"], scale=128.0 ** -0.5)
                        P.tt("pool" if kt % 2 else "dve", pT[pp][:].rearrange("p (a b) -> p a b", a=4),
                             pT[pp][:].rearrange("p (a b) -> p a b", a=4),
                             maskT[:, kt, :].unsqueeze(1).to_broadcast([128, 4, 128]), ALU.mult,
                             [f"pT{pp}", f"maskT{kt}"], [f"pT{pp}"])
                        P.mm(psO[:], Vc[:, kt, :], pT[pp][:], kt == 0, kt == nkt - 1, ["Vc", f"pT{pp}"], ["psO"])
                        P.mm(psR[:], ones[:], pT[pp][:], kt == 0, kt == nkt - 1, ["@ones", f"pT{pp}"], ["psI"])
                    P.recip(rinv[:], psR[:], ["psI"], ["rinv"])
                    P.tt("dve", ost[par][:, 4 * half:4 * half + 4, :].rearrange("p a b -> p (a b)"), psO[:], rinv[:],
                         ALU.mult, ["psO", "rinv"], [f"ost{par}_{half}"])
                P.dma(OAT[:, :, r0:r0 + 128], ost[par][:], f"s_ost{par}", reads=[f"ost{par}_0", f"ost{par}_1"])
            P.end_task()
        P.run_tasks()
        P.close()

    def phase_s5(l):
        P = Phase(nc, f"s{l}")
        arow = P.sb("arow", [128, 3, 2048], F32)
        w0 = [P.sb(f"w0{i}", [128, 2048], F32) for i in range(8)]
        wi32 = P.sb("wi32", [128, 2048], I32)
        ast = P.sb("ast", [128, 3, 16], F32)
        sm = [P.sb(f"sm{i}", [128, 16], F32) for i in range(8)]
        smi = P.sb("smi", [128, 16], I32)
        rho = P.sb("rho", [128, 16], F32)
        ETc = P.sb("ETc", [128, 16], F32)
        ETs = P.sb("ETs", [128, 16], F32)
        Ec = P.sb("Ec", [128, 16, T], BF16)
        Es = P.sb("Es", [128, 16, T], BF16)
        tmpi = P.sb("tmpi", [128, T], I32)
        fence = P.sb("fence", [128, 1], F32)
        Bre = P.sb("Bre", [128, 16, 128], BF16)
        Bim = P.sb("Bim", [128, 16, 128], BF16)
        Cre = P.sb("Cre", [128, 16, 128], BF16)
        Cim = P.sb("Cim", [128, 16, 128], BF16)
        Wg = P.sb("Wg", [128, 4, 512], BF16)
        stg = [P.sb(f"stg{i}", [128, 2048], F32) for i in range(1)]
        dsk = P.sb("dsk", [128, 4], F32)
        uT = [P.sb(f"uT{i}", [128, 4, T], BF16) for i in range(2)]
        sre = [P.sb(f"sre{i}", [128, T], BF16) for i in range(2)]
        sim = [P.sb(f"sim{i}", [128, T], BF16) for i in range(2)]
        car = P.sb("car", [128, 2, 16], F32)
        zl = P.sb("zl", [128, 2, 16], F32)
        ct = [P.sb(f"ct{i}", [128, 16], F32) for i in range(4)]
        ygf = P.sb("ygf", [128, 4, T], F32)
        ygb = P.sb("ygb", [128, 4, T], BF16)
        ost = [P.sb(f"ost{i}", [128, T], BF16) for i in range(2)]
        psA = [P.ps(f"psA{i}") for i in range(2)]
        psB = [P.ps(f"psB{i}") for i in range(2)]
        psY = [P.ps(f"psY{i}") for i in range(2)]
        psG = [P.ps(f"psG{i}") for i in range(2)]

        for a_ in range(3):
            P.dma(arow[:, a_, :], I["a_row"][l, a_:a_ + 1, :].partition_broadcast(128), "d_arow", writes=["arow"])
        P.dma(ast[:], I["a_st"][l].rearrange("a p t -> p a t"), "d_ast", writes=["ast"])
        P.dma(dsk[:], I["dskip"][l], "d_dsk", writes=["@dsk"])

        def lam_setup(eng, are, aim, ldt, W, Wi, pre, srck):
            dt_, ar, th, mag, cs, sn, tf, t2_ = W
            k = [f"{pre}{i}" for i in range(8)]
            P.act(dt_, ldt, AF.Exp, [srck], [k[0]])
            P.tt(eng, ar, are, dt_, ALU.mult, [srck, k[0]], [k[1]])
            P.tt(eng, th, aim, dt_, ALU.mult, [srck, k[0]], [k[2]])
            P.act(mag, ar, AF.Exp, [k[1]], [k[3]])
            P.ts(eng, sn, th, TWO_PI, None, ALU.add, None, [k[2]], [k[5]])
            P.ts(eng, cs, th, TWO_PI + PI / 2.0, None, ALU.add, None, [k[2]], [k[4]])
            P.range_reduce(eng, sn, tf, Wi, k[5], [k[6], pre + "i"])
            P.range_reduce(eng, cs, tf, Wi, k[4], [k[6], pre + "i"])
            return k

        W = [w[:] for w in w0]
        k = lam_setup("dve", arow[:, 0, :], arow[:, 1, :], arow[:, 2, :], W, wi32[:], "rw", "arow")
        dt_, ar, th, mag, cs, sn, tf, t2_ = W
        P.act(sn, sn, AF.Sin, [k[5]], [k[5]])
        P.act(cs, cs, AF.Sin, [k[4]], [k[4]])
        P.tt("dve", cs, cs, mag, ALU.mult, [k[4], k[3]], [k[4]])
        P.ts("dve", cs, cs, -1.0, None, ALU.add, None, [k[4]], [k[4]])
        P.tt("dve", sn, sn, mag, ALU.mult, [k[5], k[3]], [k[5]])
        P.tt("dve", tf, arow[:, 0, :], arow[:, 0, :], ALU.mult, ["arow"], [k[6]])
        P.tt("dve", t2_, arow[:, 1, :], arow[:, 1, :], ALU.mult, ["arow"], [k[7]])
        P.tt("dve", tf, tf, t2_, ALU.add, [k[6], k[7]], [k[6]])
        P.recip(tf, tf, [k[6]], [k[6]])
        P.tt("dve", dt_, cs, arow[:, 0, :], ALU.mult, [k[4], "arow"], [k[0]])
        P.tt("dve", t2_, sn, arow[:, 1, :], ALU.mult, [k[5], "arow"], [k[7]])
        P.tt("dve", dt_, dt_, t2_, ALU.add, [k[0], k[7]], [k[0]])
        P.tt("dve", dt_, dt_, tf, ALU.mult, [k[0], k[6]], [k[0]])
        P.tt("dve", ar, sn, arow[:, 0, :], ALU.mult, [k[5], "arow"], [k[1]])
        P.tt("dve", t2_, cs, arow[:, 1, :], ALU.mult, [k[4], "arow"], [k[7]])
        P.tt("dve", ar, ar, t2_, ALU.subtract, [k[1], k[7]], [k[1]])
        P.tt("dve", ar, ar, tf, ALU.mult, [k[1], k[6]], [k[1]])
        cre_, cim_ = dt_, ar
        P.dma(th, I["braw"][l, 0], "d_br", reads=[k[2]], writes=[k[2]])
        P.dma(mag, I["braw"][l, 1], "d_bi", reads=[k[3]], writes=[k[3]])
        P.tt("dve", cs, cre_, th, ALU.mult, [k[0], k[2]], [k[4]])
        P.tt("dve", sn, cim_, mag, ALU.mult, [k[1], k[3]], [k[5]])
        P.tt("dve", Bre[:].rearrange("p a b -> p (a b)"), cs, sn, ALU.subtract, [k[4], k[5]], ["@Bre"])
        P.tt("dve", cs, cre_, mag, ALU.mult, [k[0], k[3]], [k[4]])
        P.tt("dve", sn, cim_, th, ALU.mult, [k[1], k[2]], [k[5]])
        P.tt("dve", Bim[:].rearrange("p a b -> p (a b)"), cs, sn, ALU.add, [k[4], k[5]], ["@Bim"])
        P.dma(th, I["craw"][l, 0], "d_br", reads=[k[2]], writes=[k[2]])
        P.dma(mag, I["craw"][l, 1], "d_bi", reads=[k[3]], writes=[k[3]])
        P.cp("dve", Cre[:].rearrange("p a b -> p (a b)"), th, [k[2]], ["@Cre"])
        P.ts("dve", Cim[:].rearrange("p a b -> p (a b)"), mag, -1.0, None, ALU.mult, None, [k[3]], ["@Cim"])
        tnames = ["iota", "ang", "tmpf", "bre0", "bre1", "bim0", "bim1", "a1", "a20", "a21", "a3", "a40", "a41", "btr", "bti",
                  "zr0", "zr1", "zi0", "zi1", "m10", "m11", "m20", "m21", "m3", "m4", "yv", "x2", "inn", "sg", "sg20", "sg21"]
        P.memset("dve", fence[:], 0.0, list(k) + tnames)
        TT = {}
        for i_, nm in enumerate(tnames):
            TT[nm] = w0[i_ // 4][:, (i_ % 4) * T:(i_ % 4 + 1) * T]

        class _V:
            def __init__(self, ap):
                self.ap = ap

            def __getitem__(self, idx):
                return self.ap[idx]
        iota, ang, tmpf = _V(TT["iota"]), _V(TT["ang"]), _V(TT["tmpf"])
        bre = [_V(TT["bre0"]), _V(TT["bre1"])]
        bim = [_V(TT["bim0"]), _V(TT["bim1"])]
        a1 = _V(TT["a1"])
        a2 = [_V(TT["a20"]), _V(TT["a21"])]
        a3 = _V(TT["a3"])
        a4 = [_V(TT["a40"]), _V(TT["a41"])]
        btr, bti = _V(TT["btr"]), _V(TT["bti"])
        zr = [_V(TT["zr0"]), _V(TT["zr1"])]
        zi = [_V(TT["zi0"]), _V(TT["zi1"])]
        m1 = [_V(TT["m10"]), _V(TT["m11"])]
        m2 = [_V(TT["m20"]), _V(TT["m21"])]
        m3, m4 = _V(TT["m3"]), _V(TT["m4"])
        yv, x2, inn, sg = _V(TT["yv"]), _V(TT["x2"]), _V(TT["inn"]), _V(TT["sg"])
        sg2 = [_V(TT["sg20"]), _V(TT["sg21"])]
        P.dma(iota[:], I["iota"], "d_iota", writes=["iota"])
        S8 = [s[:] for s in sm]
        k2 = lam_setup("dve", ast[:, 0, :], ast[:, 1, :], ast[:, 2, :], S8, smi[:], "sw", "ast")
        sdt, sar, sth, smag, scs, ssn, stf, st2 = S8
        P.cp("dve", rho[:], smag, [k2[3]], ["@rho"])
        P.ts("dve", sth, ssn, PI, None, ALU.add, None, [k2[5]], [k2[2]])
        P.ts("dve", stf, sth, float(T), float((PI * T) % TWO_PI), ALU.mult, ALU.add, [k2[2]], [k2[6]])
        P.cp("dve", st2, stf, [k2[6]], [k2[7]])
        P.ts("dve", st2, st2, PI / 2.0, None, ALU.add, None, [k2[7]], [k2[7]])
        P.range_reduce("dve", stf, sdt, smi[:], k2[6], [k2[0], "swi"])
        P.range_reduce("dve", st2, sdt, smi[:], k2[7], [k2[0], "swi"])
        P.act(ETs[:], stf, AF.Sin, [k2[6]], ["ETs"])
        P.act(ETc[:], st2, AF.Sin, [k2[7]], ["ETc"])
        P.ts("dve", ang[:], iota[:], PI, None, ALU.mult, None, ["iota"], ["ang"])
        P.range_reduce("dve", ang[:], tmpf[:], tmpi[:], "ang", ["tmpf", "tmpi"])
        P.ts("dve", ang[:], ang[:], PI, None, ALU.add, None, ["ang"], ["ang"])
        for st_ in range(16):
            for cs_i, (dstT, shift) in enumerate(((Es, 0.0), (Ec, PI / 2.0))):
                P.stt("dve", a1[:], iota[:], sth[:, st_:st_ + 1], ang[:], ALU.mult, ALU.add, ["iota", k2[2], "ang"], ["a1"])
                if shift:
                    P.ts("dve", a1[:], a1[:], shift, None, ALU.add, None, ["a1"], ["a1"])
                P.range_reduce("dve", a1[:], tmpf[:], tmpi[:], "a1", ["tmpf", "tmpi"])
                P.act(dstT[:, st_, :], a1[:], AF.Sin, ["a1"], [f"@E{cs_i}_{st_}"])
        EK = lambda st_: [f"@E0_{st_}", f"@E1_{st_}"]
        P.load_w(Wg, I["w_glu"][l], 4, 512, "Wg", stg, "stg")
        WgK = [f"Wg_{kc}" for kc in range(4)]
        X1 = {nm: P.sb("x1_" + nm, [128, T], F32) for nm in ("a1", "a3", "btr", "bti", "m3", "m4", "yv", "x2", "inn", "sg")}
        singles = [dict(a1=a1, a3=a3, btr=btr, bti=bti, m3=m3, m4=m4, yv=yv, x2=x2, inn=inn, sg=sg), X1]
        dumA = P.sb("dumA", [128, 1], F32)
        dumP = P.sb("dumP", [128, 1], F32)
        P.memset("dve", fence[:], 1.0, list(k) + tnames + ["@fence"])
        P.act(dumA[:], fence[:], AF.Copy, ["@fence"], ["dumA"])
        P.cp("pool", dumP[:], fence[:], ["@fence"], ["dumP"])

        for ti in range(NT):
            t0 = ti * T
            up = ti % 2
            P.dma(uT[up][:], UT[:, :, t0:t0 + T], f"d_uT{up}", writes=[f"@uT{up}"])
            if ti % 4 == 0:
                P.memset("pool", car[:], 0.0, ["@car"])
            for e in range(2):
                P.begin_task(f"e{e}")
                sgl = singles[e]
                a1_, a3_, btr_, bti_, m3_, m4_ = sgl["a1"], sgl["a3"], sgl["btr"], sgl["bti"], sgl["m3"], sgl["m4"]
                yv_, x2_, inn_, sg_ = sgl["yv"], sgl["x2"], sgl["inn"], sgl["sg"]
                pr = e
                for c in (e, e + 2):
                    for q in range(4):
                        st_ = 4 * c + q
                        P.mm(psA[pr][:], Bre[:, st_, :], uT[up][:, c, :], True, True, ["@Bre", f"@uT{up}"], ["psA"])
                        P.mm(psB[pr][:], Bim[:, st_, :], uT[up][:, c, :], True, True, ["@Bim", f"@uT{up}"], ["psB"])
                        P.act(bre[pr][:], psA[pr][:], AF.Copy, ["psA"], ["bre"])
                        P.act(bim[pr][:], psB[pr][:], AF.Copy, ["psB"], ["bim"])
                        ek = EK(st_)
                        P.tt("pool", a2[pr][:], bim[pr][:], Es[:, st_, :], ALU.mult, ["bim"] + ek, ["a2"])
                        P.tt("pool", a4[pr][:], bre[pr][:], Es[:, st_, :], ALU.mult, ["bre"] + ek, ["a4"])
                        P.tt("dve", a1_[:], bre[pr][:], Ec[:, st_, :], ALU.mult, ["bre"] + ek, ["a1"])
                        P.tt("dve", btr_[:], a1_[:], a2[pr][:], ALU.add, ["a1", "a2"], ["btr"])
                        P.tt("pool", a3_[:], bim[pr][:], Ec[:, st_, :], ALU.mult, ["bim"] + ek, ["a3"])
                        P.tt("dve", bti_[:], a3_[:], a4[pr][:], ALU.subtract, ["a3", "a4"], ["bti"])
                        P.scan(zr[pr][:], rho[:, st_:st_ + 1].to_broadcast([128, T]), btr_[:], car[:, 0, st_:st_ + 1],
                               ["@rho", "btr", "@car"], ["zr"])
                        P.scan(zi[pr][:], rho[:, st_:st_ + 1].to_broadcast([128, T]), bti_[:], car[:, 1, st_:st_ + 1],
                               ["@rho", "bti", "@car"], ["zi"])
                        P.cp("pool", zl[:, 0, st_:st_ + 1], zr[pr][:, T - 1:T], ["zr"], [f"@zlr{st_}"])
                        P.cp("pool", zl[:, 1, st_:st_ + 1], zi[pr][:, T - 1:T], ["zi"], [f"@zli{st_}"])
                        P.tt("pool", m1[pr][:], zi[pr][:], Es[:, st_, :], ALU.mult, ["zi"] + ek, ["m1"])
                        P.tt("pool", m2[pr][:], zr[pr][:], Es[:, st_, :], ALU.mult, ["zr"] + ek, ["m2"])
                        P.tt("dve", m3_[:], zr[pr][:], Ec[:, st_, :], ALU.mult, ["zr"] + ek, ["m3"])
                        P.tt("dve", sre[pr][:], m3_[:], m1[pr][:], ALU.subtract, ["m3", "m1"], ["sre"])
                        P.tt("pool", m4_[:], zi[pr][:], Ec[:, st_, :], ALU.mult, ["zi"] + ek, ["m4"])
                        P.tt("dve", sim[pr][:], m4_[:], m2[pr][:], ALU.add, ["m4", "m2"], ["sim"])
                        P.mm(psY[pr][:], Cre[:, st_, :], sre[pr][:], q == 0, False, ["@Cre", "sre"], ["psY"])
                        P.mm(psY[pr][:], Cim[:, st_, :], sim[pr][:], False, q == 3, ["@Cim", "sim"], ["psY"])
                    P.stt("dve", yv_[:], uT[up][:, c, :], dsk[:, c:c + 1], psY[pr][:], ALU.mult, ALU.add,
                          [f"@uT{up}", "@dsk", "psY"], ["yv"])
                    P.act(x2_[:], yv_[:], AF.Square, ["yv"], ["x2"])
                    P.ts("dve", x2_[:], x2_[:], 0.044715, 1.0, ALU.mult, ALU.add, ["x2"], ["x2"])
                    P.tt("dve", inn_[:], x2_[:], yv_[:], ALU.mult, ["x2", "yv"], ["inn"])
                    P.act(sg_[:], inn_[:], AF.Sigmoid, ["inn"], ["sg"], scale=2.0 * math.sqrt(2.0 / PI))
                    P.tt("dve", ygf[:, c, :], yv_[:], sg_[:], ALU.mult, ["yv", "sg"], [f"@ygf{c}"])
                    P.cp("pool", ygb[:, c, :], ygf[:, c, :], [f"@ygf{c}"], [f"@ygb{c}"])
                P.end_task()
            P.run_tasks()
            ZK = [f"@zlr{i}" for i in range(16)] + [f"@zli{i}" for i in range(16)]
            P.tt("pool", ct[0][:], zl[:, 0, :], ETc[:], ALU.mult, ZK + ["ETc"], ["ct0"])
            P.tt("pool", ct[1][:], zl[:, 1, :], ETs[:], ALU.mult, ZK + ["ETs"], ["ct1"])
            P.tt("pool", ct[2][:], zl[:, 1, :], ETc[:], ALU.mult, ZK + ["ETc"], ["ct2"])
            P.tt("pool", ct[3][:], zl[:, 0, :], ETs[:], ALU.mult, ZK + ["ETs"], ["ct3"])
            P.tt("pool", car[:, 0, :], ct[0][:], ct[1][:], ALU.subtract, ["ct0", "ct1"], ["@car"])
            P.tt("pool", car[:, 1, :], ct[2][:], ct[3][:], ALU.add, ["ct2", "ct3"], ["@car"])
            for m in range(4):
                gp = m % 2
                P.mmg(psG[gp][:], [(Wg[:, kc, m * 128:(m + 1) * 128], ygb[:, kc, :]) for kc in range(4)],
                      WgK + [f"@ygb{kc}" for kc in range(4)], [f"psG{gp}"])
                P.act(sg2[gp][:], psG[gp][:], AF.Sigmoid, [f"psG{gp}"], [f"sg2{gp}"])
                P.tt("dve", ost[gp][:], ygf[:, m, :], sg2[gp][:], ALU.mult, [f"@ygf{m}", f"sg2{gp}"], [f"ost{gp}"])
                P.dma(OBT[:, m, t0:t0 + T], ost[gp][:], f"s_ost{gp}", reads=[f"ost{gp}"])
        P.close()

    def phase_pool(l):
        P = Phase(nc, f"c{l}")
        Wp = P.sb("Wp", [128, 4, 128], BF16)
        Wpf = P.sb("Wpf", [128, 4, 128], F32)
        psc = P.sb("psc", [128, 4], F32)
        invc = P.sb("invc", [128, 4, 16], F32)
        pb = [P.sb(f"pb{i}", [128, 4, 528], BF16) for i in range(2)]
        pf = P.sb("pf", [128, 4, 528], F32)
        sa = P.sb("sa", [128, 528], F32)
        sb_ = P.sb("sb", [128, 528], F32)
        fx = P.sb("fx", [128, 16], F32)
        pl = [P.sb(f"pl{i}", [128, T], BF16) for i in range(2)]
        ost = [P.sb(f"ost{i}", [128, T], BF16) for i in range(2)]
        P.mkbanks(2)
        P.dma(Wpf[:], I["w_pool"][l], "d_wp", writes=["Wpf"])
        P.cp("dve", Wp[:], Wpf[:], ["Wpf"], ["Wp"])
        P.dma(psc[:], I["pscale"][l], "d_psc", writes=["psc"])
        P.dma(invc[:], I["invc"], "d_invc", writes=["invc"])
        for ti in range(NT):
            t0 = ti * T
            pp = ti % 2
            first = (ti % 4 == 0)
            if first:
                P.memset("pool", pb[pp][:, :, 0:16], 0.0, [f"pb{pp}"])
                P.dma(pb[pp][:, :, 16:528], PT[:, :, t0:t0 + T], f"d_pb{pp}", writes=[f"pb{pp}"])
            else:
                P.dma(pb[pp][:, :, 1:528], PT[:, :, t0 - 15:t0 + T], f"d_pb{pp}", writes=[f"pb{pp}"])
            P.cp("pool", pf[:, :, 1:528], pb[pp][:, :, 1:528], [f"pb{pp}"], ["pf"])
            for g in range(4):
                w = 2 ** (g + 1)
                src = pf[:, g, :]
                bufs = [sa, sb_]
                cur = None
                d = 1
                i = 0
                while d < w:
                    dstb = bufs[i % 2]
                    s_in = src if cur is None else cur[:]
                    lo_c = 2 * d
                    P.tt("dve", dstb[:, lo_c:528], s_in[:, lo_c:528], s_in[:, lo_c - d:528 - d], ALU.add,
                         ["pf", "sa", "sb"], ["sa" if i % 2 == 0 else "sb"])
                    cur = dstb
                    d *= 2
                    i += 1
                gp = g % 2
                P.stt("dve", pl[gp][:], cur[:, 16:528], 1.0 / w, pf[:, g, 16:528], ALU.mult, ALU.subtract,
                      ["sa", "sb", "pf"], [f"pl{gp}"])
                if first:
                    P.tt("dve", fx[:], cur[:, 16:32], invc[:, g, :], ALU.mult, ["sa", "sb", "invc"], ["fx"])
                    P.tt("dve", pl[gp][:, 0:16], fx[:], pf[:, g, 16:32], ALU.subtract, ["fx", "pf", f"pl{gp}"], [f"pl{gp}"])
                bk, bkk = P.bank()
                P.mm(bk[:], Wp[:, g, :], pl[gp][:], True, True, ["Wp", f"pl{gp}"], [bkk])
                P.act(ost[gp][:], bk[:], AF.Copy, [bkk, "psc"], [f"ost{gp}"], scale=psc[:, g:g + 1])
                P.dma(OCT[:, g, t0:t0 + T], ost[gp][:], f"s_ost{gp}", reads=[f"ost{gp}"])
        P.close()

    def phase_merge(l, mg):
        P = Phase(nc, f"m{l}{mg}")
        Wg = [P.sb(f"Wg{b}", [128, 16, 512], BF16) for b in range(3)]
        KCB = (8, 4, 4)
        Pw = [P.sb(f"Pw{b}", [128, KCB[b], 512], BF16) for b in range(3)]
        stg = [P.sb(f"stg{i}", [128, 2048], F32) for i in range(4)]
        bg = P.sb("bg", [128, 3, 16], F32)
        hT = [P.sb(f"hT{i}", [128, 16, T], BF16) for i in range(2)]
        oT = [[P.sb(f"oT{b}_{i}", [128, KCB[b], T], BF16) for i in range(2)] for b in range(3)]
        sgb = [P.sb(f"sgb{i}", [128, T], F32) for i in range(3)]
        cb_ = [P.sb(f"cb{i}", [128, T], F32) for i in range(3)]
        ost = [P.sb(f"ost{i}", [128, T], BF16) for i in range(2)]
        P.mkbanks(8)
        P.dma(bg[:], I["b_gate"][l].rearrange("b p m -> p b m"), "d_bg", writes=["bg"])
        psrc = (I["p_a"], I["p_b"], I["p_c"])
        cs = slice(mg * 512, (mg + 1) * 512)
        i0 = 0
        for b in range(3):
            P.load_w(Wg[b], I["w_gate"][l, b][:, cs], 16, 512, f"Wg{b}", stg, "stg", i0)
            P.load_w(Pw[b], psrc[b][l][:, cs], KCB[b], 512, f"Pw{b}", stg, "stg", i0)
        OS = (OAT, OBT, OCT)
        for ti in range(NT):
            t0 = ti * T
            hp = ti % 2
            P.dma(hT[hp][:], HT[:, :, t0:t0 + T], f"d_hT{hp}", writes=[f"hT{hp}"])
            for b in range(3):
                P.dma(oT[b][hp][:], OS[b][:, :, t0:t0 + T], f"d_oT{b}{hp}", writes=[f"oT{b}{hp}"])
            for mi in range(4):
                m = mg * 4 + mi
                for b in range(3):
                    bk, bkk = P.bank()
                    P.mmg(bk[:], [(Wg[b][:, kc, mi * 128:(mi + 1) * 128], hT[hp][:, kc, :]) for kc in range(16)],
                          [f"Wg{b}_{kc}" for kc in range(16)] + [f"hT{hp}"], [bkk])
                    P.act(sgb[b][:], bk[:], AF.Sigmoid, [bkk, "bg"], [f"sgb{b}"], bias=bg[:, b, m:m + 1], scale=1.0)
                    bk2, bkk2 = P.bank()
                    P.mmg(bk2[:], [(Pw[b][:, kc, mi * 128:(mi + 1) * 128], oT[b][hp][:, kc, :]) for kc in range(KCB[b])],
                          [f"Pw{b}_{kc}" for kc in range(KCB[b])] + [f"oT{b}{hp}"], [bkk2])
                    P.tt("dve", cb_[b][:], sgb[b][:], bk2[:], ALU.mult, [f"sgb{b}", bkk2], [f"cb{b}"])
                op_ = mi % 2
                P.tt("pool", cb_[0][:], cb_[0][:], cb_[1][:], ALU.add, ["cb0", "cb1"], ["cb0"])
                P.tt("pool", ost[op_][:], cb_[0][:], cb_[2][:], ALU.add, ["cb0", "cb2"], [f"ost{op_}"])
                P.dma(MT[:, m, t0:t0 + T], ost[op_][:], f"s_ost{op_}", reads=[f"ost{op_}"])
        P.close()

    def phase_wout(l, XS, XD):
        P = Phase(nc, f"o{l}")
        Wo = P.sb("Wo", [128, 16, D], BF16)
        stg = [P.sb(f"stg{i}", [128, D], F32) for i in range(4)]
        gt = P.sb("gt", [128, D], F32)
        mT = [P.sb(f"mT{i}", [128, 16, T], BF16) for i in range(2)]
        xt = [P.sb(f"xt{i}", [128, D], F32) for i in range(2)]
        tmp = [P.sb(f"tmp{i}", [128, 512], F32) for i in range(2)]
        xo = [P.sb(f"xo{i}", [128, D], F32) for i in range(2)]
        P.mkbanks(4)
        junk = P.sb("junk", [128, D], BF16)
        xn = [P.sb(f"xn{i}", [128, D], BF16) for i in range(2)]
        hst = [P.sb(f"hst{i}", [128, 16, 128], BF16) for i in range(2)]
        ssq = [P.sb(f"ssq{i}", [128, 1], F32) for i in range(2)]
        rt = [P.sb(f"rt{i}", [128, 1], F32) for i in range(2)]
        rstd = [P.sb(f"rstd{i}", [128, 1], F32) for i in range(2)]
        epsb = P.sb("epsb", [128, 1], F32)
        ident = P.sb("ident", [128, 128], BF16)
        sc = P.sb("sc", [128, 16], F32)
        sh = [P.sb(f"sh{i}", [128, 16], F32) for i in range(NB)]
        g = P.sb("g", [128, 16], F32)
        A = [P.sb(f"A{i}", [128, 16], F32) for i in range(NB)]
        pbk = [P.ps(f"pb{i}", (128, 1024), BF16) for i in range(4)]
        P.memset("dve", epsb[:], EPS, ["epsb"])
        P.dma(ident[:], I["ident"], "d_id", writes=["ident"])
        P.dma(g[:], I["g_norm"][l, 1], "d_g", writes=["g"])
        off = 3 * D
        for b in range(NB):
            P.dma(sh[b][:], MOD[l, b, off:off + D].rearrange("(kc p) -> p kc", p=128), f"d_sh{b}", writes=[f"sh{b}"],
                  allow_slow_non_contiguous=True)
            P.dma(sc[:], MOD[l, b, off + D:off + 2 * D].rearrange("(kc p) -> p kc", p=128), "d_sc", writes=["sc"],
                  allow_slow_non_contiguous=True)
            P.stt("dve", A[b][:], sc[:], 1.0, g[:], ALU.add, ALU.mult, ["sc", "g"], [f"A{b}"])
        P.load_w(Wo, I["w_out"][l], 16, D, "Wo", stg, "stg")
        WoK = [f"Wo_{kc}" for kc in range(16)]
        pend_norm = [None]
        for ti in range(NT):
            t0 = ti * T
            b = ti // 4
            hp = ti % 2
            if ti % 4 == 0:
                P.dma(gt[:], MOD[l, b:b + 1, 2 * D:3 * D].partition_broadcast(128), "d_gt", writes=["gt"])
            P.dma(mT[hp][:], MT[:, :, t0:t0 + T], f"d_mT{hp}", writes=[f"mT{hp}"])
            for ts_ in range(4):
                xp = ts_ % 2
                r0 = t0 + ts_ * 128
                P.dma(xt[xp][:], XS[r0:r0 + 128, :], f"d_xt{xp}", writes=[f"xt{xp}"])
                for n in range(4):
                    bk, bkk = P.bank()
                    P.mmg(bk[:], [(mT[hp][:, kc, ts_ * 128:(ts_ + 1) * 128], Wo[:, kc, n * 512:(n + 1) * 512]) for kc in range(16)],
                          WoK + [f"mT{hp}"], [bkk])
                    tp = n % 2
                    P.tt("dve", tmp[tp][:], bk[:], gt[:, n * 512:(n + 1) * 512], ALU.mult, [bkk, "gt"], [f"tmp{tp}"])
                    P.tt("pool", xo[xp][:, n * 512:(n + 1) * 512], tmp[tp][:], xt[xp][:, n * 512:(n + 1) * 512], ALU.add,
                         [f"tmp{tp}", f"xt{xp}"], [f"xo{xp}_{n}"])
                XK = [f"xo{xp}_{n}" for n in range(4)]
                P.dma(XD[r0:r0 + 128, :], xo[xp][:], f"s_xo{xp}", reads=XK)
                def norm_part(xp=xp, r0=r0, b=b, XK=XK):
                    par = xp
                    P.act(junk[:], xo[xp][:], AF.Square, XK, ["junk", f"ssq{par}"], accum=ssq[par][:])
                    P.act(rt[par][:], ssq[par][:], AF.Sqrt, [f"ssq{par}", "epsb"], [f"rt{par}"], bias=epsb[:], scale=1.0 / D)
                    P.recip(rstd[par][:], rt[par][:], [f"rt{par}"], [f"rstd{par}"])
                    P.ts("dve", xn[par][:], xo[xp][:], rstd[par][:, 0:1], None, ALU.mult, None, XK + [f"rstd{par}"], [f"xn{par}"])
                    for q4 in range(4):
                        for j in range(4):
                            kc = q4 * 4 + j
                            P.tr(pbk[q4][:, j * 128:(j + 1) * 128], xn[par][:, kc * 128:(kc + 1) * 128], ident[:],
                                 [f"xn{par}", "ident"], [f"pb{q4}"])
                        for j in range(4):
                            kc = q4 * 4 + j
                            P.act(hst[par][:, kc, :], pbk[q4][:, j * 128:(j + 1) * 128], AF.Identity,
                                  [f"pb{q4}", f"A{b}", f"sh{b}"], [f"hst{par}_{kc}"], bias=sh[b][:, kc:kc + 1], scale=A[b][:, kc:kc + 1])
                    P.dma(HT[:, :, r0:r0 + 128], hst[par][:], f"s_hst{par}", reads=[f"hst{par}_{kc}" for kc in range(16)])
                if pend_norm[0] is not None:
                    pend_norm[0]()
                pend_norm[0] = norm_part
        pend_norm[0]()
        P.close()

    def phase_ffn(l, j0, nj, XS, XD):
        P = Phase(nc, f"f{l}_{j0}")
        Wa = P.sb("Wa", [128, 16, nj * 128], BF16)
        Wb = P.sb("Wb", [128, 16, nj * 128], BF16)
        Wd = P.sb("Wd", [128, nj, D], BF16)
        stg = [P.sb(f"stg{i}", [128, 1152], F32) for i in range(3)]
        cw = P.sb("cw", [128, 3, NFC], F32)
        cbias = P.sb("cbias", [128, NFC], F32)
        gt = P.sb("gt", [128, D], F32)
        hT = [P.sb(f"hT{i}", [128, 16, T], BF16) for i in range(2)]
        abuf = [P.sb(f"abuf{i}", [128, T + 2], F32) for i in range(2)]
        acc = [P.sb(f"acc{i}", [128, T], F32) for i in range(2)]
        sl = [P.sb(f"sl{i}", [128, T], F32) for i in range(2)]
        carry = P.sb("carry", [128, nj, 2], F32)
        actT = P.sb("actT", [128, nj, T], BF16)
        xt = [P.sb(f"xt{i}", [128, D], F32) for i in range(2)]
        tmp = [P.sb(f"tmp{i}", [128, 512], F32) for i in range(2)]
        P.mkbanks(8)
        P.dma(cw[:], I["conv_w"][l], "d_cw", writes=["cw"])
        P.dma(cbias[:], I["conv_b"][l], "d_cb", writes=["cbias"])
        c0 = j0 * 128
        P.load_w(Wa, I["w_up"][l][:, c0:c0 + nj * 128], 16, nj * 128, "Wa", stg, "stg")
        P.load_w(Wb, I["w_up"][l][:, DFF + c0:DFF + c0 + nj * 128], 16, nj * 128, "Wb", stg, "stg")
        for hf in range(2):
            P.load_w(Wd[:, :, hf * 1024:(hf + 1) * 1024], I["w_down"][l][c0:c0 + nj * 128, hf * 1024:(hf + 1) * 1024], nj, 1024, f"Wd{hf}", stg, "stg")
        WaK = [f"Wa_{kc}" for kc in range(16)]
        WbK = [f"Wb_{kc}" for kc in range(16)]
        state = {"cnt": 0, "ub": 0, "db": 0}

        def ubank():
            i = state["ub"] % 4
            state["ub"] += 1
            return P.banks[i], f"bk{i}"

        def dbank():
            i = 4 + state["db"] % 4
            state["db"] += 1
            return P.banks[i], f"bk{i}"

        def up_chunk(ti, jj):
            hp = ti % 2
            j = j0 + jj
            pr = state["cnt"] % 2
            state["cnt"] += 1
            bkA, kA = ubank()
            P.mmg(bkA[:], [(Wa[:, kc, jj * 128:(jj + 1) * 128], hT[hp][:, kc, :]) for kc in range(16)], WaK + [f"hT{hp}"], [kA])
            bkB, kB = ubank()
            P.mmg(bkB[:], [(Wb[:, kc, jj * 128:(jj + 1) * 128], hT[hp][:, kc, :]) for kc in range(16)], WbK + [f"hT{hp}"], [kB])
            P.cp("pool", abuf[pr][:, 0:2], carry[:, jj, :], [f"carry{jj}"], [f"abuf{pr}"])
            P.act(abuf[pr][:, 2:T + 2], bkA[:], AF.Copy, [kA], [f"abuf{pr}"])
            P.cp("pool", carry[:, jj, :], abuf[pr][:, T:T + 2], [f"abuf{pr}"], [f"carry{jj}"])
            P.act(acc[pr][:], bkA[:], AF.Identity, [kA, "cw", "cbias"], [f"acc{pr}"], bias=cbias[:, j:j + 1], scale=cw[:, 2, j:j + 1])
            P.stt("dve", acc[pr][:], abuf[pr][:, 1:T + 1], cw[:, 1, j:j + 1], acc[pr][:], ALU.mult, ALU.add,
                  [f"abuf{pr}", "cw", f"acc{pr}"], [f"acc{pr}"])
            P.stt("dve", acc[pr][:], abuf[pr][:, 0:T], cw[:, 0, j:j + 1], acc[pr][:], ALU.mult, ALU.add,
                  [f"abuf{pr}", "cw", f"acc{pr}"], [f"acc{pr}"])

            def stage2():
                P.act(sl[pr][:], acc[pr][:], AF.Silu, [f"acc{pr}"], [f"sl{pr}"])
                P.tt("dve", actT[:, jj, :], sl[pr][:], bkB[:], ALU.mult, [f"sl{pr}", kB], [f"actT{jj}"])
            return stage2

        AK = [f"actT{jj}" for jj in range(nj)]

        def down(ti):
            t0 = ti * T
            b = ti // 4
            if ti % 4 == 0:
                P.dma(gt[:], MOD[l, b:b + 1, 5 * D:6 * D].partition_broadcast(128), "d_gt", writes=["gt"])
            for ts_ in range(4):
                xp = ts_ % 2
                r0 = t0 + ts_ * 128
                P.dma(xt[xp][:], XS[r0:r0 + 128, :], f"d_xt{xp}", writes=[f"xt{xp}"])
                for n in range(4):
                    bk, bkk = dbank()
                    P.mmg(bk[:], [(actT[:, jj, ts_ * 128:(ts_ + 1) * 128], Wd[:, jj, n * 512:(n + 1) * 512]) for jj in range(nj)],
                          AK + [f"Wd{n // 2}_{jj}" for jj in range(nj)], [bkk])
                    tp = n % 2
                    P.tt("dve", tmp[tp][:], bk[:], gt[:, n * 512:(n + 1) * 512], ALU.mult, [bkk, "gt"], [f"tmp{tp}"])
                    P.tt("pool", xt[xp][:, n * 512:(n + 1) * 512], tmp[tp][:], xt[xp][:, n * 512:(n + 1) * 512], ALU.add,
                         [f"tmp{tp}", f"xt{xp}"], [f"xt{xp}"])
                P.dma(XD[r0:r0 + 128, :], xt[xp][:], f"s_xo{xp}", reads=[f"xt{xp}"])

        pend = None
        pend_down = None
        P.dma(hT[0][:], HT[:, :, 0:T], "d_hT0", writes=["hT0"])
        for ti in range(NT):
            if ti % 4 == 0:
                P.memset("pool", carry[:], 0.0, [f"carry{jj_}" for jj_ in range(nj)])
            for jj in range(nj):
                s2 = up_chunk(ti, jj)
                if pend is not None:
                    pend()
                pend = s2
                if jj == 0:
                    if pend_down is not None:
                        pend_down()
                        pend_down = None
                    if ti + 1 < NT:
                        hp2 = (ti + 1) % 2
                        P.dma(hT[hp2][:], HT[:, :, (ti + 1) * T:(ti + 2) * T], f"d_hT{hp2}", writes=[f"hT{hp2}"])
            pend_down = (lambda ti=ti: down(ti))
        pend()
        pend_down()
        P.close()

    FG = [(0, 9), (9, 9), (18, 9), (27, 8), (35, 8)]

    def run():
        phase_mod()
        if stop_now():
            return
        xin = I["x"]
        for l in range(2):
            phase_norm(l, 0, xin)
            if stop_now():
                return
            phase_win(l)
            if stop_now():
                return
            phase_dsa(l)
            if stop_now():
                return
            phase_s5(l)
            if stop_now():
                return
            phase_pool(l)
            if stop_now():
                return
            for mg in range(4):
                phase_merge(l, mg)
            if stop_now():
                return
            phase_wout(l, xin, XA)
            if stop_now():
                return
            xfinal = OUT if l == 1 else XB
            chain = [XA, XB, XA, XB, xfinal] if False else None
            for gi, (j0, nj) in enumerate(FG):
                phase_ffn(l, j0, nj, XA if gi == 0 else XB, xfinal if gi == len(FG) - 1 else XB)
            if stop_now():
                return
            xin = XB

    if only == "norm":
        phase_norm(0, 0, I["x"])
    else:
        run()
    nc._declared_inputs = list(I.keys()) + (["MOD"] if only else [])
    return nc


def _host_layout(inputs):
    f = lambda a: np.ascontiguousarray(np.asarray(a, dtype=np.float32))
    w_in = f(inputs["w_in"])
    q, k, v, qi, ki, wi, u, p = np.split(w_in, [1024, 1152, 1280, 1792, 1856, 1864, 2376], axis=-1)
    sh = {}
    sh["w_inR"] = np.ascontiguousarray(np.concatenate([q, k, qi, ki, ki, u, p], axis=-1))
    sh["w_vw"] = np.ascontiguousarray(np.concatenate([v, wi], axis=-1))
    sh["w_ada"] = f(inputs["w_ada"])
    sh["b_ada"] = f(inputs["b_ada"])
    g1 = f(inputs["g_norm1"]).reshape(2, 16, 128).transpose(0, 2, 1)
    g2 = f(inputs["g_norm2"]).reshape(2, 16, 128).transpose(0, 2, 1)
    sh["g_norm"] = np.ascontiguousarray(np.stack([g1, g2], axis=1))
    sh["g_qk"] = np.ascontiguousarray(np.stack([f(inputs["g_q"]), f(inputs["g_k"])], axis=-1))
    a_re, a_im, ldt = f(inputs["a_re"]), f(inputs["a_im"]), f(inputs["log_dt"])
    ldt_e = np.repeat(ldt[:, :, None], 64, axis=2)
    row = np.stack([a_re.reshape(2, 2048), a_im.reshape(2, 2048), ldt_e.reshape(2, 2048)], axis=1)
    sh["a_row"] = np.ascontiguousarray(row)
    sh["a_st"] = np.ascontiguousarray(row.reshape(2, 3, 16, 128).transpose(0, 1, 3, 2))
    b_re, b_im = f(inputs["b_re"]), f(inputs["b_im"])
    c_re, c_im = f(inputs["c_re"]), f(inputs["c_im"])
    braw = np.zeros((2, 2, 128, 16, 128), np.float32)
    craw = np.zeros((2, 2, 128, 16, 128), np.float32)
    for st in range(16):
        c_, q_ = st // 4, st % 4
        for gl in range(2):
            g = 8 * c_ + 2 * q_ + gl
            gc = 2 * q_ + gl
            for ri, (bsrc, csrc) in enumerate(((b_re, c_re), (b_im, c_im))):
                braw[:, ri, gc * 16:(gc + 1) * 16, st, gl * 64:(gl + 1) * 64] = bsrc[:, g].transpose(0, 2, 1)
                craw[:, ri, gl * 64:(gl + 1) * 64, st, gc * 16:(gc + 1) * 16] = csrc[:, g].transpose(0, 2, 1)
    sh["braw"] = braw.reshape(2, 2, 128, 2048)
    sh["craw"] = craw.reshape(2, 2, 128, 2048)
    sh["dskip"] = np.ascontiguousarray(f(inputs["d_skip"]).reshape(2, 4, 128).transpose(0, 2, 1))
    sh["w_glu"] = f(inputs["w_glu"])
    sh["w_pool"] = np.ascontiguousarray(f(inputs["w_pool"]).transpose(0, 2, 1, 3))
    sh["pscale"] = np.ascontiguousarray(f(inputs["pool_scale"]).reshape(2, 4, 128).transpose(0, 2, 1))
    sh["p_a"], sh["p_b"], sh["p_c"] = f(inputs["p_a"]), f(inputs["p_b"]), f(inputs["p_c"])
    sh["w_gate"] = f(inputs["w_gate"])
    sh["b_gate"] = np.ascontiguousarray(f(inputs["b_gate"]).reshape(2, 3, 16, 128).transpose(0, 1, 3, 2))
    sh["w_out"] = f(inputs["w_out"])
    sh["w_up"] = f(inputs["w_up"])
    sh["conv_w"] = np.ascontiguousarray(f(inputs["conv_w"]).reshape(2, 3, NFC, 128).transpose(0, 3, 1, 2))
    sh["conv_b"] = np.ascontiguousarray(f(inputs["conv_b"]).reshape(2, NFC, 128).transpose(0, 2, 1))
    sh["w_down"] = f(inputs["w_down"])
    sh["ident"] = np.eye(128, dtype=np.float32).astype(ml_dtypes.bfloat16)
    rm = np.zeros((2, 128, 128), np.float32)
    for j in range(16):
        rm[0, 16 + j, j] = -1.0
        rm[0, j, 16 + j] = 1.0
    for hb in (0, 64):
        for j in range(8):
            rm[1, hb + 8 + j, hb + j] = -1.0
            rm[1, hb + j, hb + 8 + j] = 1.0
    sh["rmat"] = rm
    invf = np.zeros((128, 2), np.float32)
    fq = (500000.0 ** (-np.arange(16, dtype=np.float32) * np.float32(2.0 / 32))).astype(np.float32)
    fi = (500000.0 ** (-np.arange(8, dtype=np.float32) * np.float32(2.0 / 16))).astype(np.float32)
    invf[0:16, 0] = fq
    invf[16:32, 0] = fq
    for hb in (0, 64):
        invf[hb:hb + 8, 1] = fi
        invf[hb + 8:hb + 16, 1] = fi
    sh["invf"] = invf
    sh["iota"] = np.ascontiguousarray(np.broadcast_to(np.arange(T, dtype=np.float32), (128, T)))
    q3 = np.array([[k * 0.25 ** (it + 1) for k in (1, 2, 3)] for it in range(NQ)], np.float32).reshape(-1)
    sh["halv"] = np.ascontiguousarray(np.broadcast_to(q3, (128, 3 * NQ)))
    invc = np.zeros((128, 4, 16), np.float32)
    for g in range(4):
        w = 2 ** (g + 1)
        invc[:, g, :] = 1.0 / np.minimum(np.arange(1, 17), w)
    sh["invc"] = invc
    return sh


def _in_maps(inputs, ncores):
    sh = _host_layout(inputs)
    x = np.asarray(inputs["x"], np.float32)
    c = np.asarray(inputs["c"], np.float32)
    pos = np.asarray(inputs["positions"], np.int32)
    maps = []
    for i in range(ncores):
        m = dict(sh)
        m["x"] = np.ascontiguousarray(x[NB * i:NB * (i + 1)].reshape(NTOK, D))
        m["c"] = np.ascontiguousarray(c[NB * i:NB * (i + 1)])
        m["pos"] = np.ascontiguousarray(pos[NB * i:NB * (i + 1)])
        maps.append(m)
    return maps


def kernel(**inputs):
    nc = build()
    maps = _in_maps(inputs, NCORES)
    maps = [{k: m[k] for k in nc._declared_inputs} for m in maps]
    res = run_bass_kernel_spmd(nc, maps, core_ids=list(range(NCORES)))
    outs = [np.asarray(r["out"], np.float32).reshape(NB, SEQ, D) for r in res.results]
    return np.concatenate(outs, axis=0)
```

```python
from contextlib import ExitStack
import math
import numpy as np
import ml_dtypes
import concourse.bass as bass
import concourse.mybir as mybir
from concourse.bass_utils import run_bass_kernel_spmd

F32 = mybir.dt.float32
BF16 = mybir.dt.bfloat16
I32 = mybir.dt.int32
ALU = mybir.AluOpType
AF = mybir.ActivationFunctionType

NCORES = 8
D = 2048
SEQ = 2048
NB = 2
NTOK = NB * SEQ
T = 512
NT = NTOK // T
DFF = 5504
NFC = DFF // 128
EPS = 1e-6
PI = math.pi
TWO_PI = 2.0 * math.pi
NBIS = 18
NQ = 9
ENGS = ("pe", "act", "dve", "pool", "sp")


class Sched:
    def __init__(self, nc, name):
        self.nc = nc
        self.name = name
        self.streams = {e: [] for e in ENGS}
        self.cnt = {}
        self.res = {}
        self.seen = {e: {} for e in ENGS}

    def op(self, eng, fn, reads=(), writes=(), dma=None, ndma=1):
        deps = {}

        def add(tok):
            if tok is None:
                return
            k, v = tok
            if deps.get(k, 0) < v:
                deps[k] = v

        for r in reads:
            st = self.res.get(r)
            if st:
                add(st[0])
        for w in writes:
            st = self.res.get(w)
            if st:
                add(st[0])
                for k, v in st[1].items():
                    add((k, v))
        if dma is None:
            key = eng
            self.cnt[key] = self.cnt.get(key, 0) + 1
        else:
            key = dma
            self.cnt[key] = self.cnt.get(key, 0) + 16 * ndma
        tok = (key, self.cnt[key])
        waits = []
        seen = self.seen[eng]
        for k, v in deps.items():
            if k == eng and eng == "pe":
                continue
            if seen.get(k, 0) >= v:
                continue
            seen[k] = v
            waits.append((k, v))
        self.streams[eng].append((fn, waits, key if dma is None else None))
        for r in reads:
            st = self.res.setdefault(r, [None, {}])
            if st[1].get(key, 0) < tok[1]:
                st[1][key] = tok[1]
        for w in writes:
            self.res[w] = [tok, {}]
        return tok

    def emit(self):
        nc = self.nc
        dma_keys = [k for k in self.cnt if k not in ENGS]
        sems = {k: nc.alloc_semaphore(name=f"{self.name}_{k}") for k in self.cnt}
        with ExitStack() as st:
            block = st.enter_context(nc.Block())

            def mk(name):
                def body(eng):
                    for fn, waits, inc in self.streams[name]:
                        for k, v in waits:
                            eng.wait_ge(sems[k], v)
                        if inc is None:
                            fn(eng, sems)
                        else:
                            fn(eng).then_inc(sems[inc], 1)
                    if name == "sp":
                        for k in dma_keys:
                            eng.wait_ge(sems[k], self.cnt[k])
                return body

            block.tensor(mk("pe"))
            block.scalar(mk("act"))
            block.vector(mk("dve"))
            block.gpsimd(mk("pool"))
            block.sync(mk("sp"))
        nc.clear_and_free_semaphores(list(sems.values()))
        nc.all_engine_barrier()


class Phase:
    def __init__(self, nc, name):
        self.nc = nc
        self.name = name
        self.st = ExitStack()
        self.S = Sched(nc, name)
        self.nbank = 0
        self.banks = []
        self.rr_i = 0
        self.pref = ""
        self.cur = None
        self.tasklists = []

    def _k(self, keys):
        if not self.pref:
            return list(keys)
        return [k if k.startswith("@") else self.pref + k for k in keys]

    def _op(self, eng, fn, reads=(), writes=(), dma=None):
        reads = self._k(reads)
        writes = self._k(writes)
        if self.cur is not None:
            self.cur.append((eng, fn, reads, writes, dma))
        else:
            self.S.op(eng, fn, reads=reads, writes=writes, dma=dma)

    def begin_task(self, name):
        self.pref = name + "_"
        self.cur = []
        self.tasklists.append(self.cur)

    def end_task(self):
        self.pref = ""
        self.cur = None

    def run_tasks(self):
        lists = self.tasklists
        idx = [0] * len(lists)
        tot = [max(1, len(x)) for x in lists]
        while True:
            best = None
            for i, lst in enumerate(lists):
                if idx[i] < len(lst):
                    fr = idx[i] / tot[i]
                    if best is None or fr < best[0]:
                        best = (fr, i)
            if best is None:
                break
            i = best[1]
            eng, fn, reads, writes, dma = lists[i][idx[i]]
            idx[i] += 1
            self.S.op(eng, fn, reads=reads, writes=writes, dma=dma)
        self.tasklists = []

    def sb(self, name, shape, dt):
        return self.st.enter_context(self.nc.sbuf_tensor(f"{self.name}_{name}", shape, dt))

    def ps(self, name, shape=(128, 512), dt=F32):
        return self.st.enter_context(self.nc.psum_tensor(f"{self.name}_{name}", list(shape), dt))

    def mkbanks(self, n):
        self.banks = [self.ps(f"bk{i}") for i in range(n)]

    def bank(self):
        i = self.nbank % len(self.banks)
        self.nbank += 1
        return self.banks[i], f"bk{i}"

    def close(self):
        self.S.emit()
        self.st.close()

    def dma(self, out, in_, key, reads=(), writes=(), **kw):
        key = self.pref + key
        self._op("sp", lambda e, sems: e.dma_start(out=out, in_=in_, **kw).then_inc(sems[key], 16),
                 reads=reads, writes=writes, dma=key)

    def mmg(self, out, pairs, reads, writes):
        pairs = list(pairs)

        def fn(e):
            n = len(pairs)
            ins = None
            for i, (l, r) in enumerate(pairs):
                ins = e.matmul(out, lhsT=l, rhs=r, start=(i == 0), stop=(i == n - 1))
            return ins
        self._op("pe", fn, reads=reads, writes=writes)

    def mm(self, out, lhsT, rhs, start, stop, reads, writes):
        self._op("pe", lambda e: e.matmul(out, lhsT=lhsT, rhs=rhs, start=start, stop=stop), reads=reads, writes=writes)

    def tr(self, out, in_, ident, reads, writes):
        self._op("pe", lambda e: e.transpose(out, in_, ident), reads=reads, writes=writes)

    def act(self, out, in_, func, reads, writes, bias=None, scale=None, accum=None):
        kw = {}
        if bias is not None:
            kw["bias"] = bias
        if scale is not None:
            kw["scale"] = scale
        if accum is not None:
            kw["accum_out"] = accum
        self._op("act", lambda e: e.activation(out=out, in_=in_, func=func, **kw), reads=reads, writes=writes)

    def ts(self, eng, out, in0, s1, s2, op0, op1, reads, writes, accum=None):
        kw = {}
        if op1 is not None:
            kw["op1"] = op1
        if accum is not None:
            kw["accum_out"] = accum
        self._op(eng, lambda e: e.tensor_scalar(out=out, in0=in0, scalar1=s1, scalar2=s2, op0=op0, **kw),
                  reads=reads, writes=writes)

    def tt(self, eng, out, in0, in1, op, reads, writes):
        self._op(eng, lambda e: e.tensor_tensor(out=out, in0=in0, in1=in1, op=op), reads=reads, writes=writes)

    def stt(self, eng, out, in0, scalar, in1, op0, op1, reads, writes):
        self._op(eng, lambda e: e.scalar_tensor_tensor(out=out, in0=in0, scalar=scalar, in1=in1, op0=op0, op1=op1),
                  reads=reads, writes=writes)

    def cp(self, eng, out, in_, reads, writes):
        self._op(eng, lambda e: e.tensor_copy(out=out, in_=in_), reads=reads, writes=writes)

    def memset(self, eng, out, val, writes):
        self._op(eng, lambda e: e.memset(out, val), writes=writes)

    def recip(self, out, in_, reads, writes):
        self._op("dve", lambda e: e.reciprocal(out=out, in_=in_), reads=reads, writes=writes)

    def scan(self, out, d0, d1, init, reads, writes):
        self._op("dve", lambda e: e.tensor_tensor_scan(out=out, data0=d0, data1=d1, initial=init,
                                                        op0=ALU.mult, op1=ALU.add), reads=reads, writes=writes)

    def range_reduce(self, eng, x, tmpf, tmpi, key, tkeys):
        self.ts(eng, tmpf, x, 1.0 / TWO_PI, None, ALU.mult, None, [key], [tkeys[0]])
        self.cp(eng, tmpi, tmpf, [tkeys[0]], [tkeys[1]])
        self.cp(eng, tmpf, tmpi, [tkeys[1]], [tkeys[0]])
        self.stt(eng, x, tmpf, -TWO_PI, x, ALU.mult, ALU.add, [tkeys[0], key], [key])
        self.ts(eng, tmpf, x, PI, -TWO_PI, ALU.is_gt, ALU.mult, [key], [tkeys[0]])
        self.tt(eng, x, x, tmpf, ALU.add, [key, tkeys[0]], [key])
        self.ts(eng, tmpf, x, -PI, TWO_PI, ALU.is_lt, ALU.mult, [key], [tkeys[0]])
        self.tt(eng, x, x, tmpf, ALU.add, [key, tkeys[0]], [key])

    def load_w(self, dst, src, kcs, ncols, key, stg, stgkey, i0=0):
        cap = stg[0].shape[1]
        g = max(1, min(kcs, cap // ncols))
        ns = len(stg)
        engs = ("pool", "act", "dve")
        for gi, kc0 in enumerate(range(0, kcs, g)):
            gg = min(g, kcs - kc0)
            cnt = self.rr_i
            self.rr_i += 1
            par = cnt % ns
            sview = stg[par][:, 0:gg * ncols].rearrange("p (k n) -> p k n", k=gg)
            self.dma(sview, src[kc0 * 128:(kc0 + gg) * 128, :].rearrange("(k p) n -> p k n", p=128),
                     f"d_{stgkey}{par}", writes=[f"{stgkey}{par}"])
            eng = engs[cnt % 3]
            wk = [f"{key}_{kc}" for kc in range(kc0, kc0 + gg)]
            if eng == "act":
                self.act(dst[:, kc0:kc0 + gg, :], sview, AF.Copy, [f"{stgkey}{par}"], wk)
            else:
                self.cp(eng, dst[:, kc0:kc0 + gg, :], sview, [f"{stgkey}{par}"], wk)


def build(upto=None, debug=False, only=None):
    nc = bass.Bass("TRN2", target_bir_lowering=False)
    kinds = {}

    def din(name, shape, dt=F32):
        return nc.dram_tensor(name, list(shape), dt, kind="ExternalInput").ap()

    def dscr(name, shape, dt):
        kind = "ExternalOutput" if debug else "Internal"
        return nc.dram_tensor(name, list(shape), dt, kind=kind).ap()

    SPECS = {
        "x": ([NTOK, D], F32),
        "c": ([NB, D], F32),
        "pos": ([NB, SEQ], I32),
        "w_ada": ([2, D, 6 * D], F32),
        "b_ada": ([2, 6 * D], F32),
        "g_norm": ([2, 2, 128, 16], F32),
        "w_inR": ([2, D, 2816], F32),
        "w_vw": ([2, D, 136], F32),
        "g_qk": ([2, 128, 2], F32),
        "a_row": ([2, 3, 2048], F32),
        "a_st": ([2, 3, 128, 16], F32),
        "braw": ([2, 2, 128, 2048], F32),
        "craw": ([2, 2, 128, 2048], F32),
        "dskip": ([2, 128, 4], F32),
        "w_glu": ([2, 512, 512], F32),
        "w_pool": ([2, 128, 4, 128], F32),
        "pscale": ([2, 128, 4], F32),
        "p_a": ([2, 1024, D], F32),
        "p_b": ([2, 512, D], F32),
        "p_c": ([2, 512, D], F32),
        "w_gate": ([2, 3, D, D], F32),
        "b_gate": ([2, 3, 128, 16], F32),
        "w_out": ([2, D, D], F32),
        "w_up": ([2, D, 2 * DFF], F32),
        "conv_w": ([2, 128, 3, NFC], F32),
        "conv_b": ([2, 128, NFC], F32),
        "w_down": ([2, DFF, D], F32),
        "ident": ([128, 128], BF16),
        "rmat": ([2, 128, 128], F32),
        "invf": ([128, 2], F32),
        "iota": ([128, T], F32),
        "halv": ([128, 3 * NQ], F32),
        "invc": ([128, 4, 16], F32),
    }

    class _Lazy(dict):
        def __missing__(self, key):
            shape, dt = SPECS[key]
            v = din(key, shape, dt)
            self[key] = v
            return v
    I = _Lazy()
    OUT = nc.dram_tensor("out", [NTOK, D], F32, kind="ExternalOutput").ap()

    MOD = din("MOD", [2, NB, 6 * D]) if only else dscr("MOD", [2, NB, 6 * D], F32)
    HT = dscr("HT", [128, 16, NTOK], BF16)
    QT = dscr("QT", [128, 8, NTOK], BF16)
    KT = dscr("KT", [128, NTOK], BF16)
    QIT = dscr("QIT", [128, 4, NTOK], BF16)
    KIT = dscr("KIT", [128, NTOK], BF16)
    VV = dscr("VV", [NTOK, 128], BF16)
    WI = dscr("WI", [NTOK, 8], F32)
    UT = dscr("UT", [128, 4, NTOK], BF16)
    PT = dscr("PT", [128, 4, NTOK], BF16)
    OAT = dscr("OAT", [128, 8, NTOK], BF16)
    OBT = dscr("OBT", [128, 4, NTOK], BF16)
    OCT = dscr("OCT", [128, 4, NTOK], BF16)
    MT = dscr("MT", [128, 16, NTOK], BF16)
    XA = dscr("XA", [NTOK, D], F32)
    XB = dscr("XB", [NTOK, D], F32)

    phases_done = [0]

    def stop_now():
        phases_done[0] += 1
        return upto is not None and phases_done[0] > upto

    def phase_mod():
        P = Phase(nc, "p0")
        cT = P.sb("cT", [128, NB, 16], F32)
        cA = P.sb("cA", [128, NB, 16], F32)
        bada = P.sb("bada", [NB, 6 * D], F32)
        wst = [P.sb(f"wst{i}", [128, 4096], F32) for i in range(4)]
        mrow = [P.sb(f"mrow{i}", [NB, 512], F32) for i in range(2)]
        P.mkbanks(8)
        for b in range(NB):
            P.dma(cT[:, b, :], I["c"][b, :].rearrange("(kc p) -> p kc", p=128), "d_c", writes=["cT"], allow_slow_non_contiguous=True)
        P.act(cA[:], cT[:], AF.Silu, ["cT"], ["cA"])
        cnt = 0
        for l in range(2):
            P.dma(bada[:], I["b_ada"][l:l + 1, :].partition_broadcast(NB), "d_bada", writes=["bada"])
            for third in range(3):
                c0 = third * 4096
                for kc in range(16):
                    par = cnt % 4
                    cnt += 1
                    P.dma(wst[par][:], I["w_ada"][l, kc * 128:(kc + 1) * 128, c0:c0 + 4096], f"d_wst{par}", writes=[f"wst{par}"])
                    for g_ in range(8):
                        P.mm(P.banks[g_][0:NB, :], cA[:, :, kc], wst[par][:, g_ * 512:(g_ + 1) * 512], kc == 0, kc == 15,
                             ["cA", f"wst{par}"], [f"bk{g_}"])
                for g_ in range(8):
                    cg = third * 8 + g_
                    mp = cg % 2
                    P.tt("dve", mrow[mp][:], P.banks[g_][0:NB, :], bada[:, cg * 512:(cg + 1) * 512], ALU.add, [f"bk{g_}", "bada"], [f"mrow{mp}"])
                    P.dma(MOD[l, :, cg * 512:(cg + 1) * 512], mrow[mp][:], f"s_mrow{mp}", reads=[f"mrow{mp}"])
        P.close()

    def phase_norm(l, which, XS):
        P = Phase(nc, f"n{l}{which}")
        epsb = P.sb("epsb", [128, 1], F32)
        ident = P.sb("ident", [128, 128], BF16)
        sc = P.sb("sc", [128, 16], F32)
        sh = [P.sb(f"sh{i}", [128, 16], F32) for i in range(NB)]
        g = P.sb("g", [128, 16], F32)
        A = [P.sb(f"A{i}", [128, 16], F32) for i in range(NB)]
        P.memset("dve", epsb[:], EPS, ["@epsb"])
        P.dma(ident[:], I["ident"], "d_id", writes=["@ident"])
        P.dma(g[:], I["g_norm"][l, which], "d_g", writes=["g"])
        off = 3 * which * D
        for b in range(NB):
            P.dma(sh[b][:], MOD[l, b, off:off + D].rearrange("(kc p) -> p kc", p=128), f"d_sh{b}", writes=[f"@sh{b}"],
                  allow_slow_non_contiguous=True)
            P.dma(sc[:], MOD[l, b, off + D:off + 2 * D].rearrange("(kc p) -> p kc", p=128), "d_sc", writes=["sc"],
                  allow_slow_non_contiguous=True)
            P.stt("dve", A[b][:], sc[:], 1.0, g[:], ALU.add, ALU.mult, ["sc", "g"], [f"@A{b}"])
        for b in range(NB):
            sfx = f"_{b}"
            xt = [P.sb(f"xt{i}" + sfx, [128, D], F32) for i in range(2)]
            junk = P.sb("junk" + sfx, [128, D], BF16)
            xn = [P.sb(f"xn{i}" + sfx, [128, D], BF16) for i in range(2)]
            hst = [P.sb(f"hst{i}" + sfx, [128, 16, 128], BF16) for i in range(2)]
            ssq = [P.sb(f"ssq{i}" + sfx, [128, 1], F32) for i in range(2)]
            rt = [P.sb(f"rt{i}" + sfx, [128, 1], F32) for i in range(2)]
            rstd = [P.sb(f"rstd{i}" + sfx, [128, 1], F32) for i in range(2)]
            pbk = [P.ps(f"pb{i}" + sfx, (128, 1024), BF16) for i in range(4)]
            P.begin_task(f"b{b}")
            for R in range(b * 16, (b + 1) * 16):
                par = R % 2
                P.dma(xt[par][:], XS[R * 128:(R + 1) * 128, :], f"d_xt{par}", writes=[f"xt{par}"])
                P.act(junk[:], xt[par][:], AF.Square, [f"xt{par}"], ["junk", f"ssq{par}"], accum=ssq[par][:])
                P.act(rt[par][:], ssq[par][:], AF.Sqrt, [f"ssq{par}", "@epsb"], [f"rt{par}"], bias=epsb[:], scale=1.0 / D)
                P.recip(rstd[par][:], rt[par][:], [f"rt{par}"], [f"rstd{par}"])
                P.ts("dve", xn[par][:], xt[par][:], rstd[par][:, 0:1], None, ALU.mult, None, [f"xt{par}", f"rstd{par}"], [f"xn{par}"])
                for q4 in range(4):
                    for j in range(4):
                        kc = q4 * 4 + j
                        P.tr(pbk[q4][:, j * 128:(j + 1) * 128], xn[par][:, kc * 128:(kc + 1) * 128], ident[:],
                             [f"xn{par}", "@ident"], [f"pb{q4}"])
                    for j in range(4):
                        kc = q4 * 4 + j
                        P.act(hst[par][:, kc, :], pbk[q4][:, j * 128:(j + 1) * 128], AF.Identity,
                              [f"pb{q4}", f"@A{b}", f"@sh{b}"], [f"hst{par}_{kc}"], bias=sh[b][:, kc:kc + 1], scale=A[b][:, kc:kc + 1])
                P.dma(HT[:, :, R * 128:(R + 1) * 128], hst[par][:], f"s_hst{par}",
                      reads=[f"hst{par}_{kc}" for kc in range(16)])
            P.end_task()
        P.run_tasks()
        P.close()

    def phase_win(l):
        P = Phase(nc, f"w{l}")
        Wq = P.sb("Wq", [128, 16, 2816], BF16)
        Wv = P.sb("Wv", [128, 16, 136], BF16)
        stg = [P.sb(f"stg{i}", [128, 2816], F32) for i in range(3)]
        hT = [P.sb(f"hT{i}", [128, 16, T], BF16) for i in range(2)]
        ones = P.sb("ones", [128, 128], BF16)
        rmat = P.sb("rmat", [128, 2, 128], F32)
        invf = P.sb("invf", [128, 2], F32)
        gqk = P.sb("gqk", [128, 2], F32)
        epsb = P.sb("epsb", [128, 1], F32)
        posi = [P.sb(f"posi{i}", [128, T], I32) for i in range(2)]
        posf = [P.sb(f"posf{i}", [128, T], F32) for i in range(2)]
        tabs = [[P.sb(f"tab{j}_{i}", [128, T], F32) for i in range(4)] for j in range(2)]
        tmpf = P.sb("tmpf", [128, T], F32)
        tmpi = P.sb("tmpi", [128, T], I32)
        sqb = [P.sb(f"sqb{i}", [128, T], BF16) for i in range(2)]
        rtt = [P.sb(f"rtt{i}", [128, T], F32) for i in range(2)]
        qn = [P.sb(f"qn{i}", [128, T], F32) for i in range(2)]
        t1 = [P.sb(f"t1{i}", [128, T], F32) for i in range(2)]
        t2 = [P.sb(f"t2{i}", [128, T], F32) for i in range(2)]
        ost = [P.sb(f"ost{i}", [128, T], BF16) for i in range(2)]
        vst = [P.sb(f"vst{i}", [128, 128], BF16) for i in range(2)]
        wst = [P.sb(f"wst{i}", [128, 8], F32) for i in range(2)]
        P.mkbanks(8)
        P.memset("dve", epsb[:], EPS, ["epsb"])
        P.memset("dve", ones[:], 1.0 / 128.0, ["ones"])
        P.dma(rmat[:], I["rmat"].rearrange("r k m -> k r m"), "d_rm", writes=["rmat"])
        P.dma(invf[:], I["invf"], "d_if", writes=["invf"])
        P.dma(gqk[:], I["g_qk"][l], "d_gqk", writes=["gqk"])
        P.load_w(Wv, I["w_vw"][l], 16, 136, "Wv", stg, "stg")
        P.load_w(Wq, I["w_inR"][l], 16, 2816, "Wq", stg, "stg")
        WqK = [f"Wq_{kc}" for kc in range(16)]
        WvK = [f"Wv_{kc}" for kc in range(16)]
        def table_ops(ti):
            b_ = ti // 4
            t0_ = ti * T
            tp = ti % 2
            pk = f"tb{tp}_"
            ops = []
            ops.append(lambda: P.dma(posi[tp][:], I["pos"][b_:b_ + 1, t0_ - b_ * SEQ:t0_ - b_ * SEQ + T].partition_broadcast(128),
                                     f"d_pos{tp}", writes=[pk + "posi"]))
            ops.append(lambda: P.cp("dve", posf[tp][:], posi[tp][:], [pk + "posi"], [pk + "posf"]))
            for k in range(4):
                which = k // 2
                shift = (PI / 2.0) if (k % 2 == 0) else 0.0
                x = tabs[tp][k][:]
                key = pk + f"tab{k}"
                ops.append(lambda x=x, key=key, which=which, shift=shift: P.ts(
                    "dve", x, posf[tp][:], invf[:, which:which + 1], shift, ALU.mult, ALU.add, [pk + "posf", "invf"], [key]))
                tk = ["tmpf", "tmpi"]
                ops.append(lambda x=x, key=key: P.ts("dve", tmpf[:], x, 1.0 / TWO_PI, None, ALU.mult, None, [key], [tk[0]]))
                ops.append(lambda: P.cp("dve", tmpi[:], tmpf[:], [tk[0]], [tk[1]]))
                ops.append(lambda: P.cp("dve", tmpf[:], tmpi[:], [tk[1]], [tk[0]]))
                ops.append(lambda x=x, key=key: P.stt("dve", x, tmpf[:], -TWO_PI, x, ALU.mult, ALU.add, [tk[0], key], [key]))
                ops.append(lambda x=x, key=key: P.ts("dve", tmpf[:], x, PI, -TWO_PI, ALU.is_gt, ALU.mult, [key], [tk[0]]))
                ops.append(lambda x=x, key=key: P.tt("dve", x, x, tmpf[:], ALU.add, [key, tk[0]], [key]))
                ops.append(lambda x=x, key=key: P.ts("dve", tmpf[:], x, -PI, TWO_PI, ALU.is_lt, ALU.mult, [key], [tk[0]]))
                ops.append(lambda x=x, key=key: P.tt("dve", x, x, tmpf[:], ALU.add, [key, tk[0]], [key]))
                ops.append(lambda x=x, key=key: P.act(x, x, AF.Sin, [key], [key]))
            return ops

        cnt = 0
        for ti in range(NT):
            b = ti // 4
            t0 = ti * T
            hp = ti % 2
            P.dma(hT[hp][:], HT[:, :, t0:t0 + T], f"d_hT{hp}", writes=[f"hT{hp}"])
            if ti == 0:
                for th_ in table_ops(0):
                    th_()
            nxt = table_ops(ti + 1) if ti + 1 < NT else []
            tab = tabs[ti % 2]
            tpk = f"tb{ti % 2}_"
            for m in range(22):
                bk, bkk = P.bank()
                P.mmg(bk[:], [(Wq[:, kc, m * 128:(m + 1) * 128], hT[hp][:, kc, :]) for kc in range(16)],
                      WqK + [f"hT{hp}"], [bkk])
                pr = cnt % 2
                cnt += 1
                if m <= 13:
                    isqk = m <= 8
                    if isqk:
                        gcol = 0 if m < 8 else 1
                        P.act(sqb[pr][:], bk[:], AF.Square, [bkk], [f"sqb{pr}"])
                        bk2, bkk2 = P.bank()
                        P.mm(bk2[:], ones[:], sqb[pr][:], True, True, ["ones", f"sqb{pr}"], [bkk2])
                        P.act(rtt[pr][:], bk2[:], AF.Sqrt, [bkk2, "epsb"], [f"rtt{pr}"], bias=epsb[:], scale=1.0)
                        P.recip(rtt[pr][:], rtt[pr][:], [f"rtt{pr}"], [f"rtt{pr}"])
                        P.stt("dve", qn[pr][:], bk[:], gqk[:, gcol:gcol + 1], rtt[pr][:], ALU.mult, ALU.mult,
                              [bkk, "gqk", f"rtt{pr}"], [f"qn{pr}"])
                        ri, Ct, St, Ck, Sk = 0, tab[0], tab[1], tpk + "tab0", tpk + "tab1"
                    else:
                        P.act(qn[pr][:], bk[:], AF.Copy, [bkk], [f"qn{pr}"])
                        ri, Ct, St, Ck, Sk = 1, tab[2], tab[3], tpk + "tab2", tpk + "tab3"
                    bk3, bkk3 = P.bank()
                    P.mm(bk3[:], rmat[:, ri, :], qn[pr][:], True, True, ["rmat", f"qn{pr}"], [bkk3])
                    P.tt("pool", t1[pr][:], qn[pr][:], Ct[:], ALU.mult, [f"qn{pr}", Ck], [f"t1{pr}"])
                    P.tt("dve", t2[pr][:], bk3[:], St[:], ALU.mult, [bkk3, Sk], [f"t2{pr}"])
                    P.tt("dve", ost[pr][:], t1[pr][:], t2[pr][:], ALU.add, [f"t1{pr}", f"t2{pr}"], [f"ost{pr}"])
                    if m < 8:
                        dst = QT[:, m, t0:t0 + T]
                    elif m == 8:
                        dst = KT[:, t0:t0 + T]
                    elif m < 13:
                        dst = QIT[:, m - 9, t0:t0 + T]
                    else:
                        dst = KIT[:, t0:t0 + T]
                else:
                    P.act(ost[pr][:], bk[:], AF.Copy, [bkk], [f"ost{pr}"])
                    dst = UT[:, m - 14, t0:t0 + T] if m < 18 else PT[:, m - 18, t0:t0 + T]
                P.dma(dst, ost[pr][:], f"s_ost{pr}", reads=[f"ost{pr}"])
                for _ in range(2):
                    if nxt:
                        nxt.pop(0)()
            while nxt:
                nxt.pop(0)()
            for ts_ in range(4):
                bk, bkk = P.bank()
                pr = ts_ % 2
                P.mmg(bk[:, 0:136], [(hT[hp][:, kc, ts_ * 128:(ts_ + 1) * 128], Wv[:, kc, :]) for kc in range(16)],
                      WvK + [f"hT{hp}"], [bkk])
                P.act(vst[pr][:], bk[:, 0:128], AF.Copy, [bkk], [f"vst{pr}"])
                P.act(wst[pr][:], bk[:, 128:136], AF.Copy, [bkk], [f"wst{pr}"], scale=8.0 ** -0.5)
                r0 = t0 + ts_ * 128
                P.dma(VV[r0:r0 + 128, :], vst[pr][:], f"s_vst{pr}", reads=[f"vst{pr}"])
                P.dma(WI[r0:r0 + 128, :], wst[pr][:], f"s_wst{pr}", reads=[f"wst{pr}"])
        P.close()

    def phase_dsa(l):
        P = Phase(nc, f"a{l}")
        ident = P.sb("ident", [128, 128], BF16)
        ones = P.sb("ones", [128, 128], BF16)
        halv = P.sb("halv", [128, 3 * NQ], F32)
        P.dma(ident[:], I["ident"], "d_id", writes=["@ident"])
        P.dma(halv[:], I["halv"], "d_hv", writes=["@halv"])
        P.memset("dve", ones[:], 1.0, ["@ones"])
        AXX = mybir.AxisListType.X
        for b in range(NB):
            sfx = f"_{b}"
            Kc = P.sb("Kc" + sfx, [128, SEQ], BF16)
            Kic = P.sb("Kic" + sfx, [128, SEQ], BF16)
            Vc = P.sb("Vc" + sfx, [128, 16, 128], BF16)
            qT = [P.sb(f"qT{i}" + sfx, [128, 8, 128], BF16) for i in range(2)]
            qiT = [P.sb(f"qiT{i}" + sfx, [128, 4, 128], BF16) for i in range(2)]
            wi = [P.sb(f"wi{i}" + sfx, [128, 8], F32) for i in range(2)]
            score = P.sb("score" + sfx, [128, SEQ], F32)
            rl = [P.sb(f"rl{i}" + sfx, [128, 512], F32) for i in range(3)]
            junk = P.sb("junk" + sfx, [128, SEQ], F32)
            mask = P.sb("mask" + sfx, [128, SEQ], BF16)
            maskT = P.sb("maskT" + sfx, [128, 16, 128], BF16)
            pT = [P.sb(f"pT{i}" + sfx, [128, 512], BF16) for i in range(2)]
            wk = P.sb("wk" + sfx, [128, 3 * NQ], F32)
            th3 = P.sb("th3" + sfx, [128, 3], F32)
            cs = P.sb("cs" + sfx, [128, 2], F32)
            g2j = P.sb("g2j" + sfx, [128, 2], F32)
            gs = P.sb("gs" + sfx, [128, 1], F32)
            gs2 = P.sb("gs2" + sfx, [128, 1], F32)
            junkA = P.sb("junkA" + sfx, [128, SEQ], BF16)
            junkB = P.sb("junkB" + sfx, [128, SEQ], BF16)
            lo = P.sb("lo" + sfx, [128, 1], F32)
            hi = P.sb("hi" + sfx, [128, 1], F32)
            mid = P.sb("mid" + sfx, [128, 1], F32)
            cntt = P.sb("cntt" + sfx, [128, 1], F32)
            stp = P.sb("stp" + sfx, [128, 1], F32)
            rinv = P.sb("rinv" + sfx, [128, 512], F32)
            ost = [P.sb(f"ost{i}" + sfx, [128, 8, 128], BF16) for i in range(2)]
            psS = [P.ps("psS0" + sfx), P.ps("psS1" + sfx)]
            psO = P.ps("psO" + sfx)
            psI = P.ps("psI" + sfx)
            psR = psI
            psT = psI.bitcast(BF16)
            ibanks = [(psI, "psI"), (psS[0], "psS0"), (psS[1], "psS1")]
            P.begin_task(f"b{b}")
            s0 = b * SEQ
            P.dma(Kc[:], KT[:, s0:s0 + SEQ], "d_Kc", writes=["Kc"])
            P.dma(Kic[:], KIT[:, s0:s0 + SEQ], "d_Kic", writes=["Kic"])
            P.dma(Vc[:], VV[s0:s0 + SEQ, :].rearrange("(kt p) d -> p kt d", p=128), "d_Vc", writes=["Vc"])
            for qb in range(16):
                par = qb % 2
                r0 = s0 + qb * 128
                N = 128 * (qb + 1)
                nkt = qb + 1
                P.dma(qT[par][:], QT[:, :, r0:r0 + 128], f"d_qT{par}", writes=[f"qT{par}"])
                P.dma(qiT[par][:], QIT[:, :, r0:r0 + 128], f"d_qiT{par}", writes=[f"qiT{par}"])
                P.dma(wi[par][:], WI[r0:r0 + 128, :], f"d_wi{par}", writes=[f"wi{par}"])
                nkb = (N + 511) // 512
                ci = 0
                for kb in range(nkb):
                    c0 = kb * 512
                    cw_ = min(512, N - c0)
                    for h in range(8):
                        po = 64 * (h % 2)
                        ibk, ibkk = ibanks[ci % 3]
                        P.mm(ibk[:, 0:cw_], qiT[par][po:po + 64, h // 2, :], Kic[po:po + 64, c0:c0 + cw_], True, True,
                             [f"qiT{par}", "Kic"], [ibkk])
                        rp = ci % 3
                        ci += 1
                        P.act(rl[rp][:, 0:cw_], ibk[:, 0:cw_], AF.Relu, [ibkk], [f"rl{rp}"], scale=0.125)
                        if h == 0:
                            P.ts("dve", score[:, c0:c0 + cw_], rl[rp][:, 0:cw_], wi[par][:, 0:1], None, ALU.mult, None,
                                 [f"rl{rp}", f"wi{par}"], ["score"])
                        else:
                            P.stt("dve", score[:, c0:c0 + cw_], rl[rp][:, 0:cw_], wi[par][:, h:h + 1], score[:, c0:c0 + cw_],
                                  ALU.mult, ALU.add, [f"rl{rp}", f"wi{par}", "score"], ["score"])
                if qb >= 2:
                    P._op("dve", lambda e, N=N, lo=lo, score=score: e.tensor_reduce(out=lo[:], in_=score[:, 0:N - 64], axis=AXX, op=ALU.min),
                          reads=["score"], writes=["lo"])
                    P._op("dve", lambda e, N=N, hi=hi, score=score: e.tensor_reduce(out=hi[:], in_=score[:, 0:N - 64], axis=AXX, op=ALU.max),
                          reads=["score"], writes=["hi"])
                    P._op("dve", lambda e, N=N, mid=mid, score=score: e.tensor_reduce(out=mid[64:128, :], in_=score[64:128, N - 64:N], axis=AXX, op=ALU.min),
                          reads=["score"], writes=["mid"])
                    P.tt("dve", lo[64:128, :], lo[64:128, :], mid[64:128, :], ALU.min, ["lo", "mid"], ["lo"])
                    P._op("dve", lambda e, N=N, mid=mid, score=score: e.tensor_reduce(out=mid[64:128, :], in_=score[64:128, N - 64:N], axis=AXX, op=ALU.max),
                          reads=["score"], writes=["mid"])
                    P.tt("dve", hi[64:128, :], hi[64:128, :], mid[64:128, :], ALU.max, ["hi", "mid"], ["hi"])
                P.memset("dve", score[0:64, N - 64:N], -1e30, ["score"])
                if qb >= 2:
                    P.tt("dve", hi[:], hi[:], lo[:], ALU.subtract, ["hi", "lo"], ["hi"])
                    P.ts("dve", hi[:], hi[:], 1.001, 1e-6, ALU.mult, ALU.add, ["hi"], ["hi"])
                    P.stt("dve", lo[:], hi[:], -0.0005, lo[:], ALU.mult, ALU.add, ["hi", "lo"], ["lo"])
                    P.ts("dve", wk[:], halv[:], hi[:, 0:1], None, ALU.mult, None, ["@halv", "hi"], ["wk"])
                    for it in range(NQ):
                        P.ts("dve", th3[:], wk[:, 3 * it:3 * it + 3], lo[:, 0:1], None, ALU.add, None, ["wk", "lo"], ["th3"])
                        P.act(junkA[:, 0:N], score[:, 0:N], AF.Sign, ["score", "th3"], ["junkA", "cs0"], bias=th3[:, 0:1], scale=-1.0,
                              accum=cs[:, 0:1])
                        P.act(junkB[:, 0:N], score[:, 0:N], AF.Sign, ["score", "th3"], ["junkB", "cs1"], bias=th3[:, 1:2], scale=-1.0,
                              accum=cs[:, 1:2])
                        P.ts("dve", junk[:, 0:N], score[:, 0:N], th3[:, 2:3], 0.0, ALU.is_gt, ALU.add, ["score", "th3"],
                             ["junk", "cntt"], accum=cntt[:])
                        P.ts("dve", g2j[:], cs[:], float(N - 511), 0.0, ALU.is_le, ALU.add, ["cs0", "cs1"], ["g2j", "gs"], accum=gs[:])
                        P.stt("dve", gs2[:], cntt[:], 255.5, gs[:], ALU.is_ge, ALU.add, ["cntt", "gs"], ["gs2"])
                        P.stt("dve", lo[:], gs2[:], wk[:, 3 * it:3 * it + 1], lo[:], ALU.mult, ALU.add, ["gs2", "wk", "lo"], ["lo"])
                else:
                    P.memset("dve", lo[:], -1e29, ["lo"])
                P.ts("dve", mask[:, 0:N], score[:, 0:N], lo[:, 0:1], None, ALU.is_gt, None, ["score", "lo"], ["mask"])
                for kt in range(nkt):
                    j = kt % 4
                    P.tr(psT[:, j * 128:(j + 1) * 128], mask[:, kt * 128:(kt + 1) * 128], ident[:], ["mask", "@ident"], ["psI"])
                    P.act(maskT[:, kt, :], psT[:, j * 128:(j + 1) * 128], AF.Copy, ["psI"], [f"maskT{kt}"])
                for half in range(2):
                    for kt in range(nkt):
                        pp = kt % 2
                        P.mm(psS[pp][:], Kc[:, kt * 128:(kt + 1) * 128],
                             qT[par][:, 4 * half:4 * half + 4, :].rearrange("p a b -> p (a b)"), True, True,
                             ["Kc", f"qT{par}"], [f"psS{pp}"])
                        P.act(pT[pp][:], psS[pp][:], AF.Exp, [f"psS{pp}"], [f"pT{pp}"], scale=128.0 ** -0.5)
                        P.tt("pool" if kt % 2 else "dve", pT[pp][:].rearrange("p (a b) -> p a b", a=4),
                             pT[pp][:].rearrange("p (a b) -> p a b", a=4),
                             maskT[:, kt, :].unsqueeze(1).to_broadcast([128, 4, 128]), ALU.mult,
                             [f"pT{pp}", f"maskT{kt}"], [f"pT{pp}"])
                        P.mm(psO[:], Vc[:, kt, :], pT[pp][:], kt == 0, kt == nkt - 1, ["Vc", f"pT{pp}"], ["psO"])
                        P.mm(psR[:], ones[:], pT[pp][:], kt == 0, kt == nkt - 1, ["@ones", f"pT{pp}"], ["psI"])
                    P.recip(rinv[:], psR[:], ["psI"], ["rinv"])
                    P.tt("dve", ost[par][:, 4 * half:4 * half + 4, :].rearrange("p a b -> p (a b)"), psO[:], rinv[:],
                         ALU.mult, ["psO", "rinv"], [f"ost{par}_{half}"])
                P.dma(OAT[:, :, r0:r0 + 128], ost[par][:], f"s_ost{par}", reads=[f"ost{par}_0", f"ost{par}_1"])
            P.end_task()
        P.run_tasks()
        P.close()

    def phase_s5(l):
        P = Phase(nc, f"s{l}")
        arow = P.sb("arow", [128, 3, 2048], F32)
        w0 = [P.sb(f"w0{i}", [128, 2048], F32) for i in range(8)]
        wi32 = P.sb("wi32", [128, 2048], I32)
        ast = P.sb("ast", [128, 3, 16], F32)
        sm = [P.sb(f"sm{i}", [128, 16], F32) for i in range(8)]
        smi = P.sb("smi", [128, 16], I32)
        rho = P.sb("rho", [128, 16], F32)
        ETc = P.sb("ETc", [128, 16], F32)
        ETs = P.sb("ETs", [128, 16], F32)
        Ec = P.sb("Ec", [128, 16, T], BF16)
        Es = P.sb("Es", [128, 16, T], BF16)
        tmpi = P.sb("tmpi", [128, T], I32)
        fence = P.sb("fence", [128, 1], F32)
        Bre = P.sb("Bre", [128, 16, 128], BF16)
        Bim = P.sb("Bim", [128, 16, 128], BF16)
        Cre = P.sb("Cre", [128, 16, 128], BF16)
        Cim = P.sb("Cim", [128, 16, 128], BF16)
        Wg = P.sb("Wg", [128, 4, 512], BF16)
        stg = [P.sb(f"stg{i}", [128, 2048], F32) for i in range(1)]
        dsk = P.sb("dsk", [128, 4], F32)
        uT = [P.sb(f"uT{i}", [128, 4, T], BF16) for i in range(2)]
        sre = [P.sb(f"sre{i}", [128, T], BF16) for i in range(2)]
        sim = [P.sb(f"sim{i}", [128, T], BF16) for i in range(2)]
        car = P.sb("car", [128, 2, 16], F32)
        zl = P.sb("zl", [128, 2, 16], F32)
        ct = [P.sb(f"ct{i}", [128, 16], F32) for i in range(4)]
        ygf = P.sb("ygf", [128, 4, T], F32)
        ygb = P.sb("ygb", [128, 4, T], BF16)
        ost = [P.sb(f"ost{i}", [128, T], BF16) for i in range(2)]
        psA = [P.ps(f"psA{i}") for i in range(2)]
        psB = [P.ps(f"psB{i}") for i in range(2)]
        psY = [P.ps(f"psY{i}") for i in range(2)]
        psG = [P.ps(f"psG{i}") for i in range(2)]

        for a_ in range(3):
            P.dma(arow[:, a_, :], I["a_row"][l, a_:a_ + 1, :].partition_broadcast(128), "d_arow", writes=["arow"])
        P.dma(ast[:], I["a_st"][l].rearrange("a p t -> p a t"), "d_ast", writes=["ast"])
        P.dma(dsk[:], I["dskip"][l], "d_dsk", writes=["@dsk"])

        def lam_setup(eng, are, aim, ldt, W, Wi, pre, srck):
            dt_, ar, th, mag, cs, sn, tf, t2_ = W
            k = [f"{pre}{i}" for i in range(8)]
            P.act(dt_, ldt, AF.Exp, [srck], [k[0]])
            P.tt(eng, ar, are, dt_, ALU.mult, [srck, k[0]], [k[1]])
            P.tt(eng, th, aim, dt_, ALU.mult, [srck, k[0]], [k[2]])
            P.act(mag, ar, AF.Exp, [k[1]], [k[3]])
            P.ts(eng, sn, th, TWO_PI, None, ALU.add, None, [k[2]], [k[5]])
            P.ts(eng, cs, th, TWO_PI + PI / 2.0, None, ALU.add, None, [k[2]], [k[4]])
            P.range_reduce(eng, sn, tf, Wi, k[5], [k[6], pre + "i"])
            P.range_reduce(eng, cs, tf, Wi, k[4], [k[6], pre + "i"])
            return k

        W = [w[:] for w in w0]
        k = lam_setup("dve", arow[:, 0, :], arow[:, 1, :], arow[:, 2, :], W, wi32[:], "rw", "arow")
        dt_, ar, th, mag, cs, sn, tf, t2_ = W
        P.act(sn, sn, AF.Sin, [k[5]], [k[5]])
        P.act(cs, cs, AF.Sin, [k[4]], [k[4]])
        P.tt("dve", cs, cs, mag, ALU.mult, [k[4], k[3]], [k[4]])
        P.ts("dve", cs, cs, -1.0, None, ALU.add, None, [k[4]], [k[4]])
        P.tt("dve", sn, sn, mag, ALU.mult, [k[5], k[3]], [k[5]])
        P.tt("dve", tf, arow[:, 0, :], arow[:, 0, :], ALU.mult, ["arow"], [k[6]])
        P.tt("dve", t2_, arow[:, 1, :], arow[:, 1, :], ALU.mult, ["arow"], [k[7]])
        P.tt("dve", tf, tf, t2_, ALU.add, [k[6], k[7]], [k[6]])
        P.recip(tf, tf, [k[6]], [k[6]])
        P.tt("dve", dt_, cs, arow[:, 0, :], ALU.mult, [k[4], "arow"], [k[0]])
        P.tt("dve", t2_, sn, arow[:, 1, :], ALU.mult, [k[5], "arow"], [k[7]])
        P.tt("dve", dt_, dt_, t2_, ALU.add, [k[0], k[7]], [k[0]])
        P.tt("dve", dt_, dt_, tf, ALU.mult, [k[0], k[6]], [k[0]])
        P.tt("dve", ar, sn, arow[:, 0, :], ALU.mult, [k[5], "arow"], [k[1]])
        P.tt("dve", t2_, cs, arow[:, 1, :], ALU.mult, [k[4], "arow"], [k[7]])
        P.tt("dve", ar, ar, t2_, ALU.subtract, [k[1], k[7]], [k[1]])
        P.tt("dve", ar, ar, tf, ALU.mult, [k[1], k[6]], [k[1]])
        cre_, cim_ = dt_, ar
        P.dma(th, I["braw"][l, 0], "d_br", reads=[k[2]], writes=[k[2]])
        P.dma(mag, I["braw"][l, 1], "d_bi", reads=[k[3]], writes=[k[3]])
        P.tt("dve", cs, cre_, th, ALU.mult, [k[0], k[2]], [k[4]])
        P.tt("dve", sn, cim_, mag, ALU.mult, [k[1], k[3]], [k[5]])
        P.tt("dve", Bre[:].rearrange("p a b -> p (a b)"), cs, sn, ALU.subtract, [k[4], k[5]], ["@Bre"])
        P.tt("dve", cs, cre_, mag, ALU.mult, [k[0], k[3]], [k[4]])
        P.tt("dve", sn, cim_, th, ALU.mult, [k[1], k[2]], [k[5]])
        P.tt("dve", Bim[:].rearrange("p a b -> p (a b)"), cs, sn, ALU.add, [k[4], k[5]], ["@Bim"])
        P.dma(th, I["craw"][l, 0], "d_br", reads=[k[2]], writes=[k[2]])
        P.dma(mag, I["craw"][l, 1], "d_bi", reads=[k[3]], writes=[k[3]])
        P.cp("dve", Cre[:].rearrange("p a b -> p (a b)"), th, [k[2]], ["@Cre"])
        P.ts("dve", Cim[:].rearrange("p a b -> p (a b)"), mag, -1.0, None, ALU.mult, None, [k[3]], ["@Cim"])
        tnames = ["iota", "ang", "tmpf", "bre0", "bre1", "bim0", "bim1", "a1", "a20", "a21", "a3", "a40", "a41", "btr", "bti",
                  "zr0", "zr1", "zi0", "zi1", "m10", "m11", "m20", "m21", "m3", "m4", "yv", "x2", "inn", "sg", "sg20", "sg21"]
        P.memset("dve", fence[:], 0.0, list(k) + tnames)
        TT = {}
        for i_, nm in enumerate(tnames):
            TT[nm] = w0[i_ // 4][:, (i_ % 4) * T:(i_ % 4 + 1) * T]

        class _V:
            def __init__(self, ap):
                self.ap = ap

            def __getitem__(self, idx):
                return self.ap[idx]
        iota, ang, tmpf = _V(TT["iota"]), _V(TT["ang"]), _V(TT["tmpf"])
        bre = [_V(TT["bre0"]), _V(TT["bre1"])]
        bim = [_V(TT["bim0"]), _V(TT["bim1"])]
        a1 = _V(TT["a1"])
        a2 = [_V(TT["a20"]), _V(TT["a21"])]
        a3 = _V(TT["a3"])
        a4 = [_V(TT["a40"]), _V(TT["a41"])]
        btr, bti = _V(TT["btr"]), _V(TT["bti"])
        zr = [_V(TT["zr0"]), _V(TT["zr1"])]
        zi = [_V(TT["zi0"]), _V(TT["zi1"])]
        m1 = [_V(TT["m10"]), _V(TT["m11"])]
        m2 = [_V(TT["m20"]), _V(TT["m21"])]
        m3, m4 = _V(TT["m3"]), _V(TT["m4"])
        yv, x2, inn, sg = _V(TT["yv"]), _V(TT["x2"]), _V(TT["inn"]), _V(TT["sg"])
        sg2 = [_V(TT["sg20"]), _V(TT["sg21"])]
        P.dma(iota[:], I["iota"], "d_iota", writes=["iota"])
        S8 = [s[:] for s in sm]
        k2 = lam_setup("dve", ast[:, 0, :], ast[:, 1, :], ast[:, 2, :], S8, smi[:], "sw", "ast")
        sdt, sar, sth, smag, scs, ssn, stf, st2 = S8
        P.cp("dve", rho[:], smag, [k2[3]], ["@rho"])
        P.ts("dve", sth, ssn, PI, None, ALU.add, None, [k2[5]], [k2[2]])
        P.ts("dve", stf, sth, float(T), float((PI * T) % TWO_PI), ALU.mult, ALU.add, [k2[2]], [k2[6]])
        P.cp("dve", st2, stf, [k2[6]], [k2[7]])
        P.ts("dve", st2, st2, PI / 2.0, None, ALU.add, None, [k2[7]], [k2[7]])
        P.range_reduce("dve", stf, sdt, smi[:], k2[6], [k2[0], "swi"])
        P.range_reduce("dve", st2, sdt, smi[:], k2[7], [k2[0], "swi"])
        P.act(ETs[:], stf, AF.Sin, [k2[6]], ["ETs"])
        P.act(ETc[:], st2, AF.Sin, [k2[7]], ["ETc"])
        P.ts("dve", ang[:], iota[:], PI, None, ALU.mult, None, ["iota"], ["ang"])
        P.range_reduce("dve", ang[:], tmpf[:], tmpi[:], "ang", ["tmpf", "tmpi"])
        P.ts("dve", ang[:], ang[:], PI, None, ALU.add, None, ["ang"], ["ang"])
        for st_ in range(16):
            for cs_i, (dstT, shift) in enumerate(((Es, 0.0), (Ec, PI / 2.0))):
                P.stt("dve", a1[:], iota[:], sth[:, st_:st_ + 1], ang[:], ALU.mult, ALU.add, ["iota", k2[2], "ang"], ["a1"])
                if shift:
                    P.ts("dve", a1[:], a1[:], shift, None, ALU.add, None, ["a1"], ["a1"])
                P.range_reduce("dve", a1[:], tmpf[:], tmpi[:], "a1", ["tmpf", "tmpi"])
                P.act(dstT[:, st_, :], a1[:], AF.Sin, ["a1"], [f"@E{cs_i}_{st_}"])
        EK = lambda st_: [f"@E0_{st_}", f"@E1_{st_}"]
        P.load_w(Wg, I["w_glu"][l], 4, 512, "Wg", stg, "stg")
        WgK = [f"Wg_{kc}" for kc in range(4)]
        X1 = {nm: P.sb("x1_" + nm, [128, T], F32) for nm in ("a1", "a3", "btr", "bti", "m3", "m4", "yv", "x2", "inn", "sg")}
        singles = [dict(a1=a1, a3=a3, btr=btr, bti=bti, m3=m3, m4=m4, yv=yv, x2=x2, inn=inn, sg=sg), X1]
        dumA = P.sb("dumA", [128, 1], F32)
        dumP = P.sb("dumP", [128, 1], F32)
        P.memset("dve", fence[:], 1.0, list(k) + tnames + ["@fence"])
        P.act(dumA[:], fence[:], AF.Copy, ["@fence"], ["dumA"])
        P.cp("pool", dumP[:], fence[:], ["@fence"], ["dumP"])

        for ti in range(NT):
            t0 = ti * T
            up = ti % 2
            P.dma(uT[up][:], UT[:, :, t0:t0 + T], f"d_uT{up}", writes=[f"@uT{up}"])
            if ti % 4 == 0:
                P.memset("pool", car[:], 0.0, ["@car"])
            for e in range(2):
                P.begin_task(f"e{e}")
                sgl = singles[e]
                a1_, a3_, btr_, bti_, m3_, m4_ = sgl["a1"], sgl["a3"], sgl["btr"], sgl["bti"], sgl["m3"], sgl["m4"]
                yv_, x2_, inn_, sg_ = sgl["yv"], sgl["x2"], sgl["inn"], sgl["sg"]
                pr = e
                for c in (e, e + 2):
                    for q in range(4):
                        st_ = 4 * c + q
                        P.mm(psA[pr][:], Bre[:, st_, :], uT[up][:, c, :], True, True, ["@Bre", f"@uT{up}"], ["psA"])
                        P.mm(psB[pr][:], Bim[:, st_, :], uT[up][:, c, :], True, True, ["@Bim", f"@uT{up}"], ["psB"])
                        P.act(bre[pr][:], psA[pr][:], AF.Copy, ["psA"], ["bre"])
                        P.act(bim[pr][:], psB[pr][:], AF.Copy, ["psB"], ["bim"])
                        ek = EK(st_)
                        P.tt("pool", a2[pr][:], bim[pr][:], Es[:, st_, :], ALU.mult, ["bim"] + ek, ["a2"])
                        P.tt("pool", a4[pr][:], bre[pr][:], Es[:, st_, :], ALU.mult, ["bre"] + ek, ["a4"])
                        P.tt("dve", a1_[:], bre[pr][:], Ec[:, st_, :], ALU.mult, ["bre"] + ek, ["a1"])
                        P.tt("dve", btr_[:], a1_[:], a2[pr][:], ALU.add, ["a1", "a2"], ["btr"])
                        P.tt("pool", a3_[:], bim[pr][:], Ec[:, st_, :], ALU.mult, ["bim"] + ek, ["a3"])
                        P.tt("dve", bti_[:], a3_[:], a4[pr][:], ALU.subtract, ["a3", "a4"], ["bti"])
                        P.scan(zr[pr][:], rho[:, st_:st_ + 1].to_broadcast([128, T]), btr_[:], car[:, 0, st_:st_ + 1],
                               ["@rho", "btr", "@car"], ["zr"])
                        P.scan(zi[pr][:], rho[:, st_:st_ + 1].to_broadcast([128, T]), bti_[:], car[:, 1, st_:st_ + 1],
                               ["@rho", "bti", "@car"], ["zi"])
                        P.cp("pool", zl[:, 0, st_:st_ + 1], zr[pr][:, T - 1:T], ["zr"], [f"@zlr{st_}"])
                        P.cp("pool", zl[:, 1, st_:st_ + 1], zi[pr][:, T - 1:T], ["zi"], [f"@zli{st_}"])
                        P.tt("pool", m1[pr][:], zi[pr][:], Es[:, st_, :], ALU.mult, ["zi"] + ek, ["m1"])
                        P.tt("pool", m2[pr][:], zr[pr][:], Es[:, st_, :], ALU.mult, ["zr"] + ek, ["m2"])
                        P.tt("dve", m3_[:], zr[pr][:], Ec[:, st_, :], ALU.mult, ["zr"] + ek, ["m3"])
                        P.tt("dve", sre[pr][:], m3_[:], m1[pr][:], ALU.subtract, ["m3", "m1"], ["sre"])
                        P.tt("pool", m4_[:], zi[pr][:], Ec[:, st_, :], ALU.mult, ["zi"] + ek, ["m4"])
                        P.tt("dve", sim[pr][:], m4_[:], m2[pr][:], ALU.add, ["m4", "m2"], ["sim"])
                        P.mm(psY[pr][:], Cre[:, st_, :], sre[pr][:], q == 0, False, ["@Cre", "sre"], ["psY"])
                        P.mm(psY[pr][:], Cim[:, st_, :], sim[pr][:], False, q == 3, ["@Cim", "sim"], ["psY"])
                    P.stt("dve", yv_[:], uT[up][:, c, :], dsk[:, c:c + 1], psY[pr][:], ALU.mult, ALU.add,
                          [f"@uT{up}", "@dsk", "psY"], ["yv"])
                    P.act(x2_[:], yv_[:], AF.Square, ["yv"], ["x2"])
                    P.ts("dve", x2_[:], x2_[:], 0.044715, 1.0, ALU.mult, ALU.add, ["x2"], ["x2"])
                    P.tt("dve", inn_[:], x2_[:], yv_[:], ALU.mult, ["x2", "yv"], ["inn"])
                    P.act(sg_[:], inn_[:], AF.Sigmoid, ["inn"], ["sg"], scale=2.0 * math.sqrt(2.0 / PI))
                    P.tt("dve", ygf[:, c, :], yv_[:], sg_[:], ALU.mult, ["yv", "sg"], [f"@ygf{c}"])
                    P.cp("pool", ygb[:, c, :], ygf[:, c, :], [f"@ygf{c}"], [f"@ygb{c}"])
                P.end_task()
            P.run_tasks()
            ZK = [f"@zlr{i}" for i in range(16)] + [f"@zli{i}" for i in range(16)]
            P.tt("pool", ct[0][:], zl[:, 0, :], ETc[:], ALU.mult, ZK + ["ETc"], ["ct0"])
            P.tt("pool", ct[1][:], zl[:, 1, :], ETs[:], ALU.mult, ZK + ["ETs"], ["ct1"])
            P.tt("pool", ct[2][:], zl[:, 1, :], ETc[:], ALU.mult, ZK + ["ETc"], ["ct2"])
            P.tt("pool", ct[3][:], zl[:, 0, :], ETs[:], ALU.mult, ZK + ["ETs"], ["ct3"])
            P.tt("pool", car[:, 0, :], ct[0][:], ct[1][:], ALU.subtract, ["ct0", "ct1"], ["@car"])
            P.tt("pool", car[:, 1, :], ct[2][:], ct[3][:], ALU.add, ["ct2", "ct3"], ["@car"])
            for m in range(4):
                gp = m % 2
                P.mmg(psG[gp][:], [(Wg[:, kc, m * 128:(m + 1) * 128], ygb[:, kc, :]) for kc in range(4)],
                      WgK + [f"@ygb{kc}" for kc in range(4)], [f"psG{gp}"])
                P.act(sg2[gp][:], psG[gp][:], AF.Sigmoid, [f"psG{gp}"], [f"sg2{gp}"])
                P.tt("dve", ost[gp][:], ygf[:, m, :], sg2[gp][:], ALU.mult, [f"@ygf{m}", f"sg2{gp}"], [f"ost{gp}"])
                P.dma(OBT[:, m, t0:t0 + T], ost[gp][:], f"s_ost{gp}", reads=[f"ost{gp}"])
        P.close()

    def phase_pool(l):
        P = Phase(nc, f"c{l}")
        Wp = P.sb("Wp", [128, 4, 128], BF16)
        Wpf = P.sb("Wpf", [128, 4, 128], F32)
        psc = P.sb("psc", [128, 4], F32)
        invc = P.sb("invc", [128, 4, 16], F32)
        pb = [P.sb(f"pb{i}", [128, 4, 528], BF16) for i in range(2)]
        pf = P.sb("pf", [128, 4, 528], F32)
        sa = P.sb("sa", [128, 528], F32)
        sb_ = P.sb("sb", [128, 528], F32)
        fx = P.sb("fx", [128, 16], F32)
        pl = [P.sb(f"pl{i}", [128, T], BF16) for i in range(2)]
        ost = [P.sb(f"ost{i}", [128, T], BF16) for i in range(2)]
        P.mkbanks(2)
        P.dma(Wpf[:], I["w_pool"][l], "d_wp", writes=["Wpf"])
        P.cp("dve", Wp[:], Wpf[:], ["Wpf"], ["Wp"])
        P.dma(psc[:], I["pscale"][l], "d_psc", writes=["psc"])
        P.dma(invc[:], I["invc"], "d_invc", writes=["invc"])
        for ti in range(NT):
            t0 = ti * T
            pp = ti % 2
            first = (ti % 4 == 0)
            if first:
                P.memset("pool", pb[pp][:, :, 0:16], 0.0, [f"pb{pp}"])
                P.dma(pb[pp][:, :, 16:528], PT[:, :, t0:t0 + T], f"d_pb{pp}", writes=[f"pb{pp}"])
            else:
                P.dma(pb[pp][:, :, 1:528], PT[:, :, t0 - 15:t0 + T], f"d_pb{pp}", writes=[f"pb{pp}"])
            P.cp("pool", pf[:, :, 1:528], pb[pp][:, :, 1:528], [f"pb{pp}"], ["pf"])
            for g in range(4):
                w = 2 ** (g + 1)
                src = pf[:, g, :]
                bufs = [sa, sb_]
                cur = None
                d = 1
                i = 0
                while d < w:
                    dstb = bufs[i % 2]
                    s_in = src if cur is None else cur[:]
                    lo_c = 2 * d
                    P.tt("dve", dstb[:, lo_c:528], s_in[:, lo_c:528], s_in[:, lo_c - d:528 - d], ALU.add,
                         ["pf", "sa", "sb"], ["sa" if i % 2 == 0 else "sb"])
                    cur = dstb
                    d *= 2
                    i += 1
                gp = g % 2
                P.stt("dve", pl[gp][:], cur[:, 16:528], 1.0 / w, pf[:, g, 16:528], ALU.mult, ALU.subtract,
                      ["sa", "sb", "pf"], [f"pl{gp}"])
                if first:
                    P.tt("dve", fx[:], cur[:, 16:32], invc[:, g, :], ALU.mult, ["sa", "sb", "invc"], ["fx"])
                    P.tt("dve", pl[gp][:, 0:16], fx[:], pf[:, g, 16:32], ALU.subtract, ["fx", "pf", f"pl{gp}"], [f"pl{gp}"])
                bk, bkk = P.bank()
                P.mm(bk[:], Wp[:, g, :], pl[gp][:], True, True, ["Wp", f"pl{gp}"], [bkk])
                P.act(ost[gp][:], bk[:], AF.Copy, [bkk, "psc"], [f"ost{gp}"], scale=psc[:, g:g + 1])
                P.dma(OCT[:, g, t0:t0 + T], ost[gp][:], f"s_ost{gp}", reads=[f"ost{gp}"])
        P.close()

    def phase_merge(l, mg):
        P = Phase(nc, f"m{l}{mg}")
        Wg = [P.sb(f"Wg{b}", [128, 16, 512], BF16) for b in range(3)]
        KCB = (8, 4, 4)
        Pw = [P.sb(f"Pw{b}", [128, KCB[b], 512], BF16) for b in range(3)]
        stg = [P.sb(f"stg{i}", [128, 2048], F32) for i in range(4)]
        bg = P.sb("bg", [128, 3, 16], F32)
        hT = [P.sb(f"hT{i}", [128, 16, T], BF16) for i in range(2)]
        oT = [[P.sb(f"oT{b}_{i}", [128, KCB[b], T], BF16) for i in range(2)] for b in range(3)]
        sgb = [P.sb(f"sgb{i}", [128, T], F32) for i in range(3)]
        cb_ = [P.sb(f"cb{i}", [128, T], F32) for i in range(3)]
        ost = [P.sb(f"ost{i}", [128, T], BF16) for i in range(2)]
        P.mkbanks(8)
        P.dma(bg[:], I["b_gate"][l].rearrange("b p m -> p b m"), "d_bg", writes=["bg"])
        psrc = (I["p_a"], I["p_b"], I["p_c"])
        cs = slice(mg * 512, (mg + 1) * 512)
        i0 = 0
        for b in range(3):
            P.load_w(Wg[b], I["w_gate"][l, b][:, cs], 16, 512, f"Wg{b}", stg, "stg", i0)
            P.load_w(Pw[b], psrc[b][l][:, cs], KCB[b], 512, f"Pw{b}", stg, "stg", i0)
        OS = (OAT, OBT, OCT)
        for ti in range(NT):
            t0 = ti * T
            hp = ti % 2
            P.dma(hT[hp][:], HT[:, :, t0:t0 + T], f"d_hT{hp}", writes=[f"hT{hp}"])
            for b in range(3):
                P.dma(oT[b][hp][:], OS[b][:, :, t0:t0 + T], f"d_oT{b}{hp}", writes=[f"oT{b}{hp}"])
            for mi in range(4):
                m = mg * 4 + mi
                for b in range(3):
                    bk, bkk = P.bank()
                    P.mmg(bk[:], [(Wg[b][:, kc, mi * 128:(mi + 1) * 128], hT[hp][:, kc, :]) for kc in range(16)],
                          [f"Wg{b}_{kc}" for kc in range(16)] + [f"hT{hp}"], [bkk])
                    P.act(sgb[b][:], bk[:], AF.Sigmoid, [bkk, "bg"], [f"sgb{b}"], bias=bg[:, b, m:m + 1], scale=1.0)
                    bk2, bkk2 = P.bank()
                    P.mmg(bk2[:], [(Pw[b][:, kc, mi * 128:(mi + 1) * 128], oT[b][hp][:, kc, :]) for kc in range(KCB[b])],
                          [f"Pw{b}_{kc}" for kc in range(KCB[b])] + [f"oT{b}{hp}"], [bkk2])
                    P.tt("dve", cb_[b][:], sgb[b][:], bk2[:], ALU.mult, [f"sgb{b}", bkk2], [f"cb{b}"])
                op_ = mi % 2
                P.tt("pool", cb_[0][:], cb_[0][:], cb_[1][:], ALU.add, ["cb0", "cb1"], ["cb0"])
                P.tt("pool", ost[op_][:], cb_[0][:], cb_[2][:], ALU.add, ["cb0", "cb2"], [f"ost{op_}"])
                P.dma(MT[:, m, t0:t0 + T], ost[op_][:], f"s_ost{op_}", reads=[f"ost{op_}"])
        P.close()

    def phase_wout(l, XS, XD):
        P = Phase(nc, f"o{l}")
        Wo = P.sb("Wo", [128, 16, D], BF16)
        stg = [P.sb(f"stg{i}", [128, D], F32) for i in range(4)]
        gt = P.sb("gt", [128, D], F32)
        mT = [P.sb(f"mT{i}", [128, 16, T], BF16) for i in range(2)]
        xt = [P.sb(f"xt{i}", [128, D], F32) for i in range(2)]
        tmp = [P.sb(f"tmp{i}", [128, 512], F32) for i in range(2)]
        xo = [P.sb(f"xo{i}", [128, D], F32) for i in range(2)]
        P.mkbanks(4)
        junk = P.sb("junk", [128, D], BF16)
        xn = [P.sb(f"xn{i}", [128, D], BF16) for i in range(2)]
        hst = [P.sb(f"hst{i}", [128, 16, 128], BF16) for i in range(2)]
        ssq = [P.sb(f"ssq{i}", [128, 1], F32) for i in range(2)]
        rt = [P.sb(f"rt{i}", [128, 1], F32) for i in range(2)]
        rstd = [P.sb(f"rstd{i}", [128, 1], F32) for i in range(2)]
        epsb = P.sb("epsb", [128, 1], F32)
        ident = P.sb("ident", [128, 128], BF16)
        sc = P.sb("sc", [128, 16], F32)
        sh = [P.sb(f"sh{i}", [128, 16], F32) for i in range(NB)]
        g = P.sb("g", [128, 16], F32)
        A = [P.sb(f"A{i}", [128, 16], F32) for i in range(NB)]
        pbk = [P.ps(f"pb{i}", (128, 1024), BF16) for i in range(4)]
        P.memset("dve", epsb[:], EPS, ["epsb"])
        P.dma(ident[:], I["ident"], "d_id", writes=["ident"])
        P.dma(g[:], I["g_norm"][l, 1], "d_g", writes=["g"])
        off = 3 * D
        for b in range(NB):
            P.dma(sh[b][:], MOD[l, b, off:off + D].rearrange("(kc p) -> p kc", p=128), f"d_sh{b}", writes=[f"sh{b}"],
                  allow_slow_non_contiguous=True)
            P.dma(sc[:], MOD[l, b, off + D:off + 2 * D].rearrange("(kc p) -> p kc", p=128), "d_sc", writes=["sc"],
                  allow_slow_non_contiguous=True)
            P.stt("dve", A[b][:], sc[:], 1.0, g[:], ALU.add, ALU.mult, ["sc", "g"], [f"A{b}"])
        P.load_w(Wo, I["w_out"][l], 16, D, "Wo", stg, "stg")
        WoK = [f"Wo_{kc}" for kc in range(16)]
        pend_norm = [None]
        for ti in range(NT):
            t0 = ti * T
            b = ti // 4
            hp = ti % 2
            if ti % 4 == 0:
                P.dma(gt[:], MOD[l, b:b + 1, 2 * D:3 * D].partition_broadcast(128), "d_gt", writes=["gt"])
            P.dma(mT[hp][:], MT[:, :, t0:t0 + T], f"d_mT{hp}", writes=[f"mT{hp}"])
            for ts_ in range(4):
                xp = ts_ % 2
                r0 = t0 + ts_ * 128
                P.dma(xt[xp][:], XS[r0:r0 + 128, :], f"d_xt{xp}", writes=[f"xt{xp}"])
                for n in range(4):
                    bk, bkk = P.bank()
                    P.mmg(bk[:], [(mT[hp][:, kc, ts_ * 128:(ts_ + 1) * 128], Wo[:, kc, n * 512:(n + 1) * 512]) for kc in range(16)],
                          WoK + [f"mT{hp}"], [bkk])
                    tp = n % 2
                    P.tt("dve", tmp[tp][:], bk[:], gt[:, n * 512:(n + 1) * 512], ALU.mult, [bkk, "gt"], [f"tmp{tp}"])
                    P.tt("pool", xo[xp][:, n * 512:(n + 1) * 512], tmp[tp][:], xt[xp][:, n * 512:(n + 1) * 512], ALU.add,
                         [f"tmp{tp}", f"xt{xp}"], [f"xo{xp}_{n}"])
                XK = [f"xo{xp}_{n}" for n in range(4)]
                P.dma(XD[r0:r0 + 128, :], xo[xp][:], f"s_xo{xp}", reads=XK)
                def norm_part(xp=xp, r0=r0, b=b, XK=XK):
                    par = xp
                    P.act(junk[:], xo[xp][:], AF.Square, XK, ["junk", f"ssq{par}"], accum=ssq[par][:])
                    P.act(rt[par][:], ssq[par][:], AF.Sqrt, [f"ssq{par}", "epsb"], [f"rt{par}"], bias=epsb[:], scale=1.0 / D)
                    P.recip(rstd[par][:], rt[par][:], [f"rt{par}"], [f"rstd{par}"])
                    P.ts("dve", xn[par][:], xo[xp][:], rstd[par][:, 0:1], None, ALU.mult, None, XK + [f"rstd{par}"], [f"xn{par}"])
                    for q4 in range(4):
                        for j in range(4):
                            kc = q4 * 4 + j
                            P.tr(pbk[q4][:, j * 128:(j + 1) * 128], xn[par][:, kc * 128:(kc + 1) * 128], ident[:],
                                 [f"xn{par}", "ident"], [f"pb{q4}"])
                        for j in range(4):
                            kc = q4 * 4 + j
                            P.act(hst[par][:, kc, :], pbk[q4][:, j * 128:(j + 1) * 128], AF.Identity,
                                  [f"pb{q4}", f"A{b}", f"sh{b}"], [f"hst{par}_{kc}"], bias=sh[b][:, kc:kc + 1], scale=A[b][:, kc:kc + 1])
                    P.dma(HT[:, :, r0:r0 + 128], hst[par][:], f"s_hst{par}", reads=[f"hst{par}_{kc}" for kc in range(16)])
                if pend_norm[0] is not None:
                    pend_norm[0]()
                pend_norm[0] = norm_part
        pend_norm[0]()
        P.close()

    def phase_ffn(l, j0, nj, XS, XD):
        P = Phase(nc, f"f{l}_{j0}")
        Wa = P.sb("Wa", [128, 16, nj * 128], BF16)
        Wb = P.sb("Wb", [128, 16, nj * 128], BF16)
        Wd = P.sb("Wd", [128, nj, D], BF16)
        stg = [P.sb(f"stg{i}", [128, 1152], F32) for i in range(3)]
        cw = P.sb("cw", [128, 3, NFC], F32)
        cbias = P.sb("cbias", [128, NFC], F32)
        gt = P.sb("gt", [128, D], F32)
        hT = [P.sb(f"hT{i}", [128, 16, T], BF16) for i in range(2)]
        abuf = [P.sb(f"abuf{i}", [128, T + 2], F32) for i in range(2)]
        acc = [P.sb(f"acc{i}", [128, T], F32) for i in range(2)]
        sl = [P.sb(f"sl{i}", [128, T], F32) for i in range(2)]
        carry = P.sb("carry", [128, nj, 2], F32)
        actT = P.sb("actT", [128, nj, T], BF16)
        xt = [P.sb(f"xt{i}", [128, D], F32) for i in range(2)]
        tmp = [P.sb(f"tmp{i}", [128, 512], F32) for i in range(2)]
        P.mkbanks(8)
        P.dma(cw[:], I["conv_w"][l], "d_cw", writes=["cw"])
        P.dma(cbias[:], I["conv_b"][l], "d_cb", writes=["cbias"])
        c0 = j0 * 128
        P.load_w(Wa, I["w_up"][l][:, c0:c0 + nj * 128], 16, nj * 128, "Wa", stg, "stg")
        P.load_w(Wb, I["w_up"][l][:, DFF + c0:DFF + c0 + nj * 128], 16, nj * 128, "Wb", stg, "stg")
        for hf in range(2):
            P.load_w(Wd[:, :, hf * 1024:(hf + 1) * 1024], I["w_down"][l][c0:c0 + nj * 128, hf * 1024:(hf + 1) * 1024], nj, 1024, f"Wd{hf}", stg, "stg")
        WaK = [f"Wa_{kc}" for kc in range(16)]
        WbK = [f"Wb_{kc}" for kc in range(16)]
        state = {"cnt": 0, "ub": 0, "db": 0}

        def ubank():
            i = state["ub"] % 4
            state["ub"] += 1
            return P.banks[i], f"bk{i}"

        def dbank():
            i = 4 + state["db"] % 4
            state["db"] += 1
            return P.banks[i], f"bk{i}"

        def up_chunk(ti, jj):
            hp = ti % 2
            j = j0 + jj
            pr = state["cnt"] % 2
            state["cnt"] += 1
            bkA, kA = ubank()
            P.mmg(bkA[:], [(Wa[:, kc, jj * 128:(jj + 1) * 128], hT[hp][:, kc, :]) for kc in range(16)], WaK + [f"hT{hp}"], [kA])
            bkB, kB = ubank()
            P.mmg(bkB[:], [(Wb[:, kc, jj * 128:(jj + 1) * 128], hT[hp][:, kc, :]) for kc in range(16)], WbK + [f"hT{hp}"], [kB])
            P.cp("pool", abuf[pr][:, 0:2], carry[:, jj, :], [f"carry{jj}"], [f"abuf{pr}"])
            P.act(abuf[pr][:, 2:T + 2], bkA[:], AF.Copy, [kA], [f"abuf{pr}"])
            P.cp("pool", carry[:, jj, :], abuf[pr][:, T:T + 2], [f"abuf{pr}"], [f"carry{jj}"])
            P.act(acc[pr][:], bkA[:], AF.Identity, [kA, "cw", "cbias"], [f"acc{pr}"], bias=cbias[:, j:j + 1], scale=cw[:, 2, j:j + 1])
            P.stt("dve", acc[pr][:], abuf[pr][:, 1:T + 1], cw[:, 1, j:j + 1], acc[pr][:], ALU.mult, ALU.add,
                  [f"abuf{pr}", "cw", f"acc{pr}"], [f"acc{pr}"])
            P.stt("dve", acc[pr][:], abuf[pr][:, 0:T], cw[:, 0, j:j + 1], acc[pr][:], ALU.mult, ALU.add,
                  [f"abuf{pr}", "cw", f"acc{pr}"], [f"acc{pr}"])

            def stage2():
                P.act(sl[pr][:], acc[pr][:], AF.Silu, [f"acc{pr}"], [f"sl{pr}"])
                P.tt("dve", actT[:, jj, :], sl[pr][:], bkB[:], ALU.mult, [f"sl{pr}", kB], [f"actT{jj}"])
            return stage2

        AK = [f"actT{jj}" for jj in range(nj)]

        def down(ti):
            t0 = ti * T
            b = ti // 4
            if ti % 4 == 0:
                P.dma(gt[:], MOD[l, b:b + 1, 5 * D:6 * D].partition_broadcast(128), "d_gt", writes=["gt"])
            for ts_ in range(4):
                xp = ts_ % 2
                r0 = t0 + ts_ * 128
                P.dma(xt[xp][:], XS[r0:r0 + 128, :], f"d_xt{xp}", writes=[f"xt{xp}"])
                for n in range(4):
                    bk, bkk = dbank()
                    P.mmg(bk[:], [(actT[:, jj, ts_ * 128:(ts_ + 1) * 128], Wd[:, jj, n * 512:(n + 1) * 512]) for jj in range(nj)],
                          AK + [f"Wd{n // 2}_{jj}" for jj in range(nj)], [bkk])
                    tp = n % 2
                    P.tt("dve", tmp[tp][:], bk[:], gt[:, n * 512:(n + 1) * 512], ALU.mult, [bkk, "gt"], [f"tmp{tp}"])
                    P.tt("pool", xt[xp][:, n * 512:(n + 1) * 512], tmp[tp][:], xt[xp][:, n * 512:(n + 1) * 512], ALU.add,
                         [f"tmp{tp}", f"xt{xp}"], [f"xt{xp}"])
                P.dma(XD[r0:r0 + 128, :], xt[xp][:], f"s_xo{xp}", reads=[f"xt{xp}"])

        pend = None
        pend_down = None
        P.dma(hT[0][:], HT[:, :, 0:T], "d_hT0", writes=["hT0"])
        for ti in range(NT):
            if ti % 4 == 0:
                P.memset("pool", carry[:], 0.0, [f"carry{jj_}" for jj_ in range(nj)])
            for jj in range(nj):
                s2 = up_chunk(ti, jj)
                if pend is not None:
                    pend()
                pend = s2
                if jj == 0:
                    if pend_down is not None:
                        pend_down()
                        pend_down = None
                    if ti + 1 < NT:
                        hp2 = (ti + 1) % 2
                        P.dma(hT[hp2][:], HT[:, :, (ti + 1) * T:(ti + 2) * T], f"d_hT{hp2}", writes=[f"hT{hp2}"])
            pend_down = (lambda ti=ti: down(ti))
        pend()
        pend_down()
        P.close()

    FG = [(0, 9), (9, 9), (18, 9), (27, 8), (35, 8)]

    def run():
        phase_mod()
        if stop_now():
            return
        xin = I["x"]
        for l in range(2):
            phase_norm(l, 0, xin)
            if stop_now():
                return
            phase_win(l)
            if stop_now():
                return
            phase_dsa(l)
            if stop_now():
                return
            phase_s5(l)
            if stop_now():
                return
            phase_pool(l)
            if stop_now():
                return
            for mg in range(4):
                phase_merge(l, mg)
            if stop_now():
                return
            phase_wout(l, xin, XA)
            if stop_now():
                return
            xfinal = OUT if l == 1 else XB
            chain = [XA, XB, XA, XB, xfinal] if False else None
            for gi, (j0, nj) in enumerate(FG):
                phase_ffn(l, j0, nj, XA if gi == 0 else XB, xfinal if gi == len(FG) - 1 else XB)
            if stop_now():
                return
            xin = XB

    if only == "norm":
        phase_norm(0, 0, I["x"])
    else:
        run()
    nc._declared_inputs = list(I.keys()) + (["MOD"] if only else [])
    return nc


def _host_layout(inputs):
    f = lambda a: np.ascontiguousarray(np.asarray(a, dtype=np.float32))
    w_in = f(inputs["w_in"])
    q, k, v, qi, ki, wi, u, p = np.split(w_in, [1024, 1152, 1280, 1792, 1856, 1864, 2376], axis=-1)
    sh = {}
    sh["w_inR"] = np.ascontiguousarray(np.concatenate([q, k, qi, ki, ki, u, p], axis=-1))
    sh["w_vw"] = np.ascontiguousarray(np.concatenate([v, wi], axis=-1))
    sh["w_ada"] = f(inputs["w_ada"])
    sh["b_ada"] = f(inputs["b_ada"])
    g1 = f(inputs["g_norm1"]).reshape(2, 16, 128).transpose(0, 2, 1)
    g2 = f(inputs["g_norm2"]).reshape(2, 16, 128).transpose(0, 2, 1)
    sh["g_norm"] = np.ascontiguousarray(np.stack([g1, g2], axis=1))
    sh["g_qk"] = np.ascontiguousarray(np.stack([f(inputs["g_q"]), f(inputs["g_k"])], axis=-1))
    a_re, a_im, ldt = f(inputs["a_re"]), f(inputs["a_im"]), f(inputs["log_dt"])
    ldt_e = np.repeat(ldt[:, :, None], 64, axis=2)
    row = np.stack([a_re.reshape(2, 2048), a_im.reshape(2, 2048), ldt_e.reshape(2, 2048)], axis=1)
    sh["a_row"] = np.ascontiguousarray(row)
    sh["a_st"] = np.ascontiguousarray(row.reshape(2, 3, 16, 128).transpose(0, 1, 3, 2))
    b_re, b_im = f(inputs["b_re"]), f(inputs["b_im"])
    c_re, c_im = f(inputs["c_re"]), f(inputs["c_im"])
    braw = np.zeros((2, 2, 128, 16, 128), np.float32)
    craw = np.zeros((2, 2, 128, 16, 128), np.float32)
    for st in range(16):
        c_, q_ = st // 4, st % 4
        for gl in range(2):
            g = 8 * c_ + 2 * q_ + gl
            gc = 2 * q_ + gl
            for ri, (bsrc, csrc) in enumerate(((b_re, c_re), (b_im, c_im))):
                braw[:, ri, gc * 16:(gc + 1) * 16, st, gl * 64:(gl + 1) * 64] = bsrc[:, g].transpose(0, 2, 1)
                craw[:, ri, gl * 64:(gl + 1) * 64, st, gc * 16:(gc + 1) * 16] = csrc[:, g].transpose(0, 2, 1)
    sh["braw"] = braw.reshape(2, 2, 128, 2048)
    sh["craw"] = craw.reshape(2, 2, 128, 2048)
    sh["dskip"] = np.ascontiguousarray(f(inputs["d_skip"]).reshape(2, 4, 128).transpose(0, 2, 1))
    sh["w_glu"] = f(inputs["w_glu"])
    sh["w_pool"] = np.ascontiguousarray(f(inputs["w_pool"]).transpose(0, 2, 1, 3))
    sh["pscale"] = np.ascontiguousarray(f(inputs["pool_scale"]).reshape(2, 4, 128).transpose(0, 2, 1))
    sh["p_a"], sh["p_b"], sh["p_c"] = f(inputs["p_a"]), f(inputs["p_b"]), f(inputs["p_c"])
    sh["w_gate"] = f(inputs["w_gate"])
    sh["b_gate"] = np.ascontiguousarray(f(inputs["b_gate"]).reshape(2, 3, 16, 128).transpose(0, 1, 3, 2))
    sh["w_out"] = f(inputs["w_out"])
    sh["w_up"] = f(inputs["w_up"])
    sh["conv_w"] = np.ascontiguousarray(f(inputs["conv_w"]).reshape(2, 3, NFC, 128).transpose(0, 3, 1, 2))
    sh["conv_b"] = np.ascontiguousarray(f(inputs["conv_b"]).reshape(2, NFC, 128).transpose(0, 2, 1))
    sh["w_down"] = f(inputs["w_down"])
    sh["ident"] = np.eye(128, dtype=np.float32).astype(ml_dtypes.bfloat16)
    rm = np.zeros((2, 128, 128), np.float32)
    for j in range(16):
        rm[0, 16 + j, j] = -1.0
        rm[0, j, 16 + j] = 1.0
    for hb in (0, 64):
        for j in range(8):
            rm[1, hb + 8 + j, hb + j] = -1.0
            rm[1, hb + j, hb + 8 + j] = 1.0
    sh["rmat"] = rm
    invf = np.zeros((128, 2), np.float32)
    fq = (500000.0 ** (-np.arange(16, dtype=np.float32) * np.float32(2.0 / 32))).astype(np.float32)
    fi = (500000.0 ** (-np.arange(8, dtype=np.float32) * np.float32(2.0 / 16))).astype(np.float32)
    invf[0:16, 0] = fq
    invf[16:32, 0] = fq
    for hb in (0, 64):
        invf[hb:hb + 8, 1] = fi
        invf[hb + 8:hb + 16, 1] = fi
    sh["invf"] = invf
    sh["iota"] = np.ascontiguousarray(np.broadcast_to(np.arange(T, dtype=np.float32), (128, T)))
    q3 = np.array([[k * 0.25 ** (it + 1) for k in (1, 2, 3)] for it in range(NQ)], np.float32).reshape(-1)
    sh["halv"] = np.ascontiguousarray(np.broadcast_to(q3, (128, 3 * NQ)))
    invc = np.zeros((128, 4, 16), np.float32)
    for g in range(4):
        w = 2 ** (g + 1)
        invc[:, g, :] = 1.0 / np.minimum(np.arange(1, 17), w)
    sh["invc"] = invc
    return sh


def _in_maps(inputs, ncores):
    sh = _host_layout(inputs)
    x = np.asarray(inputs["x"], np.float32)
    c = np.asarray(inputs["c"], np.float32)
    pos = np.asarray(inputs["positions"], np.int32)
    maps = []
    for i in range(ncores):
        m = dict(sh)
        m["x"] = np.ascontiguousarray(x[NB * i:NB * (i + 1)].reshape(NTOK, D))
        m["c"] = np.ascontiguousarray(c[NB * i:NB * (i + 1)])
        m["pos"] = np.ascontiguousarray(pos[NB * i:NB * (i + 1)])
        maps.append(m)
    return maps


def kernel(**inputs):
    nc = build()
    maps = _in_maps(inputs, NCORES)
    maps = [{k: m[k] for k in nc._declared_inputs} for m in maps]
    res = run_bass_kernel_spmd(nc, maps, core_ids=list(range(NCORES)))
    outs = [np.asarray(r["out"], np.float32).reshape(NB, SEQ, D) for r in res.results]
    return np.concatenate(outs, axis=0)
```
